# Optimizing a Trainium2 kernel written in Bass

```python
import math
import jax, jax.numpy as jnp
from jax import lax
import numpy as np

D_MODEL = 2048
BATCH = 2
SEQ = 16384
DEPTH = 2

GRID_W = 64
CTX_LEN = 256
HEAD_DIM = 128
A_HEADS = D_MODEL // (2 * HEAD_DIM)
A_HALF = HEAD_DIM // 2
B_HEADS = D_MODEL // (2 * HEAD_DIM)
B_KV_HEADS = 2
B_GROUP = B_HEADS // B_KV_HEADS
WINDOW = 128
BLOCK = 128
ROPE_BASE = 10000.0
D_MIX = (A_HEADS + B_HEADS) * HEAD_DIM
Q_COLS = (A_HEADS + B_HEADS) * HEAD_DIM
KV_COLS = 2 * A_HEADS * HEAD_DIM + 2 * B_KV_HEADS * HEAD_DIM
KV_SPLITS = [A_HEADS * HEAD_DIM, 2 * A_HEADS * HEAD_DIM, 2 * A_HEADS * HEAD_DIM + B_KV_HEADS * HEAD_DIM]
IN_ATTN = Q_COLS + KV_COLS
HY_ORDER = 2
HY_EMB = 33
HY_FILTER_W = 64
HY_SHORT = 3
HY_DECAY_TARGET = 1e-2
HY_MAX_DECAY_PCT = 0.3
HY_MIN_DECAY_PCT = 1.5
D_FF = 5632
FFN_CONV = 3
EPS = 1e-6
NEG = -1e30

kernel_name = 'hybrid_diffattn_swa_hyena_dit'


def _rms(x, g):
    xf = x.astype(jnp.float32)
    y = xf * lax.rsqrt(jnp.mean(xf * xf, axis=-1, keepdims=True) + EPS)
    return (y * g.astype(jnp.float32)).astype(x.dtype)


def _dwconv(z, w, b):
    K = w.shape[0]
    pad = (K - 1) // 2
    L = z.shape[1]
    zp = jnp.pad(z, ((0, 0), (pad, pad), (0, 0)))
    out = zp[:, 0:L] * w[0] + b
    for k in range(1, K):
        out = out + zp[:, k:k + L] * w[k]
    return out


def _axial_rope_tables(L, dim):
    rows = L // GRID_W
    row = jnp.repeat(jnp.arange(rows, dtype=jnp.float32), GRID_W)
    col = jnp.tile(jnp.arange(GRID_W, dtype=jnp.float32), rows)
    n_f = dim // 4
    inv = ROPE_BASE ** (-jnp.arange(n_f, dtype=jnp.float32) / n_f)
    ang = jnp.stack([row[:, None] * inv, col[:, None] * inv], axis=1)
    return jnp.cos(ang), jnp.sin(ang)


def _apply_rope(x, cos, sin):
    shp = x.shape
    n_f = shp[-1] // 4
    xr = x.astype(jnp.float32).reshape(shp[:-1] + (2, 2, n_f))
    x1, x2 = xr[..., 0, :], xr[..., 1, :]
    bshape = (shp[1],) + (1,) * (x.ndim - 3) + (2, n_f)
    c = cos.reshape(bshape)
    s = sin.reshape(bshape)
    out = jnp.stack([x1 * c - x2 * s, x2 * c + x1 * s], axis=-2)
    return out.reshape(shp).astype(x.dtype)


def _q_heads(t):
    B_, n = t.shape[:2]
    qa, qb = jnp.split(t, [A_HEADS * HEAD_DIM], axis=-1)
    return (qa.reshape(B_, n, A_HEADS, 2, A_HALF),
            qb.reshape(B_, n, B_KV_HEADS, B_GROUP, HEAD_DIM))


def _kv_heads(t):
    B_, n = t.shape[:2]
    ka, va, kb, vb = jnp.split(t, KV_SPLITS, axis=-1)
    return (ka.reshape(B_, n, A_HEADS, 2, A_HALF), va.reshape(B_, n, A_HEADS, HEAD_DIM),
            kb.reshape(B_, n, B_KV_HEADS, HEAD_DIM), vb.reshape(B_, n, B_KV_HEADS, HEAD_DIM))


def _diff_softmax(q, k, v, lam):
    s = jnp.einsum('bqhmd,bkhmd->bhmqk', q, k, preferred_element_type=jnp.float32)
    p = jax.nn.softmax(s, axis=-1)
    w = p[:, :, 0] - lam * p[:, :, 1]
    return jnp.einsum('bhqk,bkhd->bqhd', w.astype(v.dtype), v)


def _diff_attn_blocks(q, k_all, v_all, lam):
    B_, L = q.shape[:2]
    nb = L // BLOCK
    qblocks = jnp.moveaxis(q.reshape((B_, nb, BLOCK) + q.shape[2:]), 1, 0)
    out = lax.map(lambda qi: _diff_softmax(qi, k_all, v_all, lam), qblocks)
    return jnp.moveaxis(out, 0, 1).reshape(B_, L, A_HEADS, HEAD_DIM)


def _sink_softmax(q, kc, vc, sink):
    s = jnp.einsum('bqgrd,bcgd->bgrqc', q, kc, preferred_element_type=jnp.float32)
    s_sink = jnp.broadcast_to(sink.astype(jnp.float32)[None, :, :, None, None], s.shape[:-1] + (1,))
    p = jax.nn.softmax(jnp.concatenate([s, s_sink], axis=-1), axis=-1)[..., :kc.shape[1]]
    return jnp.einsum('bgrqc,bcgd->bqgrd', p.astype(vc.dtype), vc)


def _window_attn_blocks(q, k, v, kc, vc, sink):
    B_, L = q.shape[:2]
    nb = L // BLOCK
    C = kc.shape[1]
    pad = ((0, 0), (BLOCK, BLOCK), (0, 0), (0, 0))
    kp = jnp.pad(k, pad)
    vp = jnp.pad(v, pad)
    qblocks = jnp.moveaxis(q.reshape((B_, nb, BLOCK) + q.shape[2:]), 1, 0)
    offs_q = jnp.arange(BLOCK)
    offs_k = jnp.arange(3 * BLOCK) - BLOCK

    def one(args):
        i, qi = args
        start = i * BLOCK
        kb = lax.dynamic_slice_in_dim(kp, start, 3 * BLOCK, axis=1)
        vb = lax.dynamic_slice_in_dim(vp, start, 3 * BLOCK, axis=1)
        qpos = start + offs_q
        kpos = start + offs_k
        valid = ((jnp.abs(kpos[None, :] - qpos[:, None]) <= WINDOW)
                 & (kpos >= 0)[None, :] & (kpos < L)[None, :])
        s_lat = jnp.where(valid, jnp.einsum('bqgrd,bkgd->bgrqk', qi, kb,
                                            preferred_element_type=jnp.float32), NEG)
        s_ctx = jnp.einsum('bqgrd,bcgd->bgrqc', qi, kc, preferred_element_type=jnp.float32)
        s_sink = jnp.broadcast_to(sink.astype(jnp.float32)[None, :, :, None, None], s_ctx.shape[:-1] + (1,))
        prob = jax.nn.softmax(jnp.concatenate([s_lat, s_ctx, s_sink], axis=-1), axis=-1)
        p_lat = prob[..., :3 * BLOCK].astype(v.dtype)
        p_ctx = prob[..., 3 * BLOCK:3 * BLOCK + C].astype(v.dtype)
        return (jnp.einsum('bgrqk,bkgd->bqgrd', p_lat, vb)
                + jnp.einsum('bgrqc,bcgd->bqgrd', p_ctx, vc))

    out = lax.map(one, (jnp.arange(nb), qblocks))
    return jnp.moveaxis(out, 0, 1).reshape(B_, L, B_HEADS, HEAD_DIM)


def _attn_mixer(h, hc, ctx_out, lam_init, w_in, w_out, a_q_g, a_k_g, lq1, lk1, lq2, lk2,
                a_sub_g, b_q_g, b_k_g, b_sink):
    B_, L, _ = h.shape
    f32 = jnp.float32
    lam = (jnp.exp(jnp.sum(lq1.astype(f32) * lk1.astype(f32)))
           - jnp.exp(jnp.sum(lq2.astype(f32) * lk2.astype(f32))) + lam_init)
    sink = b_sink.reshape(B_KV_HEADS, B_GROUP)
    cos_a, sin_a = _axial_rope_tables(L, A_HALF)
    cos_b, sin_b = _axial_rope_tables(L, HEAD_DIM)
    p = h @ w_in
    qa, qb = _q_heads(p[..., :Q_COLS])
    ka, va, kb, vb = _kv_heads(p[..., Q_COLS:])
    qa = _apply_rope(_rms(qa, a_q_g), cos_a, sin_a) * A_HALF ** -0.5
    ka = _apply_rope(_rms(ka, a_k_g), cos_a, sin_a)
    qb = _apply_rope(_rms(qb, b_q_g), cos_b, sin_b) * HEAD_DIM ** -0.5
    kb = _apply_rope(_rms(kb, b_k_g), cos_b, sin_b)
    pc = hc @ (w_in if ctx_out else w_in[:, Q_COLS:])
    kac, vac, kbc, vbc = _kv_heads(pc[..., -KV_COLS:])
    kac = _rms(kac, a_k_g)
    kbc = _rms(kbc, b_k_g)
    k_all = jnp.concatenate([kac, ka], axis=1)
    v_all = jnp.concatenate([vac, va], axis=1)
    oa = _rms(_diff_attn_blocks(qa, k_all, v_all, lam), a_sub_g) * (1.0 - lam_init)
    ob = _window_attn_blocks(qb, kb, vb, kbc, vbc, sink)
    y = jnp.concatenate([oa.reshape(B_, L, -1), ob.reshape(B_, L, -1)], axis=-1) @ w_out
    yc = None
    if ctx_out:
        C = hc.shape[1]
        qac, qbc = _q_heads(pc[..., :Q_COLS])
        qac = _rms(qac, a_q_g) * A_HALF ** -0.5
        qbc = _rms(qbc, b_q_g) * HEAD_DIM ** -0.5
        oac = _rms(_diff_softmax(qac, kac, vac, lam), a_sub_g) * (1.0 - lam_init)
        obc = _sink_softmax(qbc, kbc, vbc, sink)
        yc = jnp.concatenate([oac.reshape(B_, C, -1), obc.reshape(B_, C, -1)], axis=-1) @ w_out
    return y, yc


def _hyena_filters(L, f_w0, f_b0, f_w1, f_b1, f_w2, f_b2, f_freq, f_w3):
    f32 = jnp.float32
    t = jnp.linspace(0.0, 1.0, L, dtype=f32)[:, None]
    bands = (HY_EMB - 1) // 2
    w = 2.0 * math.pi * jnp.arange(L, dtype=f32)[:, None] / L
    f = jnp.linspace(1e-4, bands - 1, bands, dtype=f32)[None, :]
    z = jnp.concatenate([t, jnp.cos(f * w), -jnp.sin(f * w)], axis=-1)
    freq = f_freq.astype(f32)
    a = jnp.sin(freq * (z @ f_w0.astype(f32) + f_b0.astype(f32)))
    a = jnp.sin(freq * (a @ f_w1.astype(f32) + f_b1.astype(f32)))
    a = jnp.sin(freq * (a @ f_w2.astype(f32) + f_b2.astype(f32)))
    hf = (a @ f_w3.astype(f32)).reshape(L, HY_ORDER, 2, D_MODEL)
    max_decay = math.log(HY_DECAY_TARGET) / HY_MAX_DECAY_PCT
    min_decay = math.log(HY_DECAY_TARGET) / HY_MIN_DECAY_PCT
    deltas = jnp.linspace(min_decay, max_decay, D_MODEL, dtype=f32)
    decay = jnp.exp(-t * jnp.abs(deltas))
    return hf * decay[:, None, None, :]


def _bidir_long_conv(u, h_fwd, h_bwd, d_skip):
    L, D = u.shape[1], u.shape[2]
    filt = jnp.concatenate([h_fwd, jnp.zeros((1, D), h_fwd.dtype), h_bwd[:0:-1]], axis=0)
    filt = filt / jnp.sum(jnp.abs(filt), axis=0, keepdims=True)
    n = 2 * L
    uf = jnp.fft.rfft(u.astype(jnp.float32), n=n, axis=1)
    ff = jnp.fft.rfft(filt, n=n, axis=0)
    y = jnp.fft.irfft(uf * ff[None], n=n, axis=1)[:, :L]
    return (y + u.astype(jnp.float32) * d_skip.astype(jnp.float32)).astype(u.dtype)


def _hyena(h, w_in, b_in, sconv_w, sconv_b, f_w0, f_b0, f_w1, f_b1, f_w2, f_b2, f_freq, f_w3,
           d_skip, w_out):
    L = h.shape[1]
    z = _dwconv(h @ w_in + b_in, sconv_w, sconv_b)
    v, x1, x2 = jnp.split(z, 3, axis=-1)
    filt = _hyena_filters(L, f_w0, f_b0, f_w1, f_b1, f_w2, f_b2, f_freq, f_w3)
    y = v
    for o, gate in enumerate((x1, x2)):
        y = gate * _bidir_long_conv(y, filt[:, o, 0], filt[:, o, 1], d_skip[o])
    return y @ w_out


def _conv_ffn(h, w_gate, w_val, conv_w, conv_b, w_down):
    g = _dwconv(h @ w_gate, conv_w, conv_b)
    return (jax.nn.gelu(g) * (h @ w_val)) @ w_down


def _ctx_read_later(l):
    return any(j % 2 == 0 for j in range(l + 1, DEPTH))


def setup_inputs(seed: int = 0) -> dict:
    key = jax.random.key(seed)
    keys = iter(jax.random.split(key, 64))

    def nrm(shape, scale):
        return jax.random.normal(next(keys), shape, jnp.float32) * scale

    def gain(shape):
        return 1.0 + nrm(shape, 0.02)

    NA = (DEPTH + 1) // 2
    NH = DEPTH // 2
    D = D_MODEL
    return {
        'x': nrm((BATCH, SEQ, D), 1.0),
        'c': nrm((BATCH, D), 1.0),
        'ctx': nrm((BATCH, CTX_LEN, D), 1.0),
        'c_ctx': nrm((D,), 1.0),
        'ada_w': nrm((DEPTH, D, 6 * D), 0.5 * D ** -0.5),
        'ada_b': nrm((DEPTH, 6 * D), 0.02),
        'norm1_g': gain((DEPTH, D)),
        'norm2_g': gain((DEPTH, D)),
        'attn_w_in': nrm((NA, D, IN_ATTN), D ** -0.5),
        'attn_w_out': nrm((NA, D_MIX, D), D_MIX ** -0.5),
        'a_q_g': gain((NA, A_HALF)),
        'a_k_g': gain((NA, A_HALF)),
        'a_lam_q1': nrm((NA, A_HALF), 0.1),
        'a_lam_k1': nrm((NA, A_HALF), 0.1),
        'a_lam_q2': nrm((NA, A_HALF), 0.1),
        'a_lam_k2': nrm((NA, A_HALF), 0.1),
        'a_sub_g': gain((NA, HEAD_DIM)),
        'b_q_g': gain((NA, HEAD_DIM)),
        'b_k_g': gain((NA, HEAD_DIM)),
        'b_sink': nrm((NA, B_HEADS), 0.5),
        'hy_w_in': nrm((NH, D, 3 * D), D ** -0.5),
        'hy_b_in': nrm((NH, 3 * D), 0.02),
        'hy_sconv_w': nrm((NH, HY_SHORT, 3 * D), HY_SHORT ** -0.5),
        'hy_sconv_b': nrm((NH, 3 * D), 0.02),
        'hy_f_w0': nrm((NH, HY_EMB, HY_FILTER_W), HY_EMB ** -0.5),
        'hy_f_b0': nrm((NH, HY_FILTER_W), 0.02),
        'hy_f_w1': nrm((NH, HY_FILTER_W, HY_FILTER_W), HY_FILTER_W ** -0.5),
        'hy_f_b1': nrm((NH, HY_FILTER_W), 0.02),
        'hy_f_w2': nrm((NH, HY_FILTER_W, HY_FILTER_W), HY_FILTER_W ** -0.5),
        'hy_f_b2': nrm((NH, HY_FILTER_W), 0.02),
        'hy_f_freq': gain((NH, HY_FILTER_W)),
        'hy_f_w3': nrm((NH, HY_FILTER_W, HY_ORDER * 2 * D), HY_FILTER_W ** -0.5),
        'hy_d': nrm((NH, HY_ORDER, D), 0.5),
        'hy_w_out': nrm((NH, D, D), D ** -0.5),
        'ffn_w_gate': nrm((DEPTH, D, D_FF), D ** -0.5),
        'ffn_w_val': nrm((DEPTH, D, D_FF), D ** -0.5),
        'ffn_conv_w': nrm((DEPTH, FFN_CONV, D_FF), FFN_CONV ** -0.5),
        'ffn_conv_b': nrm((DEPTH, D_FF), 0.02),
        'ffn_w_down': nrm((DEPTH, D_FF, D), D_FF ** -0.5),
    }


def reference(x, c, ctx, c_ctx, ada_w, ada_b, norm1_g, norm2_g,
              attn_w_in, attn_w_out, a_q_g, a_k_g, a_lam_q1, a_lam_k1, a_lam_q2, a_lam_k2, a_sub_g,
              b_q_g, b_k_g, b_sink,
              hy_w_in, hy_b_in, hy_sconv_w, hy_sconv_b, hy_f_w0, hy_f_b0, hy_f_w1, hy_f_b1,
              hy_f_w2, hy_f_b2, hy_f_freq, hy_f_w3, hy_d, hy_w_out,
              ffn_w_gate, ffn_w_val, ffn_conv_w, ffn_conv_b, ffn_w_down):
    s_lat = jax.nn.silu(c)
    s_ctx = jax.nn.silu(c_ctx)
    xs, cs = x, ctx
    for l in range(DEPTH):
        ctx_out = _ctx_read_later(l)
        ctx_in = (l % 2 == 0) or ctx_out
        i = l // 2
        m = s_lat @ ada_w[l] + ada_b[l]
        sh1, sc1, g1, sh2, sc2, g2 = (t[:, None, :] for t in jnp.split(m, 6, axis=-1))
        h = _rms(xs, norm1_g[l]) * (1.0 + sc1) + sh1
        hc = None
        if ctx_in:
            mc = s_ctx @ ada_w[l] + ada_b[l]
            csh1, csc1, cg1, csh2, csc2, cg2 = jnp.split(mc, 6)
            hc = _rms(cs, norm1_g[l]) * (1.0 + csc1) + csh1
        if l % 2 == 0:
            y, yc = _attn_mixer(h, hc, ctx_out, 0.8 - 0.6 * math.exp(-0.3 * l),
                                attn_w_in[i], attn_w_out[i], a_q_g[i], a_k_g[i],
                                a_lam_q1[i], a_lam_k1[i], a_lam_q2[i], a_lam_k2[i], a_sub_g[i],
                                b_q_g[i], b_k_g[i], b_sink[i])
        else:
            hy = (hy_w_in[i], hy_b_in[i], hy_sconv_w[i], hy_sconv_b[i], hy_f_w0[i], hy_f_b0[i],
                  hy_f_w1[i], hy_f_b1[i], hy_f_w2[i], hy_f_b2[i], hy_f_freq[i], hy_f_w3[i],
                  hy_d[i], hy_w_out[i])
            y = _hyena(h, *hy)
            yc = _hyena(hc, *hy) if ctx_out else None
        xs = xs + g1 * y
        ffn = (ffn_w_gate[l], ffn_w_val[l], ffn_conv_w[l], ffn_conv_b[l], ffn_w_down[l])
        xs = xs + g2 * _conv_ffn(_rms(xs, norm2_g[l]) * (1.0 + sc2) + sh2, *ffn)
        if ctx_out:
            cs = cs + cg1 * yc
            cs = cs + cg2 * _conv_ffn(_rms(cs, norm2_g[l]) * (1.0 + csc2) + csh2, *ffn)
    return xs
```

```python
import math
import numpy as np
import concourse.bass as bass
import concourse.mybir as mybir
from concourse.bass_utils import run_bass_kernel_spmd

F32 = mybir.dt.float32
BF16 = mybir.dt.bfloat16
AF = mybir.ActivationFunctionType
ALU = mybir.AluOpType
AX = mybir.AxisListType


class Prog:
    ENGS = ("pe", "act", "dve", "pool", "sp")

    def __init__(self, nc):
        self.nc = nc
        self.ops = []
        self.relaxed = set()
        self.last_w = {}
        self.readers = {}

    def op(self, eng, fn, reads=(), writes=(), dma=False, relaxed=False):
        self.ops.append((eng, fn, tuple(reads), tuple(writes), dma))
        if relaxed:
            self.relaxed.add(len(self.ops) - 1)

    def emit(self):
        nc = self.nc
        ops = self.ops
        n = len(ops)
        deps = [None] * n
        last_w, readers = {}, {}
        has_dep = [False] * n
        for i, (eng, fn, reads, writes, dma) in enumerate(ops):
            d = set()
            for r in reads:
                if r in last_w:
                    d.add(last_w[r])
            for w in writes:
                if w in last_w:
                    d.add(last_w[w])
                for rr in readers.get(w, ()):
                    d.add(rr)
            d.discard(i)
            deps[i] = d
            for j in d:
                has_dep[j] = True
            for r in reads:
                readers.setdefault(r, []).append(i)
            for w in writes:
                last_w[w] = i
                readers[w] = []
        eng_cnt = {e: 0 for e in self.ENGS}
        sig = [None] * n
        prevw = [None] * n
        NDMA = {"sp": 12, "pool": 8, "act": 4, "dve": 2, "pe": 2}
        dma_cnt = {(e, k): 0 for e in self.ENGS for k in range(NDMA[e])}
        dma_rr = {e: 0 for e in self.ENGS}
        for i, (eng, fn, reads, writes, dma) in enumerate(ops):
            if dma:
                k = dma_rr[eng] % NDMA[eng]
                dma_rr[eng] += 1
                prevw[i] = (("dma", eng, k), dma_cnt[(eng, k)])
                dma_cnt[(eng, k)] += 16
                sig[i] = (("dma", eng, k), dma_cnt[(eng, k)])
            elif has_dep[i]:
                eng_cnt[eng] += 1
                sig[i] = ((eng,), eng_cnt[eng])
        self.n_sig = dict(eng_cnt)
        streams = {e: [] for e in self.ENGS}
        waited = {e: {} for e in self.ENGS}
        for i, (eng, fn, reads, writes, dma) in enumerate(ops):
            need = {}
            for j in deps[i]:
                sk, v = sig[j]
                ej = ops[j][0]
                if (not ops[j][4]) and ej == eng and eng == "pe":
                    continue
                if (not ops[j][4]) and ej == eng and i in self.relaxed:
                    continue
                if v > need.get(sk, 0):
                    need[sk] = v
            ws = []
            if prevw[i] is not None and prevw[i][1] > 0:
                sk, v = prevw[i]
                if v > need.get(sk, 0):
                    need[sk] = v
            for sk, v in need.items():
                if waited[eng].get(sk, 0) >= v:
                    continue
                waited[eng][sk] = v
                ws.append((sk, v))
            streams[eng].append((ws, fn, sig[i]))
        self.streams = streams
        import contextlib
        with contextlib.ExitStack() as es:
            sems = {}
            for e in self.ENGS:
                sems[(e,)] = es.enter_context(nc.semaphore("s_" + e))
            for (e, k), c in dma_cnt.items():
                if c:
                    sems[("dma", e, k)] = es.enter_context(nc.semaphore("s_dma_%s%d" % (e, k)))
            block = es.enter_context(nc.Block())
            engobj = {"pe": "tensor", "act": "scalar", "dve": "vector", "pool": "gpsimd", "sp": "sync"}

            def mk(ename):
                def body(eng):
                    for ws, fn, sg in streams[ename]:
                        for sk, v in ws:
                            eng.wait_ge(sems[sk], v)
                        ins = fn(eng)
                        if sg is not None:
                            sk, v = sg
                            ins.then_inc(sems[sk], 16 if sk[0] == "dma" else 1)
                    for k in range(NDMA[ename]):
                        if dma_cnt[(ename, k)]:
                            eng.wait_ge(sems[("dma", ename, k)], dma_cnt[(ename, k)])
                return body

            for e in self.ENGS:
                if streams[e]:
                    getattr(block, engobj[e])(mk(e))


import contextlib
import ml_dtypes
NPBF = ml_dtypes.bfloat16


class Ctx:
    def __init__(self):
        self.nc = bass.Bass("TRN2", target_bir_lowering=False)
        self.P = Prog(self.nc)
        self.es = contextlib.ExitStack()
        self.n = 0

    def din(self, name, shape, dt):
        return self.nc.dram_tensor(name, list(shape), dt, kind="ExternalInput").ap()

    def dout(self, name, shape, dt):
        return self.nc.dram_tensor(name, list(shape), dt, kind="ExternalOutput").ap()

    def dint(self, name, shape, dt):
        return self.nc.dram_tensor(name, list(shape), dt, kind="Internal").ap()

    def sb(self, name, shape, dt):
        return self.es.enter_context(self.nc.sbuf_tensor(name, list(shape), dt))

    def ps(self, name, shape, dt):
        return self.es.enter_context(self.nc.psum_tensor(name, list(shape), dt))

    def finish(self):
        self.P.emit()
        self.es.close()
        return self.nc


def run(nc, in_maps):
    res = run_bass_kernel_spmd(nc, in_maps, core_ids=list(range(len(in_maps))))
    return res.results


def pl(v, p=128):
    v = np.asarray(v)
    return np.ascontiguousarray(v.reshape((-1, p) + v.shape[1:]).swapaxes(0, 1))


def windows(TOK, WN):
    out, s = [], 0
    while s < TOK:
        n = min(WN, TOK - s)
        out.append((s, n))
        s += n
    return out


def build_ffn(TOK, D, DFF, WN=382):
    C = Ctx()
    P = C.P
    KC, FC = D // 128, DFF // 128
    NCB = D // 512
    xh = C.din("xh", [TOK + 2, D], F32)
    edge = C.din("edge", [128, 2], F32)
    modp = C.din("modp", [128, 3, KC], F32)
    g2row = C.din("g2row", [1, D], F32)
    wg = C.din("wg", [D, DFF], BF16)
    wv = C.din("wv", [D, DFF], BF16)
    wd = C.din("wd", [DFF, D], BF16)
    cw = C.din("cw", [128, 3, FC], F32)
    cb = C.din("cb", [128, FC], F32)
    ident = C.din("ident", [128, 128], BF16)
    out = C.dout("out", [TOK, D], F32)

    NT = 3
    xw = C.sb("xw", [128, NT, D], F32)
    xr = C.sb("xr", [128, NT, D], F32)
    xn = [C.sb("xn%d" % i, [128, D], BF16) for i in range(2)]
    hT = C.sb("hT", [128, KC, NT * 128], BF16)
    uT = C.sb("uT", [128, FC, NT * 128], BF16)
    NR = 3
    ring = [C.sb("ring%d" % i, [128, 16, 512], BF16) for i in range(NR)]
    g2t = C.sb("g2t", [128, D], F32)
    idb = C.sb("idb", [128, 128], BF16)
    mp = C.sb("mp", [128, 3, KC], F32)
    A2 = C.sb("A2", [128, KC], F32)
    cwt = C.sb("cwt", [128, 3, FC], F32)
    cbt = C.sb("cbt", [128, FC], F32)
    edg = C.sb("edg", [128, 2], F32)
    epsb = C.sb("epsb", [128, 1], F32)
    ss = C.sb("ss", [128, NT], F32)
    rstd = C.sb("rstd", [128, NT], F32)
    junk = C.sb("junk", [128, D], BF16)
    acc = [C.sb("acc%d" % i, [128, NT * 128], F32) for i in range(2)]
    gl = [C.sb("gl%d" % i, [128, NT * 128], F32) for i in range(2)]
    tmp = [C.sb("tmp%d" % i, [128, 512], F32) for i in range(2)]
    pT = C.ps("pT", [128, 2, 512], BF16)
    pG = [C.ps("pG%d" % i, [128, 512], F32) for i in range(2)]
    pV = [C.ps("pV%d" % i, [128, 512], F32) for i in range(2)]
    pD = [C.ps("pD%d" % i, [128, 512], F32) for i in range(NT)]

    P.op("sp", lambda e: e.dma_start(out=idb[:], in_=ident), writes=["idb"], dma=True)
    P.op("sp", lambda e: e.dma_start(out=mp[:], in_=modp), writes=["mp"], dma=True)
    P.op("sp", lambda e: e.dma_start(out=cwt[:], in_=cw), writes=["cwt"], dma=True)
    P.op("sp", lambda e: e.dma_start(out=cbt[:], in_=cb), writes=["cbt"], dma=True)
    P.op("sp", lambda e: e.dma_start(out=edg[:], in_=edge), writes=["edg"], dma=True)
    P.op("sp", lambda e: e.dma_start(out=g2t[:], in_=g2row.partition_broadcast(128)), writes=["g2t"], dma=True)
    P.op("dve", lambda e: e.memset(epsb[:], 1e-6), writes=["epsb"])
    P.op("dve", lambda e: e.scalar_tensor_tensor(out=A2[:], in0=mp[:, 1, :], scalar=1.0, in1=mp[:, 0, :], op0=ALU.add, op1=ALU.mult),
         reads=["mp"], writes=["A2"])

    rcnt = [0]

    def load_block(src_ap, nk):
        i = rcnt[0] % NR
        rcnt[0] += 1
        buf = ring[i]
        key = "ring%d" % i
        P.op("sp", lambda e: e.dma_start(out=buf[:, 0:nk, :], in_=src_ap.rearrange("(c p) n -> p c n", p=128)),
             writes=[key], dma=True)
        return buf, key

    wins = windows(TOK, WN)
    def do_window(wi, s, n_out):
        n_in = n_out + 2
        tiles = [(j, min(128, n_in - j * 128)) for j in range((n_in + 127) // 128)]
        for j, rows in tiles:
            P.op("sp", lambda e, j=j, rows=rows: e.dma_start(out=xw[:rows, j, :], in_=xh[s + j * 128: s + j * 128 + rows, :]),
                 writes=["xw%d" % j], dma=True)
        for j in range((n_out + 127) // 128):
            rows = min(128, n_out - j * 128)
            P.op("sp", lambda e, j=j, rows=rows: e.dma_start(out=xr[:rows, j, :], in_=xh[s + 1 + j * 128: s + 1 + j * 128 + rows, :]),
                 writes=["xr%d" % j], dma=True)
        for j, rows in tiles:
            P.op("act", lambda e, j=j, rows=rows: e.activation(out=junk[:rows, :], in_=xw[:rows, j, :], func=AF.Square, accum_out=ss[:rows, j:j + 1]),
                 reads=["xw%d" % j], writes=["junk", "ss%d" % j])
        for j, rows in tiles:
            P.op("act", lambda e, j=j, rows=rows: e.activation(out=rstd[:rows, j:j + 1], in_=ss[:rows, j:j + 1], func=AF.Sqrt, scale=1.0 / D, bias=epsb[:rows, :]),
                 reads=["ss%d" % j, "epsb"], writes=["rstd%d" % j])
            P.op("dve", lambda e, j=j, rows=rows: e.reciprocal(out=rstd[:rows, j:j + 1], in_=rstd[:rows, j:j + 1]),
                 reads=["rstd%d" % j], writes=["rstd%d" % j])
        for j, rows in tiles:
            xb = xn[j % 2]
            P.op("dve", lambda e, j=j, rows=rows, xb=xb: e.tensor_scalar(out=xb[:rows, :], in0=xw[:rows, j, :], scalar1=rstd[:rows, j:j + 1], scalar2=None, op0=ALU.mult),
                 reads=["xw%d" % j, "rstd%d" % j], writes=["xn%d" % (j % 2)])
            for g4 in range(KC // 4):
                slot = g4 % 2
                for q in range(4):
                    kc = g4 * 4 + q
                    P.op("pe", lambda e, kc=kc, q=q, rows=rows, xb=xb, slot=slot: e.transpose(out=pT[:, slot, q * 128:q * 128 + rows], in_=xb[:rows, kc * 128:(kc + 1) * 128], identity=idb[:rows, :rows]),
                         reads=["xn%d" % (j % 2), "idb"], writes=["pT"])
                for q in range(4):
                    kc = g4 * 4 + q
                    P.op("act", lambda e, kc=kc, q=q, j=j, rows=rows, slot=slot: e.activation(out=hT[:, kc, j * 128:j * 128 + rows], in_=pT[:, slot, q * 128:q * 128 + rows], func=AF.Identity, scale=A2[:, kc:kc + 1], bias=mp[:, 2, kc:kc + 1]),
                         reads=["pT", "A2", "mp"], writes=["hT"])
        if wi == 0:
            P.op("dve", lambda e: e.tensor_scalar(out=hT[:, :, 0:1], in0=hT[:, :, 0:1], scalar1=edg[:, 0:1], scalar2=None, op0=ALU.mult),
                 reads=["hT", "edg"], writes=["hT"])
        if wi == len(wins) - 1:
            P.op("dve", lambda e, c=n_in - 1: e.tensor_scalar(out=hT[:, :, c:c + 1], in0=hT[:, :, c:c + 1], scalar1=edg[:, 1:2], scalar2=None, op0=ALU.mult),
                 reads=["hT", "edg"], writes=["hT"])
        for fb in range(FC // 4):
            gb, gk = load_block(wg[:, fb * 512:(fb + 1) * 512], KC)
            vb, vk = load_block(wv[:, fb * 512:(fb + 1) * 512], KC)
            for q in range(4):
                fc = fb * 4 + q
                sl = fc % 2
                for kc in range(KC):
                    P.op("pe", lambda e, kc=kc, q=q, sl=sl, gb=gb: e.matmul(pG[sl][:, 0:n_in], lhsT=gb[:, kc, q * 128:(q + 1) * 128], rhs=hT[:, kc, 0:n_in], start=(kc == 0), stop=(kc == KC - 1)),
                         reads=[gk, "hT"], writes=["pG%d" % sl])
                for kc in range(KC):
                    P.op("pe", lambda e, kc=kc, q=q, sl=sl, vb=vb: e.matmul(pV[sl][:, 0:n_in], lhsT=vb[:, kc, q * 128:(q + 1) * 128], rhs=hT[:, kc, 0:n_in], start=(kc == 0), stop=(kc == KC - 1)),
                         reads=[vk, "hT"], writes=["pV%d" % sl])
                a, g_ = acc[sl], gl[sl]
                P.op("dve", lambda e, fc=fc, sl=sl, a=a: e.tensor_scalar(out=a[:, 0:n_out], in0=pG[sl][:, 0:n_out], scalar1=cwt[:, 0, fc:fc + 1], scalar2=None, op0=ALU.mult),
                     reads=["pG%d" % sl, "cwt"], writes=["acc%d" % sl])
                P.op("dve", lambda e, fc=fc, sl=sl, a=a: e.scalar_tensor_tensor(out=a[:, 0:n_out], in0=pG[sl][:, 1:n_out + 1], scalar=cwt[:, 1, fc:fc + 1], in1=a[:, 0:n_out], op0=ALU.mult, op1=ALU.add),
                     reads=["pG%d" % sl, "cwt", "acc%d" % sl], writes=["acc%d" % sl])
                P.op("dve", lambda e, fc=fc, sl=sl, a=a: e.scalar_tensor_tensor(out=a[:, 0:n_out], in0=pG[sl][:, 2:n_out + 2], scalar=cwt[:, 2, fc:fc + 1], in1=a[:, 0:n_out], op0=ALU.mult, op1=ALU.add),
                     reads=["pG%d" % sl, "cwt", "acc%d" % sl], writes=["acc%d" % sl])
                P.op("act", lambda e, fc=fc, sl=sl, a=a, g_=g_: e.activation(out=g_[:, 0:n_out], in_=a[:, 0:n_out], func=AF.Gelu_apprx_tanh, bias=cbt[:, fc:fc + 1], scale=1.0),
                     reads=["acc%d" % sl, "cbt"], writes=["gl%d" % sl])
                P.op("dve", lambda e, fc=fc, sl=sl, g_=g_: e.tensor_tensor(out=uT[:, fc, 0:n_out], in0=g_[:, 0:n_out], in1=pV[sl][:, 1:n_out + 1], op=ALU.mult),
                     reads=["gl%d" % sl, "pV%d" % sl], writes=["uT"])
        otiles = [(j, min(128, n_out - j * 128)) for j in range((n_out + 127) // 128)]
        kparts = []
        k0 = 0
        while k0 < FC:
            kparts.append((k0, min(16, FC - k0)))
            k0 += 16
        for cbk in range(NCB):
            for pi, (k0, nk) in enumerate(kparts):
                db, dk = load_block(wd[k0 * 128:(k0 + nk) * 128, cbk * 512:(cbk + 1) * 512], nk)
                for j, rows in otiles:
                    for kk in range(nk):
                        fc = k0 + kk
                        P.op("pe", lambda e, j=j, rows=rows, kk=kk, fc=fc, db=db: e.matmul(pD[j][:rows, :], lhsT=uT[:, fc, j * 128:j * 128 + rows], rhs=db[:, kk, :], start=(fc == 0), stop=(fc == FC - 1)),
                             reads=[dk, "uT"], writes=["pD%d" % j])
            for j, rows in otiles:
                t = tmp[j % 2]
                cs = slice(cbk * 512, (cbk + 1) * 512)
                P.op("dve", lambda e, j=j, rows=rows, t=t, cs=cs: e.tensor_tensor(out=t[:rows, :], in0=pD[j][:rows, :], in1=g2t[:rows, cs], op=ALU.mult),
                     reads=["pD%d" % j, "g2t"], writes=["tmp%d" % (j % 2)])
                P.op("dve", lambda e, j=j, rows=rows, t=t, cs=cs: e.tensor_tensor(out=xr[:rows, j, cs], in0=t[:rows, :], in1=xr[:rows, j, cs], op=ALU.add),
                     reads=["tmp%d" % (j % 2), "xr%d" % j], writes=["xr%d" % j])
        for j, rows in otiles:
            P.op("pool", lambda e, j=j, rows=rows: e.dma_start(out=out[s + j * 128:s + j * 128 + rows, :], in_=xr[:rows, j, :]),
                 reads=["xr%d" % j], dma=True)
    for wi, (s_, n_) in enumerate(wins):
        do_window(wi, s_, n_)
    return C


def build_outproj(TOK, D, DM):
    C = Ctx()
    P = C.P
    MC, NCB = DM // 128, D // 512
    x = C.din("x", [TOK, D], F32)
    mixT = C.din("mixT", [DM, TOK], BF16)
    wo = C.din("wo", [DM, D], BF16)
    g1row = C.din("g1row", [1, D], F32)
    out = C.dout("out", [TOK, D], F32)
    NT = 4
    xw = [C.sb("xw%d" % i, [128, NT, D], F32) for i in range(2)]
    mT = [C.sb("mT%d" % i, [128, MC, NT * 128], BF16) for i in range(2)]
    ring = [C.sb("ring%d" % i, [128, MC, 512], BF16) for i in range(2)]
    g1t = C.sb("g1t", [128, D], F32)
    tmp = [C.sb("tmp%d" % i, [128, 512], F32) for i in range(2)]
    pD = [C.ps("pD%d" % i, [128, 512], F32) for i in range(4)]
    P.op("sp", lambda e: e.dma_start(out=g1t[:], in_=g1row.partition_broadcast(128)), writes=["g1t"], dma=True)
    cnt = [0, 0]

    def do_window(wi, s, n):
        b = wi % 2
        X, M = xw[b], mT[b]
        tiles = [(j, min(128, n - j * 128)) for j in range((n + 127) // 128)]
        for j, rows in tiles:
            P.op("sp", lambda e, j=j, rows=rows: e.dma_start(out=X[:rows, j, :], in_=x[s + j * 128:s + j * 128 + rows, :]),
                 writes=["xw%d_%d" % (b, j)], dma=True)
        P.op("sp", lambda e: e.dma_start(out=M[:, :, 0:n], in_=mixT[:, s:s + n].rearrange("(c p) n -> p c n", p=128)),
             writes=["mT%d" % b], dma=True)
        for cbk in range(NCB):
            ri = cnt[0] % 2
            cnt[0] += 1
            R_ = ring[ri]
            P.op("sp", lambda e, cbk=cbk, R_=R_: e.dma_start(out=R_[:], in_=wo[:, cbk * 512:(cbk + 1) * 512].rearrange("(c p) n -> p c n", p=128)),
                 writes=["ring%d" % ri], dma=True)
            for j, rows in tiles:
                pi = cnt[1] % 4
                cnt[1] += 1
                pd = pD[pi]
                for kc in range(MC):
                    P.op("pe", lambda e, j=j, rows=rows, kc=kc, pd=pd, R_=R_: e.matmul(pd[:rows, :], lhsT=M[:, kc, j * 128:j * 128 + rows], rhs=R_[:, kc, :], start=(kc == 0), stop=(kc == MC - 1)),
                         reads=["ring%d" % ri, "mT%d" % b], writes=["pD%d" % pi])
                t = tmp[pi % 2]
                cs = slice(cbk * 512, (cbk + 1) * 512)
                P.op("dve", lambda e, rows=rows, pd=pd, t=t, cs=cs: e.tensor_tensor(out=t[:rows, :], in0=pd[:rows, :], in1=g1t[:rows, cs], op=ALU.mult),
                     reads=["pD%d" % pi, "g1t"], writes=["tmp%d" % (pi % 2)])
                P.op("dve", lambda e, j=j, rows=rows, t=t, cs=cs: e.tensor_tensor(out=X[:rows, j, cs], in0=t[:rows, :], in1=X[:rows, j, cs], op=ALU.add),
                     reads=["tmp%d" % (pi % 2), "xw%d_%d" % (b, j)], writes=["xw%d_%d" % (b, j)])
        for j, rows in tiles:
            P.op("pool", lambda e, j=j, rows=rows: e.dma_start(out=out[s + j * 128:s + j * 128 + rows, :], in_=X[:rows, j, :]),
                 reads=["xw%d_%d" % (b, j)], dma=True)

    s = 0
    wi = 0
    while s < TOK:
        n = min(512, TOK - s)
        do_window(wi, s, n)
        s += n
        wi += 1
    return C


def build_ada(D, NU):
    C = Ctx()
    P = C.P
    KC = D // 128
    cT = C.din("cT", [128, KC, 3], F32)
    aw = C.din("aw", [NU, D, 512], F32)
    ab = C.din("ab", [NU, 1, 512], F32)
    out = C.dout("out", [NU, 3, 512], F32)
    ct = C.sb("ct", [128, KC, 3], F32)
    st = C.sb("st", [128, KC, 3], F32)
    ones = C.sb("ones", [1, 3], F32)
    wt = [C.sb("wt%d" % i, [128, KC, 512], F32) for i in range(2)]
    bt = [C.sb("bt%d" % i, [1, 512], F32) for i in range(2)]
    ot = [C.sb("ot%d" % i, [3, 512], F32) for i in range(2)]
    pm = [C.ps("pm%d" % i, [128, 512], F32) for i in range(2)]
    P.op("sp", lambda e: e.dma_start(out=ct[:], in_=cT), writes=["ct"], dma=True)
    P.op("act", lambda e: e.activation(out=st[:], in_=ct[:], func=AF.Silu), reads=["ct"], writes=["st"])
    P.op("dve", lambda e: e.memset(ones[:], 1.0), writes=["ones"])
    for u in range(NU):
        b = u % 2
        P.op("sp", lambda e, u=u, b=b: e.dma_start(out=wt[b][:], in_=aw[u].rearrange("(c p) n -> p c n", p=128)), writes=["wt%d" % b], dma=True)
        P.op("sp", lambda e, u=u, b=b: e.dma_start(out=bt[b][:], in_=ab[u]), writes=["bt%d" % b], dma=True)
        for kc in range(KC):
            P.op("pe", lambda e, kc=kc, b=b: e.matmul(pm[b][0:3, :], lhsT=st[:, kc, :], rhs=wt[b][:, kc, :], start=(kc == 0), stop=False),
                 reads=["st", "wt%d" % b], writes=["pm%d" % b])
        P.op("pe", lambda e, b=b: e.matmul(pm[b][0:3, :], lhsT=ones[:], rhs=bt[b][:], start=False, stop=True),
             reads=["ones", "bt%d" % b], writes=["pm%d" % b])
        P.op("dve", lambda e, b=b: e.tensor_copy(out=ot[b][:], in_=pm[b][0:3, :]), reads=["pm%d" % b], writes=["ot%d" % b])
        P.op("pool", lambda e, u=u, b=b: e.dma_start(out=out[u], in_=ot[b][:]), reads=["ot%d" % b], dma=True)
    return C


def build_cast(shapes):
    C = Ctx()
    P = C.P
    for i, (r, c) in enumerate(shapes):
        src = C.din("w%d" % i, [r, c], F32)
        dst = C.dout("o%d" % i, [r, c], BF16)
        r0 = 0
        while r0 < r:
            nr = min(64, r - r0)
            P.op("pool", lambda e, src=src, dst=dst, r0=r0, nr=nr: e.dma_start(out=dst[r0:r0 + nr, :], in_=src[r0:r0 + nr, :]), dma=True)
            r0 += nr
    return C


def build_qkv(NLAT, NCTX, D, HA, HB, GB):
    C = Ctx()
    P = C.P
    KC = D // 128
    NROW = NLAT + NCTX
    QC = (HA + HB) * 128
    KA0, VA0 = QC, QC + HA * 128
    KB0 = QC + 2 * HA * 128
    VB0 = KB0 + GB * 128
    NIN = VB0 + GB * 128
    NK = HA * 128 + GB * 128
    x = C.din("x", [NROW, D], F32)
    modp = C.din("modp", [128, 2, 3, KC], F32)
    w_in = C.din("w_in", [D, NIN], BF16)
    gvec = C.din("gvec", [128, 4], F32)
    rope = C.din("rope", [4, 128, NROW], F32)
    perm = C.din("perm", [2, 128, 128], F32)
    bones = C.din("bones", [2, 128, 128], BF16)
    ident = C.din("ident", [128, 128], BF16)
    qT = C.dout("qT", [QC, NLAT], BF16)
    kT = C.dout("kT", [NK, NROW], BF16)
    v = C.dout("v", [NROW, NK], BF16)

    NT = 4
    xw = C.sb("xw", [128, NT, D], F32)
    xn = [C.sb("xn%d" % i, [128, D], BF16) for i in range(2)]
    hT = C.sb("hT", [128, KC, 512], BF16)
    NR = 3
    ring = [C.sb("ring%d" % i, [128, KC, 512], BF16) for i in range(NR)]
    rp = [C.sb("rp%d" % i, [128, 4, 512], F32) for i in range(2)]
    idb = C.sb("idb", [128, 128], BF16)
    mp = C.sb("mp", [128, 2, 3, KC], F32)
    A1 = C.sb("A1", [128, 2, KC], F32)
    gv = C.sb("gv", [128, 4], F32)
    pmt = C.sb("pmt", [128, 2, 128], F32)
    bon = C.sb("bon", [128, 2, 128], BF16)
    epsb = C.sb("epsb", [128, 1], F32)
    ss = C.sb("ss", [128, NT], F32)
    rstd = C.sb("rstd", [128, NT], F32)
    junk = C.sb("junk", [128, D], BF16)
    sq = [C.sb("sq%d" % i, [128, 512], BF16) for i in range(3)]
    rs = [C.sb("rs%d" % i, [128, 512], F32) for i in range(3)]
    qn = [C.sb("qn%d" % i, [128, 512], F32) for i in range(3)]
    t1 = [C.sb("t1%d" % i, [128, 512], F32) for i in range(2)]
    t2 = [C.sb("t2%d" % i, [128, 512], F32) for i in range(2)]
    oT = [C.sb("oT%d" % i, [128, 512], BF16) for i in range(2)]
    vs = [C.sb("vs%d" % i, [128, 512], BF16) for i in range(2)]
    pT = C.ps("pT", [128, 2, 512], BF16)
    pQ = [C.ps("pQ%d" % i, [128, 512], F32) for i in range(3)]
    pS = C.ps("pS", [128, 512], F32)
    pR = [C.ps("pR%d" % i, [128, 512], F32) for i in range(2)]
    pV = C.ps("pV", [128, 512], F32)

    P.op("sp", lambda e: e.dma_start(out=idb[:], in_=ident), writes=["idb"], dma=True)
    P.op("sp", lambda e: e.dma_start(out=mp[:], in_=modp), writes=["mp"], dma=True)
    P.op("sp", lambda e: e.dma_start(out=gv[:], in_=gvec), writes=["gv"], dma=True)
    P.op("sp", lambda e: e.dma_start(out=pmt[:], in_=perm.rearrange("t p n -> p t n")), writes=["pmt"], dma=True)
    P.op("sp", lambda e: e.dma_start(out=bon[:], in_=bones.rearrange("t p n -> p t n")), writes=["bon"], dma=True)
    P.op("dve", lambda e: e.memset(epsb[:], 1e-6), writes=["epsb"])
    for t in range(2):
        P.op("dve", lambda e, t=t: e.scalar_tensor_tensor(out=A1[:, t, :], in0=mp[:, t, 1, :], scalar=1.0, in1=mp[:, t, 0, :], op0=ALU.add, op1=ALU.mult),
             reads=["mp"], writes=["A1"])
    P.op("dve", lambda e: e.tensor_scalar(out=gv[:, 0:1], in0=gv[:, 0:1], scalar1=64.0 ** -0.5, scalar2=None, op0=ALU.mult), reads=["gv"], writes=["gv"])
    P.op("dve", lambda e: e.tensor_scalar(out=gv[:, 2:3], in0=gv[:, 2:3], scalar1=128.0 ** -0.5, scalar2=None, op0=ALU.mult), reads=["gv"], writes=["gv"])

    rcnt = [0]
    ccnt = [0]
    vcnt = [0]

    def load_block(c0, ncols):
        i = rcnt[0] % NR
        rcnt[0] += 1
        buf = ring[i]
        P.op("sp", lambda e: e.dma_start(out=buf[:, :, 0:ncols], in_=w_in[:, c0:c0 + ncols].rearrange("(c p) n -> p c n", p=128)),
             writes=["ring%d" % i], dma=True)
        return buf, "ring%d" % i

    def do_window(wi, s, n, seg):
        tiles = [(j, min(128, n - j * 128)) for j in range((n + 127) // 128)]
        rb = wi % 2
        RP = rp[rb]
        for j, rows in tiles:
            P.op("sp", lambda e, j=j, rows=rows: e.dma_start(out=xw[:rows, j, :], in_=x[s + j * 128: s + j * 128 + rows, :]),
                 writes=["xw%d" % j], dma=True)
        P.op("sp", lambda e: e.dma_start(out=RP[:, :, 0:n], in_=rope[:, :, s:s + n].rearrange("t p n -> p t n")), writes=["rp%d" % rb], dma=True)
        for j, rows in tiles:
            P.op("act", lambda e, j=j, rows=rows: e.activation(out=junk[:rows, :], in_=xw[:rows, j, :], func=AF.Square, accum_out=ss[:rows, j:j + 1]),
                 reads=["xw%d" % j], writes=["junk", "ss%d" % j])
        for j, rows in tiles:
            P.op("act", lambda e, j=j, rows=rows: e.activation(out=rstd[:rows, j:j + 1], in_=ss[:rows, j:j + 1], func=AF.Sqrt, scale=1.0 / D, bias=epsb[:rows, :]),
                 reads=["ss%d" % j, "epsb"], writes=["rstd%d" % j])
            P.op("dve", lambda e, j=j, rows=rows: e.reciprocal(out=rstd[:rows, j:j + 1], in_=rstd[:rows, j:j + 1]),
                 reads=["rstd%d" % j], writes=["rstd%d" % j])
        for j, rows in tiles:
            xb = xn[j % 2]
            P.op("dve", lambda e, j=j, rows=rows, xb=xb: e.tensor_scalar(out=xb[:rows, :], in0=xw[:rows, j, :], scalar1=rstd[:rows, j:j + 1], scalar2=None, op0=ALU.mult),
                 reads=["xw%d" % j, "rstd%d" % j], writes=["xn%d" % (j % 2)])
            for g4 in range(KC // 4):
                slot = g4 % 2
                for q in range(4):
                    kc = g4 * 4 + q
                    P.op("pe", lambda e, kc=kc, q=q, rows=rows, xb=xb, slot=slot: e.transpose(out=pT[:, slot, q * 128:q * 128 + rows], in_=xb[:rows, kc * 128:(kc + 1) * 128], identity=idb[:rows, :rows]),
                         reads=["xn%d" % (j % 2), "idb"], writes=["pT"])
                for q in range(4):
                    kc = g4 * 4 + q
                    P.op("act", lambda e, kc=kc, q=q, j=j, rows=rows, slot=slot: e.activation(out=hT[:, kc, j * 128:j * 128 + rows], in_=pT[:, slot, q * 128:q * 128 + rows], func=AF.Identity, scale=A1[:, seg, kc:kc + 1], bias=mp[:, seg, 2, kc:kc + 1]),
                         reads=["pT", "A1", "mp"], writes=["hT"])

        chunks = []
        if seg == 0:
            for blk in range(QC // 512):
                for lc in range(4):
                    ch = blk * 4 + lc
                    typ = 0 if ch < HA else 1
                    chunks.append((blk * 512, 512, lc, typ, 0 if typ == 0 else 2, qT, ch * 128))
        for blk in range(HA * 128 // 512):
            for lc in range(4):
                chunks.append((KA0 + blk * 512, 512, lc, 0, 1, kT, (blk * 4 + lc) * 128))
        for lc in range(GB):
            chunks.append((KB0, GB * 128, lc, 1, 3, kT, HA * 128 + lc * 128))
        cur = {"c0": None, "buf": None, "bk": None}
        base = ccnt[0]
        ccnt[0] += len(chunks)

        def stageA(k):
            c0, ncb, lc, typ, gi, dst, drow = chunks[k]
            if cur["c0"] != c0:
                cur["buf"], cur["bk"] = load_block(c0, ncb)
                cur["c0"] = c0
            buf, bk = cur["buf"], cur["bk"]
            i = (base + k) % 3
            for kc in range(KC):
                P.op("pe", lambda e, kc=kc: e.matmul(pQ[i][:, 0:n], lhsT=buf[:, kc, lc * 128:(lc + 1) * 128], rhs=hT[:, kc, 0:n], start=(kc == 0), stop=(kc == KC - 1)),
                     reads=[bk, "hT"], writes=["pQ%d" % i])
            P.op("act", lambda e: e.activation(out=sq[i][:, 0:n], in_=pQ[i][:, 0:n], func=AF.Square), reads=["pQ%d" % i], writes=["sq%d" % i])

        def stageB(k):
            c0, ncb, lc, typ, gi, dst, drow = chunks[k]
            i = (base + k) % 3
            dh = 64.0 if typ == 0 else 128.0
            P.op("pe", lambda e: e.matmul(pS[:, 0:n], lhsT=bon[:, typ, :], rhs=sq[i][:, 0:n], start=True, stop=True),
                 reads=["bon", "sq%d" % i], writes=["pS"])
            P.op("act", lambda e: e.activation(out=rs[i][:, 0:n], in_=pS[:, 0:n], func=AF.Sqrt, scale=1.0 / dh, bias=epsb[:, :]),
                 reads=["pS", "epsb"], writes=["rs%d" % i])
            P.op("dve", lambda e: e.reciprocal(out=rs[i][:, 0:n], in_=rs[i][:, 0:n]), reads=["rs%d" % i], writes=["rs%d" % i])
            P.op("dve", lambda e: e.scalar_tensor_tensor(out=qn[i][:, 0:n], in0=pQ[i][:, 0:n], scalar=gv[:, gi:gi + 1], in1=rs[i][:, 0:n], op0=ALU.mult, op1=ALU.mult),
                 reads=["pQ%d" % i, "gv", "rs%d" % i], writes=["qn%d" % i])

        def stageC(k):
            c0, ncb, lc, typ, gi, dst, drow = chunks[k]
            i = (base + k) % 3
            j = (base + k) % 2
            P.op("pe", lambda e: e.matmul(pR[j][:, 0:n], lhsT=pmt[:, typ, :], rhs=qn[i][:, 0:n], start=True, stop=True),
                 reads=["pmt", "qn%d" % i], writes=["pR%d" % j])
            P.op("dve", lambda e: e.tensor_tensor(out=t1[j][:, 0:n], in0=qn[i][:, 0:n], in1=RP[:, 2 * typ, 0:n], op=ALU.mult),
                 reads=["qn%d" % i, "rp%d" % rb], writes=["t1%d" % j])
            P.op("dve", lambda e: e.tensor_tensor(out=t2[j][:, 0:n], in0=pR[j][:, 0:n], in1=RP[:, 2 * typ + 1, 0:n], op=ALU.mult),
                 reads=["pR%d" % j, "rp%d" % rb], writes=["t2%d" % j])
            P.op("dve", lambda e: e.tensor_tensor(out=oT[j][:, 0:n], in0=t1[j][:, 0:n], in1=t2[j][:, 0:n], op=ALU.add),
                 reads=["t1%d" % j, "t2%d" % j], writes=["oT%d" % j])
            P.op("pool", lambda e: e.dma_start(out=dst[drow:drow + 128, s:s + n], in_=oT[j][:, 0:n]), reads=["oT%d" % j], dma=True)

        for step in range(len(chunks) + 2):
            if step < len(chunks):
                stageA(step)
            if 0 <= step - 1 < len(chunks):
                stageB(step - 1)
            if 0 <= step - 2 < len(chunks):
                stageC(step - 2)
        vblocks = [(VA0 + b * 512, 512, b * 512) for b in range(HA * 128 // 512)] + [(VB0, GB * 128, HA * 128)]
        for c0, ncols, d0 in vblocks:
            buf, bk = load_block(c0, ncols)
            for j, rows in tiles:
                for kc in range(KC):
                    P.op("pe", lambda e, kc=kc, j=j, rows=rows, ncols=ncols, buf=buf: e.matmul(pV[:rows, 0:ncols], lhsT=hT[:, kc, j * 128:j * 128 + rows], rhs=buf[:, kc, 0:ncols], start=(kc == 0), stop=(kc == KC - 1)),
                         reads=[bk, "hT"], writes=["pV"])
                vi = vcnt[0] % 2
                vcnt[0] += 1
                P.op("act", lambda e, rows=rows, vi=vi, ncols=ncols: e.activation(out=vs[vi][:rows, 0:ncols], in_=pV[:rows, 0:ncols], func=AF.Identity),
                     reads=["pV"], writes=["vs%d" % vi])
                P.op("pool", lambda e, j=j, rows=rows, vi=vi, ncols=ncols, d0=d0: e.dma_start(out=v[s + j * 128:s + j * 128 + rows, d0:d0 + ncols], in_=vs[vi][:rows, 0:ncols]),
                     reads=["vs%d" % vi], dma=True)

    wi = 0
    for (s0, n0, seg) in ((0, NLAT, 0), (NLAT, NCTX, 1)):
        s = s0
        while s < s0 + n0:
            n = min(512, s0 + n0 - s)
            do_window(wi, s, n, seg)
            s += n
            wi += 1
    return C


def build_attn(TQ, NKEY, HA, HB, GB, lam_init):
    C = Ctx()
    P = C.P
    NKT = NKEY // 128
    NQB = TQ // 128
    NLT = NQB + 4
    NKB = NLT * 128
    RG = HB // GB
    QC = (HA + HB) * 128
    qT = C.din("qT", [QC, TQ], BF16)
    kaT = C.din("kaT", [HA * 128, NKEY], BF16)
    va = C.din("va", [NKEY, HA * 128], BF16)
    kbT = C.din("kbT", [GB * 128, NKB], BF16)
    vb = C.din("vb", [NKB, GB * 128], BF16)
    kbias = C.din("kbias", [128, NLT], F32)
    masks = C.din("masks", [2, 128, 128], BF16)
    lamv = C.din("lamv", [1, 256], F32)
    sinkv = C.din("sinkv", [1, HB], F32)
    subg = C.din("subg", [128, 1], F32)
    mixT = C.dout("mixT", [QC, TQ], BF16)

    KT = C.sb("KT", [128, NKEY], BF16)
    VT = C.sb("VT", [128, NKT, 128], BF16)
    QH = [C.sb("QH%d" % i, [128, TQ], BF16) for i in range(2)]
    KB = C.sb("KB", [128, GB, NKB], BF16)
    VB = C.sb("VB", [128, NLT, GB * 128], BF16)
    Pt = [C.sb("Pt%d" % i, [128, 2, 512], BF16) for i in range(3)]
    onesb = C.sb("onesb", [128, 128], BF16)
    onesf = C.sb("onesf", [128, 128], F32)
    Lacc = [[C.sb("Lacc%d_%d" % (i, j), [128, 2, 512], F32) for j in range(2)] for i in range(2)]
    msk = C.sb("msk", [128, 2, 128], BF16)
    kbs = C.sb("kbs", [128, NLT], F32)
    lv = C.sb("lv", [128, 256], F32)
    lt = C.sb("lt", [128, 64], F32)
    ls = C.sb("ls", [128, 4], F32)
    sk = C.sb("sk", [128, HB], F32)
    sg = C.sb("sg", [128, 1], F32)
    epsb = C.sb("epsb", [128, 1], F32)
    f = {nm: [C.sb("%s%d" % (nm, i), [128, 512], F32) for i in range(2)] for nm in ("r1", "r2", "u1", "u2", "oo", "rn")}
    sqo = [C.sb("sqo%d" % i, [128, 512], BF16) for i in range(2)]
    ob = [C.sb("ob%d" % i, [128, 512], BF16) for i in range(2)]
    pS = [C.ps("pS%d" % i, [128, 2, 512], F32) for i in range(2)]
    pO = [C.ps("pO%d" % i, [128, 512], F32) for i in range(2)]
    pL = [C.ps("pL%d" % i, [128, 512], F32) for i in range(2)]

    P.op("dve", lambda e: e.memset(onesb[:], 1.0), writes=["onesb"])
    P.op("dve", lambda e: e.memset(onesf[:], 1.0), writes=["onesf"])
    P.op("dve", lambda e: e.memset(epsb[:], 1e-6), writes=["epsb"])
    P.op("sp", lambda e: e.dma_start(out=msk[:], in_=masks.rearrange("t p n -> p t n")), writes=["msk"], dma=True)
    P.op("sp", lambda e: e.dma_start(out=kbs[:], in_=kbias), writes=["kbs"], dma=True)
    P.op("sp", lambda e: e.dma_start(out=lv[:], in_=lamv.partition_broadcast(128)), writes=["lv"], dma=True)
    P.op("sp", lambda e: e.dma_start(out=sk[:], in_=sinkv.partition_broadcast(128)), writes=["sk"], dma=True)
    P.op("sp", lambda e: e.dma_start(out=sg[:], in_=subg), writes=["sg"], dma=True)
    for t in range(2):
        P.op("dve", lambda e, t=t: e.tensor_tensor(out=lt[:], in0=lv[:, t * 128:t * 128 + 64], in1=lv[:, t * 128 + 64:t * 128 + 128], op=ALU.mult),
             reads=["lv"], writes=["lt"])
        P.op("dve", lambda e, t=t: e.reduce_sum(out=ls[:, t:t + 1], in_=lt[:], axis=AX.X), reads=["lt"], writes=["ls%d" % t])
    P.op("act", lambda e: e.activation(out=ls[:, 0:2], in_=ls[:, 0:2], func=AF.Exp), reads=["ls0", "ls1"], writes=["ls0", "ls1"])
    P.op("dve", lambda e: e.tensor_tensor(out=ls[:, 2:3], in0=ls[:, 1:2], in1=ls[:, 0:1], op=ALU.subtract), reads=["ls0", "ls1"], writes=["ls2"])
    P.op("dve", lambda e: e.tensor_scalar(out=ls[:, 2:3], in0=ls[:, 2:3], scalar1=-float(lam_init), scalar2=None, op0=ALU.add), reads=["ls2"], writes=["ls2"])
    P.op("act", lambda e: e.activation(out=sk[:], in_=sk[:], func=AF.Exp), reads=["sk"], writes=["sk"])
    P.op("dve", lambda e: e.tensor_scalar(out=sg[:], in0=sg[:], scalar1=1.0 - float(lam_init), scalar2=None, op0=ALU.mult), reads=["sg"], writes=["sg"])

    cnt = {"s": 0, "p": 0, "f": 0}

    def diff_head(h):
        qb = h % 2
        Q = QH[qb]
        HK = (NKT + 1) // 2
        for hf, (k0, k1) in enumerate(((0, HK), (HK, NKT))):
            P.op("sp", lambda e, k0=k0, k1=k1: e.dma_start(out=KT[:, k0 * 128:k1 * 128], in_=kaT[h * 128:(h + 1) * 128, k0 * 128:k1 * 128]), writes=["KT%d" % hf], dma=True)
            P.op("sp", lambda e, k0=k0, k1=k1: e.dma_start(out=VT[:, k0:k1, :], in_=va[k0 * 128:k1 * 128, h * 128:(h + 1) * 128].rearrange("(t p) d -> p t d", p=128)), writes=["VT%d" % hf], dma=True)
        P.op("sp", lambda e: e.dma_start(out=Q[:], in_=qT[h * 128:(h + 1) * 128, :]), writes=["QH%d" % qb], dma=True)

        def S(qc, kt):
            si = cnt["s"] % 2
            cnt["s"] += 1
            for m in range(2):
                P.op("pe", lambda e, m=m: e.matmul(pS[si][:, m, :], lhsT=KT[m * 64:(m + 1) * 64, kt * 128:(kt + 1) * 128], rhs=Q[m * 64:(m + 1) * 64, qc * 512:(qc + 1) * 512], start=True, stop=True),
                     reads=["KT%d" % (0 if kt < HK else 1), "QH%d" % qb], writes=["pS%d" % si])
            return si

        def rest(qc, kt, si):
            pi = cnt["p"] % 3
            cnt["p"] += 1
            li = qc % 2
            P.op("act", lambda e: e.activation(out=Pt[pi][:], in_=pS[si][:], func=AF.Exp), reads=["pS%d" % si], writes=["Pt%d" % pi])
            for m in range(2):
                P.op("pe", lambda e, m=m: e.matmul(pO[m][:], lhsT=VT[:, kt, :], rhs=Pt[pi][:, m, :], start=(kt == 0), stop=(kt == NKT - 1)),
                     reads=["VT%d" % (0 if kt < HK else 1), "Pt%d" % pi], writes=["pO%d" % m])
            ab = kt % 2
            acc, ak = Lacc[li][ab], "Lacc%d_%d" % (li, ab)
            if kt < 2:
                P.op("dve", lambda e: e.tensor_copy(out=acc[:], in_=Pt[pi][:]), reads=["Pt%d" % pi], writes=[ak])
            else:
                P.op("dve", lambda e: e.tensor_tensor(out=acc[:], in0=acc[:], in1=Pt[pi][:], op=ALU.add), reads=["Pt%d" % pi, ak], writes=[ak])

        def fin(qc):
            i = cnt["f"] % 2
            cnt["f"] += 1
            r1, r2, u1, u2, oo, rn = [f[k][i] for k in ("r1", "r2", "u1", "u2", "oo", "rn")]
            li = qc % 2
            for m in range(2):
                for ab in range(2):
                    P.op("pe", lambda e, m=m, ab=ab: e.matmul(pL[m][:], lhsT=onesf[:], rhs=Lacc[li][ab][:, m, :], start=(ab == 0), stop=(ab == 1)),
                         reads=["onesf", "Lacc%d_%d" % (li, ab)], writes=["pL%d" % m])
            P.op("dve", lambda e: e.reciprocal(out=r1[:], in_=pL[0][:]), reads=["pL0"], writes=["r1%d" % i])
            P.op("dve", lambda e: e.reciprocal(out=r2[:], in_=pL[1][:]), reads=["pL1"], writes=["r2%d" % i])
            P.op("dve", lambda e: e.tensor_tensor(out=u1[:], in0=pO[0][:], in1=r1[:], op=ALU.mult), reads=["pO0", "r1%d" % i], writes=["u1%d" % i])
            P.op("dve", lambda e: e.tensor_tensor(out=u2[:], in0=pO[1][:], in1=r2[:], op=ALU.mult), reads=["pO1", "r2%d" % i], writes=["u2%d" % i])
            P.op("dve", lambda e: e.scalar_tensor_tensor(out=oo[:], in0=u2[:], scalar=ls[:, 2:3], in1=u1[:], op0=ALU.mult, op1=ALU.add),
                 reads=["u1%d" % i, "u2%d" % i, "ls2"], writes=["oo%d" % i])
            P.op("act", lambda e: e.activation(out=sqo[i][:], in_=oo[:], func=AF.Square), reads=["oo%d" % i], writes=["sqo%d" % i])
            P.op("pe", lambda e: e.matmul(pL[0][:], lhsT=onesb[:], rhs=sqo[i][:], start=True, stop=True), reads=["onesb", "sqo%d" % i], writes=["pL0"])
            P.op("act", lambda e: e.activation(out=rn[:], in_=pL[0][:], func=AF.Sqrt, scale=1.0 / 128, bias=epsb[:]), reads=["pL0", "epsb"], writes=["rn%d" % i])
            P.op("dve", lambda e: e.reciprocal(out=rn[:], in_=rn[:]), reads=["rn%d" % i], writes=["rn%d" % i])
            P.op("dve", lambda e: e.scalar_tensor_tensor(out=ob[i][:], in0=oo[:], scalar=sg[:, 0:1], in1=rn[:], op0=ALU.mult, op1=ALU.mult),
                 reads=["oo%d" % i, "sg", "rn%d" % i], writes=["ob%d" % i])
            P.op("pool", lambda e: e.dma_start(out=mixT[h * 128:(h + 1) * 128, qc * 512:(qc + 1) * 512], in_=ob[i][:]), reads=["ob%d" % i], dma=True)

        for qc in range(TQ // 512):
            si = S(qc, 0)
            for kt in range(NKT):
                nsi = S(qc, kt + 1) if kt + 1 < NKT else None
                rest(qc, kt, si)
                si = nsi
            fin(qc)

    for h in range(HA):
        diff_head(h)

    P.op("sp", lambda e: e.dma_start(out=KB[:], in_=kbT.rearrange("(g p) n -> p g n", p=128)), writes=["KB"], dma=True)
    P.op("sp", lambda e: e.dma_start(out=VB[:], in_=vb.rearrange("(t p) d -> p t d", p=128)), writes=["VB"], dma=True)

    def win_head(g, r):
        hh = g * RG + r
        qrow = (HA + hh) * 128
        qb = hh % 2
        Q = QH[qb]
        P.op("sp", lambda e: e.dma_start(out=Q[:], in_=qT[qrow:qrow + 128, :]), writes=["QH%d" % qb], dma=True)

        def chunk(qc):
            i0 = qc * 4
            items = [(0, 0, 512, []), (1, 0, 512, [])]
            for kt in range(i0 + 2, i0 + 8):
                lo, hi = max(i0, kt - 4), min(i0 + 3, kt - 2)
                if lo > hi:
                    continue
                ml = []
                if i0 <= kt - 2 <= i0 + 3:
                    ml.append((0, (kt - 2 - i0) * 128))
                if i0 <= kt - 4 <= i0 + 3:
                    ml.append((1, (kt - 4 - i0) * 128))
                items.append((kt, (lo - i0) * 128, (hi - lo + 1) * 128, ml))
            last = len(items) - 1
            for ii, (kt, c0, ncw, ml) in enumerate(items):
                si = cnt["s"] % 2
                cnt["s"] += 1
                pi = cnt["p"] % 3
                cnt["p"] += 1
                P.op("pe", lambda e, kt=kt, c0=c0, ncw=ncw, si=si: e.matmul(pS[si][:, 0, c0:c0 + ncw], lhsT=KB[:, g, kt * 128:(kt + 1) * 128], rhs=Q[:, qc * 512 + c0:qc * 512 + c0 + ncw], start=True, stop=True),
                     reads=["KB", "QH%d" % qb], writes=["pS%d" % si])
                P.op("act", lambda e, kt=kt, c0=c0, ncw=ncw, si=si, pi=pi: e.activation(out=Pt[pi][:, 0, c0:c0 + ncw], in_=pS[si][:, 0, c0:c0 + ncw], func=AF.Exp, bias=kbs[:, kt:kt + 1], scale=1.0),
                     reads=["pS%d" % si, "kbs"], writes=["Pt%d" % pi])
                for (mt, mc) in ml:
                    P.op("dve", lambda e, mt=mt, mc=mc, pi=pi: e.tensor_tensor(out=Pt[pi][:, 0, mc:mc + 128], in0=Pt[pi][:, 0, mc:mc + 128], in1=msk[:, mt, :], op=ALU.mult),
                         reads=["Pt%d" % pi, "msk"], writes=["Pt%d" % pi])
                P.op("pe", lambda e, ii=ii, kt=kt, c0=c0, ncw=ncw, pi=pi: e.matmul(pO[0][:, c0:c0 + ncw], lhsT=VB[:, kt, g * 128:(g + 1) * 128], rhs=Pt[pi][:, 0, c0:c0 + ncw], start=(ii == 0), stop=(ii == last)),
                     reads=["VB", "Pt%d" % pi], writes=["pO0"])
                P.op("pe", lambda e, ii=ii, c0=c0, ncw=ncw, pi=pi: e.matmul(pL[0][:, c0:c0 + ncw], lhsT=onesb[:], rhs=Pt[pi][:, 0, c0:c0 + ncw], start=(ii == 0), stop=(ii == last)),
                     reads=["onesb", "Pt%d" % pi], writes=["pL0"])
            i = cnt["f"] % 2
            cnt["f"] += 1
            r1 = f["r1"][i]
            P.op("dve", lambda e: e.tensor_scalar(out=r1[:], in0=pL[0][:], scalar1=sk[:, hh:hh + 1], scalar2=None, op0=ALU.add), reads=["pL0", "sk"], writes=["r1%d" % i])
            P.op("dve", lambda e: e.reciprocal(out=r1[:], in_=r1[:]), reads=["r1%d" % i], writes=["r1%d" % i])
            P.op("dve", lambda e: e.tensor_tensor(out=ob[i][:], in0=pO[0][:], in1=r1[:], op=ALU.mult), reads=["pO0", "r1%d" % i], writes=["ob%d" % i])
            P.op("pool", lambda e: e.dma_start(out=mixT[qrow:qrow + 128, qc * 512:(qc + 1) * 512], in_=ob[i][:]), reads=["ob%d" % i], dma=True)

        for qc in range(TQ // 512):
            chunk(qc)

    for g in range(GB):
        for r in range(RG):
            win_head(g, r)
    return C


def build_hyin(TOK, D, NCH):
    C = Ctx()
    P = C.P
    KC = D // 128
    CC = NCH // 128
    xh = C.din("xh", [TOK + 2, D], F32)
    edge = C.din("edge", [128, 2], F32)
    modp = C.din("modp", [128, 3, KC], F32)
    w_in = C.din("w_in", [D, NCH], BF16)
    cvec = C.din("cvec", [128, 5, CC], F32)
    ident = C.din("ident", [128, 128], BF16)
    zT = C.dout("zT", [NCH, TOK], F32)
    NT = 4
    xw = C.sb("xw", [128, NT, D], F32)
    xn = [C.sb("xn%d" % i, [128, D], BF16) for i in range(2)]
    hT = C.sb("hT", [128, KC, 512], BF16)
    NR = 3
    ring = [C.sb("ring%d" % i, [128, KC, 512], BF16) for i in range(NR)]
    idb = C.sb("idb", [128, 128], BF16)
    mp = C.sb("mp", [128, 3, KC], F32)
    A1 = C.sb("A1", [128, KC], F32)
    cv = C.sb("cv", [128, 5, CC], F32)
    edg = C.sb("edg", [128, 2], F32)
    epsb = C.sb("epsb", [128, 1], F32)
    ss = C.sb("ss", [128, NT], F32)
    rstd = C.sb("rstd", [128, NT], F32)
    junk = C.sb("junk", [128, D], BF16)
    zp = [C.sb("zp%d" % i, [128, 512], F32) for i in range(2)]
    acc = [C.sb("acc%d" % i, [128, 512], F32) for i in range(2)]
    pT = C.ps("pT", [128, 2, 512], BF16)
    pZ = [C.ps("pZ%d" % i, [128, 512], F32) for i in range(2)]
    P.op("sp", lambda e: e.dma_start(out=idb[:], in_=ident), writes=["idb"], dma=True)
    P.op("sp", lambda e: e.dma_start(out=mp[:], in_=modp), writes=["mp"], dma=True)
    P.op("sp", lambda e: e.dma_start(out=cv[:], in_=cvec), writes=["cv"], dma=True)
    P.op("sp", lambda e: e.dma_start(out=edg[:], in_=edge), writes=["edg"], dma=True)
    P.op("dve", lambda e: e.memset(epsb[:], 1e-6), writes=["epsb"])
    P.op("dve", lambda e: e.scalar_tensor_tensor(out=A1[:], in0=mp[:, 1, :], scalar=1.0, in1=mp[:, 0, :], op0=ALU.add, op1=ALU.mult), reads=["mp"], writes=["A1"])
    rcnt = [0]
    wins = windows(TOK, 510)

    def do_window(wi, s, n_out):
        n = n_out + 2
        tiles = [(j, min(128, n - j * 128)) for j in range((n + 127) // 128)]
        for j, rows in tiles:
            P.op("sp", lambda e, j=j, rows=rows: e.dma_start(out=xw[:rows, j, :], in_=xh[s + j * 128: s + j * 128 + rows, :]), writes=["xw%d" % j], dma=True)
        for j, rows in tiles:
            P.op("act", lambda e, j=j, rows=rows: e.activation(out=junk[:rows, :], in_=xw[:rows, j, :], func=AF.Square, accum_out=ss[:rows, j:j + 1]),
                 reads=["xw%d" % j], writes=["junk", "ss%d" % j])
        for j, rows in tiles:
            P.op("act", lambda e, j=j, rows=rows: e.activation(out=rstd[:rows, j:j + 1], in_=ss[:rows, j:j + 1], func=AF.Sqrt, scale=1.0 / D, bias=epsb[:rows, :]),
                 reads=["ss%d" % j, "epsb"], writes=["rstd%d" % j])
            P.op("dve", lambda e, j=j, rows=rows: e.reciprocal(out=rstd[:rows, j:j + 1], in_=rstd[:rows, j:j + 1]), reads=["rstd%d" % j], writes=["rstd%d" % j])
        for j, rows in tiles:
            xb = xn[j % 2]
            P.op("dve", lambda e, j=j, rows=rows, xb=xb: e.tensor_scalar(out=xb[:rows, :], in0=xw[:rows, j, :], scalar1=rstd[:rows, j:j + 1], scalar2=None, op0=ALU.mult),
                 reads=["xw%d" % j, "rstd%d" % j], writes=["xn%d" % (j % 2)])
            for g4 in range(KC // 4):
                slot = g4 % 2
                for q in range(4):
                    kc = g4 * 4 + q
                    P.op("pe", lambda e, kc=kc, q=q, rows=rows, xb=xb, slot=slot: e.transpose(out=pT[:, slot, q * 128:q * 128 + rows], in_=xb[:rows, kc * 128:(kc + 1) * 128], identity=idb[:rows, :rows]),
                         reads=["xn%d" % (j % 2), "idb"], writes=["pT"])
                for q in range(4):
                    kc = g4 * 4 + q
                    P.op("act", lambda e, kc=kc, q=q, j=j, rows=rows, slot=slot: e.activation(out=hT[:, kc, j * 128:j * 128 + rows], in_=pT[:, slot, q * 128:q * 128 + rows], func=AF.Identity, scale=A1[:, kc:kc + 1], bias=mp[:, 2, kc:kc + 1]),
                         reads=["pT", "A1", "mp"], writes=["hT"])

        def chunk(buf, bk, lc, cc):
            i = cc % 2
            for kc in range(KC):
                P.op("pe", lambda e, kc=kc: e.matmul(pZ[i][:, 0:n], lhsT=buf[:, kc, lc * 128:(lc + 1) * 128], rhs=hT[:, kc, 0:n], start=(kc == 0), stop=(kc == KC - 1)),
                     reads=[bk, "hT"], writes=["pZ%d" % i])
            P.op("act", lambda e: e.activation(out=zp[i][:, 0:n], in_=pZ[i][:, 0:n], func=AF.Identity, bias=cv[:, 0, cc:cc + 1], scale=1.0),
                 reads=["pZ%d" % i, "cv"], writes=["zp%d" % i])
            if wi == 0:
                P.op("dve", lambda e: e.tensor_scalar(out=zp[i][:, 0:1], in0=zp[i][:, 0:1], scalar1=edg[:, 0:1], scalar2=None, op0=ALU.mult), reads=["zp%d" % i, "edg"], writes=["zp%d" % i])
            if wi == len(wins) - 1:
                P.op("dve", lambda e: e.tensor_scalar(out=zp[i][:, n - 1:n], in0=zp[i][:, n - 1:n], scalar1=edg[:, 1:2], scalar2=None, op0=ALU.mult), reads=["zp%d" % i, "edg"], writes=["zp%d" % i])
            a = acc[i]
            P.op("dve", lambda e: e.tensor_scalar(out=a[:, 0:n_out], in0=zp[i][:, 0:n_out], scalar1=cv[:, 1, cc:cc + 1], scalar2=cv[:, 4, cc:cc + 1], op0=ALU.mult, op1=ALU.add),
                 reads=["zp%d" % i, "cv"], writes=["acc%d" % i])
            P.op("dve", lambda e: e.scalar_tensor_tensor(out=a[:, 0:n_out], in0=zp[i][:, 1:n_out + 1], scalar=cv[:, 2, cc:cc + 1], in1=a[:, 0:n_out], op0=ALU.mult, op1=ALU.add),
                 reads=["zp%d" % i, "cv", "acc%d" % i], writes=["acc%d" % i])
            P.op("dve", lambda e: e.scalar_tensor_tensor(out=a[:, 0:n_out], in0=zp[i][:, 2:n_out + 2], scalar=cv[:, 3, cc:cc + 1], in1=a[:, 0:n_out], op0=ALU.mult, op1=ALU.add),
                 reads=["zp%d" % i, "cv", "acc%d" % i], writes=["acc%d" % i])
            P.op("pool", lambda e: e.dma_start(out=zT[cc * 128:(cc + 1) * 128, s:s + n_out], in_=a[:, 0:n_out]), reads=["acc%d" % i], dma=True)

        for blk in range(NCH // 512):
            ri = rcnt[0] % NR
            rcnt[0] += 1
            buf = ring[ri]
            P.op("sp", lambda e, blk=blk, buf=buf: e.dma_start(out=buf[:], in_=w_in[:, blk * 512:(blk + 1) * 512].rearrange("(c p) n -> p c n", p=128)),
                 writes=["ring%d" % ri], dma=True)
            for lc in range(4):
                chunk(buf, "ring%d" % ri, lc, blk * 4 + lc)

    for wi, (s_, n_) in enumerate(wins):
        do_window(wi, s_, n_)
    return C


def build_hyfilt(L, CH):
    C = Ctx()
    P = C.P
    CK = CH // 128
    NPC = L // 512
    zT = C.din("zT", [33, L], F32)
    w0 = C.din("w0", [33, 64], F32)
    w12 = C.din("w12", [2, 64, 64], F32)
    bvec = C.din("bvec", [64, 4], F32)
    w3c = C.din("w3c", [64, 4, CH], F32)
    tpos = C.din("tpos", [1, L], F32)
    ndelta = C.din("ndelta", [128, CK], F32)
    dskd = C.din("dsk", [128, 2, CK], F32)
    hraw = C.dint("hraw_scratch", [4, CH, L], F32)
    hnorm = C.dout("hnorm", [4, CH, L], F32)
    hsum = C.dout("hsum", [128, 2, CK], F32)
    dkt = C.sb("dkt", [128, 2, CK], F32)
    rinv = C.sb("rinv", [128, 2, CK], F32)
    NBW = min(2048, L)
    nb = [C.sb("nb%d" % i, [128, NBW], F32) for i in range(2)]
    zt = [C.sb("zt%d" % i, [33, 512], F32) for i in range(2)]
    tb = [C.sb("tb%d" % i, [128, 512], F32) for i in range(2)]
    w0t = C.sb("w0t", [33, 64], F32)
    w12t = C.sb("w12t", [64, 2, 64], F32)
    bv = C.sb("bv", [64, 4], F32)
    f3 = C.sb("f3", [64, 1], F32)
    fb3 = C.sb("fb3", [64, 3], F32)
    w3t = C.sb("w3t", [64, 4, CH], F32)
    nd = C.sb("nd", [128, CK], F32)
    sv = [C.sb("sv%d" % i, [64, 512], F32) for i in range(2)]
    s2 = [C.sb("s2%d" % i, [64, 512], F32) for i in range(2)]
    av = [C.sb("av%d" % i, [64, 512], F32) for i in range(3)]
    dec = [C.sb("dec%d" % i, [128, 512], F32) for i in range(2)]
    hs = [C.sb("hs%d" % i, [128, 512], F32) for i in range(3)]
    part = [C.sb("part%d" % i, [128, 1], F32) for i in range(2)]
    tot = C.sb("tot", [128, 2, CK], F32)
    pm = [C.ps("pm%d" % i, [128, 512], F32) for i in range(2)]
    ph = [C.ps("ph%d" % i, [128, 512], F32) for i in range(3)]
    P.op("sp", lambda e: e.dma_start(out=w0t[:], in_=w0), writes=["w0t"], dma=True)
    P.op("sp", lambda e: e.dma_start(out=w12t[:], in_=w12.rearrange("t k n -> k t n")), writes=["w12t"], dma=True)
    P.op("sp", lambda e: e.dma_start(out=bv[:], in_=bvec), writes=["bv"], dma=True)
    P.op("sp", lambda e: e.dma_start(out=w3t[:], in_=w3c), writes=["w3t"], dma=True)
    P.op("sp", lambda e: e.dma_start(out=nd[:], in_=ndelta), writes=["nd"], dma=True)
    P.op("dve", lambda e: e.memset(tot[:], 0.0), writes=["tot"])
    P.op("dve", lambda e: e.tensor_scalar(out=f3[:], in0=bv[:, 3:4], scalar1=1.0 / 3.0, scalar2=None, op0=ALU.mult), reads=["bv"], writes=["f3"])
    P.op("dve", lambda e: e.tensor_scalar(out=fb3[:], in0=bv[:, 0:3], scalar1=f3[:, 0:1], scalar2=None, op0=ALU.mult), reads=["bv", "f3"], writes=["fb3"])
    cnt = {"m": 0, "h": 0, "p": 0}

    def sin3(src_ps, li, dst, dkey):
        i = cnt["m"] % 2
        P.op("act", lambda e: e.activation(out=sv[i][:], in_=src_ps[0:64, :], func=AF.Sin, scale=f3[:, 0:1], bias=fb3[:, li:li + 1]),
             reads=["pm%d" % i, "f3", "fb3"], writes=["sv%d" % i])
        P.op("dve", lambda e: e.tensor_tensor(out=s2[i][:], in0=sv[i][:], in1=sv[i][:], op=ALU.mult), reads=["sv%d" % i], writes=["s2%d" % i])
        P.op("dve", lambda e: e.tensor_scalar(out=s2[i][:], in0=s2[i][:], scalar1=-4.0, scalar2=3.0, op0=ALU.mult, op1=ALU.add), reads=["s2%d" % i], writes=["s2%d" % i])
        P.op("dve", lambda e: e.tensor_tensor(out=dst[:], in0=s2[i][:], in1=sv[i][:], op=ALU.mult), reads=["s2%d" % i, "sv%d" % i], writes=[dkey])

    def pchunk(pc):
        b = pc % 2
        cs = slice(pc * 512, (pc + 1) * 512)
        P.op("sp", lambda e: e.dma_start(out=zt[b][:], in_=zT[:, cs]), writes=["zt%d" % b], dma=True)
        P.op("sp", lambda e: e.dma_start(out=tb[b][:], in_=tpos[:, cs].partition_broadcast(128)), writes=["tb%d" % b], dma=True)
        srcs = [(w0t[:, :], zt[b][:, :], ["w0t", "zt%d" % b])]
        for li in range(3):
            i = cnt["m"] % 2
            lhsT, rhs, rd = srcs[-1]
            P.op("pe", lambda e, lhsT=lhsT, rhs=rhs, i=i: e.matmul(pm[i][0:64, :], lhsT=lhsT, rhs=rhs, start=True, stop=True), reads=rd, writes=["pm%d" % i])
            sin3(pm[i], li, av[li], "av%d" % li)
            cnt["m"] += 1
            if li < 2:
                srcs.append((w12t[:, li, :], av[li][:, :], ["w12t", "av%d" % li]))
        for ck in range(CK):
            d = dec[ck % 2]
            P.op("act", lambda e, ck=ck, d=d: e.activation(out=d[:], in_=tb[b][:], func=AF.Exp, scale=nd[:, ck:ck + 1]), reads=["tb%d" % b, "nd"], writes=["dec%d" % (ck % 2)])
            for od in range(4):
                hi = cnt["h"] % 3
                cnt["h"] += 1
                P.op("pe", lambda e, od=od, ck=ck, hi=hi: e.matmul(ph[hi][:], lhsT=w3t[:, od, ck * 128:(ck + 1) * 128], rhs=av[2][:], start=True, stop=True),
                     reads=["w3t", "av2"], writes=["ph%d" % hi])
                P.op("dve", lambda e, hi=hi, d=d: e.tensor_tensor(out=hs[hi][:], in0=ph[hi][:], in1=d[:], op=ALU.mult), reads=["ph%d" % hi, "dec%d" % (ck % 2)], writes=["hs%d" % hi])
                P.op("pool", lambda e, od=od, ck=ck, hi=hi: e.dma_start(out=hraw[od, ck * 128:(ck + 1) * 128, cs], in_=hs[hi][:]), reads=["hs%d" % hi], writes=["hraw_%d_%d_%d" % (od, ck, pc * 512 // NBW)], dma=True)
                pi = cnt["p"] % 2
                cnt["p"] += 1
                c0 = 1 if (pc == 0 and od % 2 == 1) else 0
                P.op("dve", lambda e, hi=hi, pi=pi, c0=c0: e.tensor_reduce(out=part[pi][:], in_=hs[hi][:, c0:512], axis=AX.X, op=ALU.add, apply_absolute_value=True),
                     reads=["hs%d" % hi], writes=["part%d" % pi])
                o = od // 2
                P.op("dve", lambda e, pi=pi, o=o, ck=ck: e.tensor_tensor(out=tot[:, o, ck:ck + 1], in0=tot[:, o, ck:ck + 1], in1=part[pi][:], op=ALU.add),
                     reads=["part%d" % pi, "tot"], writes=["tot"])

    for pc in range(NPC):
        pchunk(pc)
    P.op("pool", lambda e: e.dma_start(out=hsum, in_=tot[:]), reads=["tot"], dma=True)
    P.op("sp", lambda e: e.dma_start(out=dkt[:], in_=dskd), writes=["dkt"], dma=True)
    P.op("dve", lambda e: e.reciprocal(out=rinv[:], in_=tot[:]), reads=["tot"], writes=["rinv"])
    k = 0
    for od in range(4):
        o = od // 2
        for ck in range(CK):
            for q in range(L // NBW):
                B_ = nb[k % 2]
                bk = "nb%d" % (k % 2)
                k += 1
                cs = slice(q * NBW, (q + 1) * NBW)
                P.op("sp", lambda e, od=od, ck=ck, cs=cs, B_=B_: e.dma_start(out=B_[:], in_=hraw[od, ck * 128:(ck + 1) * 128, cs]),
                     reads=["hraw_%d_%d_%d" % (od, ck, q)], writes=[bk], dma=True)
                P.op("dve", lambda e, o=o, ck=ck, B_=B_: e.tensor_scalar(out=B_[:], in0=B_[:], scalar1=rinv[:, o, ck:ck + 1], scalar2=None, op0=ALU.mult), reads=[bk, "rinv"], writes=[bk])
                if od % 2 == 0 and q == 0:
                    P.op("dve", lambda e, o=o, ck=ck, B_=B_: e.tensor_scalar(out=B_[:, 0:1], in0=B_[:, 0:1], scalar1=dkt[:, o, ck:ck + 1], scalar2=None, op0=ALU.add), reads=[bk, "dkt"], writes=[bk])
                P.op("pool", lambda e, od=od, ck=ck, cs=cs, B_=B_: e.dma_start(out=hnorm[od, ck * 128:(ck + 1) * 128, cs], in_=B_[:]), reads=[bk], dma=True)
    return C


def build_hyconv(L, CH, NB, LB=1024):
    C = Ctx()
    P = C.P
    CK = CH // 128
    H = L // 2
    zc = C.din("zc", [3, CH, NB, L], F32)
    hraw = C.din("hraw", [4, CH, L], F32)
    hsum = C.din("hsum", [128, 2, CK], F32)
    dsk = C.din("dsk", [128, 2, CK], F32)
    yT = C.dout("yT", [CH, NB, L], BF16)
    U = C.sb("U", [128, L], F32)
    Y = C.sb("Y", [128, L], F32)
    hb = [[C.sb("hb%d_%d" % (d, i), [128, LB], F32) for i in range(2)] for d in range(2)]
    gt = [C.sb("gt%d" % i, [128, LB], F32) for i in range(2)]
    tm = [C.sb("tm%d" % i, [128, LB], F32) for i in range(2)]
    ob = [C.sb("ob%d" % i, [128, LB], BF16) for i in range(2)]
    hsm = C.sb("hsm", [128, 2, CK], F32)
    dk = C.sb("dk", [128, 2, CK], F32)
    P.op("sp", lambda e: e.dma_start(out=hsm[:], in_=hsum), writes=["hsm"], dma=True)
    P.op("sp", lambda e: e.dma_start(out=dk[:], in_=dsk), writes=["dk"], dma=True)
    P.op("dve", lambda e: e.reciprocal(out=hsm[:], in_=hsm[:]), reads=["hsm"], writes=["hsm"])
    cnt = {"h": 0, "g": 0}
    halves = (("dve", 0, L, "Ylo"),)

    def stt(eng, o_ap, in0, sc, in1, rd, wr):
        P.op(eng, lambda e: e.scalar_tensor_tensor(out=o_ap, in0=in0, scalar=sc, in1=in1, op0=ALU.mult, op1=ALU.add), reads=rd, writes=wr)

    def one(ck, b, o):
        P.op("dve", lambda e: e.memset(Y[:, :], 0.0), writes=["Ylo"])
        for lb in range(L // LB):
            bi = cnt["h"] % 2
            cnt["h"] += 1
            hf, hbk = hb[0][bi], hb[1][bi]
            kf, kb = "hb0_%d" % bi, "hb1_%d" % bi
            P.op("sp", lambda e, lb=lb, hf=hf: e.dma_start(out=hf[:], in_=hraw[2 * o, ck * 128:(ck + 1) * 128, lb * LB:(lb + 1) * LB]), writes=[kf], dma=True)
            P.op("sp", lambda e, lb=lb, hbk=hbk: e.dma_start(out=hbk[:], in_=hraw[2 * o + 1, ck * 128:(ck + 1) * 128, lb * LB:(lb + 1) * LB]), writes=[kb], dma=True)
            for j in range(LB):
                tau = lb * LB + j
                for eng, a, bnd, yk in halves:
                    t0 = max(a, tau)
                    if t0 < bnd:
                        stt(eng, Y[:, t0:bnd], U[:, t0 - tau:bnd - tau], hf[:, j:j + 1], Y[:, t0:bnd], [kf, "U", yk], [yk])
                    t1 = min(bnd, L - tau)
                    if tau >= 1 and a < t1:
                        stt(eng, Y[:, a:t1], U[:, a + tau:t1 + tau], hbk[:, j:j + 1], Y[:, a:t1], [kb, "U", yk], [yk])
        for pc in range(L // LB):
            gi = cnt["g"] % 2
            cnt["g"] += 1
            cs = slice(pc * LB, (pc + 1) * LB)
            yk = "Ylo"
            G, T = gt[gi], tm[gi]
            P.op("sp", lambda e, cs=cs, G=G: e.dma_start(out=G[:], in_=zc[1 + o, ck * 128:(ck + 1) * 128, b, cs]), writes=["gt%d" % gi], dma=True)
            P.op("dve", lambda e, cs=cs, T=T: e.tensor_scalar(out=T[:], in0=Y[:, cs], scalar1=hsm[:, o, ck:ck + 1], scalar2=None, op0=ALU.mult), reads=[yk, "hsm"], writes=["tm%d" % gi])
            P.op("dve", lambda e, cs=cs, T=T: e.scalar_tensor_tensor(out=T[:], in0=U[:, cs], scalar=dk[:, o, ck:ck + 1], in1=T[:], op0=ALU.mult, op1=ALU.add), reads=["U", "dk", "tm%d" % gi], writes=["tm%d" % gi])
            if o == 0:
                P.op("dve", lambda e, cs=cs, T=T, G=G: e.tensor_tensor(out=U[:, cs], in0=T[:], in1=G[:], op=ALU.mult), reads=["tm%d" % gi, "gt%d" % gi], writes=["U"])
            else:
                O = ob[gi]
                P.op("dve", lambda e, T=T, G=G, O=O: e.tensor_tensor(out=O[:], in0=T[:], in1=G[:], op=ALU.mult), reads=["tm%d" % gi, "gt%d" % gi], writes=["ob%d" % gi])
                P.op("sp", lambda e, cs=cs, O=O: e.dma_start(out=yT[ck * 128:(ck + 1) * 128, b, cs], in_=O[:]), reads=["ob%d" % gi], dma=True)

    for ck in range(CK):
        for b in range(NB):
            P.op("sp", lambda e, ck=ck, b=b: e.dma_start(out=U[:], in_=zc[0, ck * 128:(ck + 1) * 128, b, :]), writes=["U"], dma=True)
            for o in range(2):
                one(ck, b, o)
    return C


def fft_consts():
    N = 32768
    f64 = np.float64
    n1 = np.arange(128)
    k1 = np.arange(128)
    ang = 2 * np.pi * np.outer(n1, k1) / 128.0
    F128 = np.stack([np.cos(ang), -np.sin(ang)], 1)
    n2 = np.arange(256)
    k2 = np.arange(256)
    ang2 = 2 * np.pi * np.outer(n2, k2) / 256.0
    Fr, Fi = np.cos(ang2), -np.sin(ang2)
    F256 = np.stack([Fr, Fi, -Fi], 1).reshape(2, 128, 3, 256)
    angt = 2 * np.pi * np.outer(n2, k1) / N
    TW = np.stack([np.cos(angt), -np.sin(angt)], 1).reshape(2, 128, 2, 1, 128)
    TW2 = np.repeat(TW, 2, axis=3)
    Cr, Ci = np.cos(ang2), np.sin(ang2)
    IC = np.stack([np.concatenate([Cr, Ci], 1), np.concatenate([-Ci, Cr], 1)], 1).reshape(2, 128, 2, 512)
    angti = 2 * np.pi * np.outer(k1, n2) / N
    TWI = np.stack([np.cos(angti), np.sin(angti)], 1) / N
    angi = 2 * np.pi * np.outer(k1, n1[:64]) / 128.0
    I2M = np.stack([np.cos(angi), -np.sin(angi)], 1)
    c = lambda a: np.ascontiguousarray(a.astype(np.float32))
    return dict(F128=c(F128), F256=c(F256), TW2=c(TW2), IC=c(IC), TWI=c(TWI), I2M=c(I2M))


def build_hyfft(NG, CHG=8, NB=2, fast=True):
    C = Ctx()
    P = C.P
    S = NB * CHG
    NGF = NG
    NCHAN = NG * CHG
    assert NCHAN % S == 0
    NFG = NCHAN // S
    ut = C.din("ut", [NG, 64, S, 256], F32)
    gt = C.din("gt", [2, NG, 64, S, 256], F32)
    ft = C.din("ft", [2, NFG, 128, S, 256], F32)
    F128d = C.din("F128", [128, 2, 128], F32)
    F256d = C.din("F256", [2, 128, 3, 256], F32)
    TW2d = C.din("TW2", [2, 128, 2, 2, 128], F32)
    ICd = C.din("IC", [2, 128, 2, 512], F32)
    TWId = C.din("TWI", [128, 2, 256], F32)
    I2Md = C.din("I2M", [128, 2, 64], F32)
    HFd = C.dint("HFd", [2, NFG, 128, 2, 2, S, 128], F32)
    yt = C.dout("yt", [NG, 64, S, 256], BF16)

    MT = mybir.dt.float32r if fast else F32
    U = C.sb("U", [128, S, 256], MT)
    G = [C.sb("G%d" % i, [128, S, 256], F32) for i in range(2)]
    AT = C.sb("AT", [128, 2, 2, S, 128], MT)
    YS = C.sb("YS", [128, 2, 2, S, 128], MT)
    HF = C.sb("HF", [128, 2, 2, CHG, 128], F32)
    YO = C.sb("YO", [64, S, 256], BF16)
    F128 = C.sb("F128s", [128, 2, 128], MT)
    F256 = C.sb("F256s", [128, 2, 3, 256], MT)
    TW2 = C.sb("TW2s", [128, 2, 2, 2, 128], F32)
    IC = C.sb("ICs", [128, 2, 2, 512], MT)
    TWI = C.sb("TWIs", [128, 2, 256], F32)
    I2M = C.sb("I2Ms", [128, 2, 64], MT)
    NTMP = 2
    tmp = [[C.sb("tmp%d_%d" % (k, i), [128, 512], F32) for i in range(NTMP)] for k in range(4)]
    pA = [C.ps("pA%d" % i, [128, 512], F32) for i in range(2)]
    pX = [[C.ps("pX%d_%d" % (r, i), [128, 512], F32) for i in range(2)] for r in range(2)]
    pY = [C.ps("pY%d" % i, [128, 512], F32) for i in range(2)]
    ZT = AT

    STG = G[1][:, :, :].rearrange("p s n -> p (s n)")

    def stage(dst_flat, src_ap, nel, key):
        P.op("sp", lambda e: e.dma_start(out=STG[:, 0:nel].rearrange(src_ap[1], **src_ap[2]) if src_ap[1] else STG[:, 0:nel], in_=src_ap[0]), writes=["G1"], dma=True)
        P.op("act", lambda e: e.activation(out=dst_flat, in_=STG[:, 0:nel], func=AF.Identity), reads=["G1"], writes=[key])

    stage(F128[:, :, :].rearrange("p r n -> p (r n)"), (F128d.rearrange("p r n -> p (r n)"), None, None), 256, "F128")
    stage(F256[:, :, :, :].rearrange("p t v n -> p (t v n)"), (F256d.rearrange("t p v n -> p t (v n)"), "p (t x) -> p t x", dict(t=2)), 1536, "F256")
    stage(IC[:, :, :, :].rearrange("p t v n -> p (t v n)"), (ICd.rearrange("t p v n -> p t (v n)"), "p (t x) -> p t x", dict(t=2)), 2048, "IC")
    stage(I2M[:, :, :].rearrange("p v n -> p (v n)"), (I2Md.rearrange("p v n -> p (v n)"), None, None), 128, "I2M")
    P.op("sp", lambda e: e.dma_start(out=TW2[:], in_=TW2d.rearrange("h p r s n -> p h r s n")), writes=["TW2"], dma=True)
    P.op("sp", lambda e: e.dma_start(out=TWI[:], in_=TWId), writes=["TWI"], dma=True)
    cnt = {"a": 0, "x": 0, "t": 0, "y": 0}

    def fr(ap):
        return ap

    def cmul(ar, ai, br, bi, outr, outi, rd, wr, neg_first=True):
        ti = cnt["t"] % NTMP
        cnt["t"] += 1
        n = None
        T = [tmp[k][ti] for k in range(4)]
        keys = ["tmp%d_%d" % (k, ti) for k in range(4)]

        def view(t, ref):
            return t

        P.op("dve", lambda e: e.tensor_tensor(out=shape_like(T[0], ar), in0=ar, in1=br, op=ALU.mult), reads=rd, writes=[keys[0]])
        P.op("dve", lambda e: e.tensor_tensor(out=shape_like(T[1], ar), in0=ai, in1=bi, op=ALU.mult), reads=rd, writes=[keys[1]])
        P.op("dve", lambda e: e.tensor_tensor(out=shape_like(T[2], ar), in0=ar, in1=bi, op=ALU.mult), reads=rd, writes=[keys[2]])
        P.op("dve", lambda e: e.tensor_tensor(out=shape_like(T[3], ar), in0=ai, in1=br, op=ALU.mult), reads=rd, writes=[keys[3]])
        P.op("pool", lambda e: e.tensor_tensor(out=outr, in0=shape_like(T[0], ar), in1=shape_like(T[1], ar), op=ALU.subtract), reads=keys[0:2], writes=wr)
        P.op("pool", lambda e: e.tensor_tensor(out=outi, in0=shape_like(T[2], ar), in1=shape_like(T[3], ar), op=ALU.add), reads=keys[2:4], writes=wr)

    def shape_like(t, ref):
        shp = list(ref.shape)
        if len(shp) == 2:
            return t[:, 0:shp[1]]
        assert len(shp) == 3
        return t[:, 0:shp[1] * shp[2]].rearrange("p (a b) -> p a b", a=shp[1])

    def forward(K, ukey):
        for s in range(0, S, 2):
            for h in range(2):
                ai = cnt["a"] % 2
                cnt["a"] += 1
                for d in range(2):
                    P.op("pe", lambda e, s=s, h=h, d=d, ai=ai: e.matmul(pA[ai][:, d * 256:(d + 1) * 256], lhsT=fr(U[0:K, s + d, h * 128:(h + 1) * 128]), rhs=fr(F128[0:K, :, :].rearrange("p r n -> p (r n)")), start=True, stop=True),
                         reads=[ukey, "F128"], writes=["pA%d" % ai])
                pv = pA[ai][:, :].rearrange("p (d r n) -> p d r n", d=2, r=2)
                cmul(pv[:, :, 0, :], pv[:, :, 1, :], TW2[:, h, 0, :, :], TW2[:, h, 1, :, :],
                     AT[:, h, 0, s:s + 2, :], AT[:, h, 1, s:s + 2, :], ["pA%d" % ai, "TW2"], ["AT"])

    def second(consume):
        for m in range(2):
            for c in range(S // 4):
                xi = cnt["x"] % 2
                cnt["x"] += 1
                pr, pi_ = pX[0][xi], pX[1][xi]
                kr, ki = "pX0_%d" % xi, "pX1_%d" % xi
                ms = slice(m * 128, (m + 1) * 128)
                for h in range(2):
                    ar = AT[:, h, 0, c * 4:(c + 1) * 4, :].rearrange("p s n -> p (s n)")
                    ai_ = AT[:, h, 1, c * 4:(c + 1) * 4, :].rearrange("p s n -> p (s n)")
                    P.op("pe", lambda e, h=h, ar=ar, pr=pr, ms=ms: e.matmul(pr[:], lhsT=fr(F256[:, h, 0, ms]), rhs=fr(ar), start=(h == 0), stop=False), reads=["F256", "AT"], writes=[kr])
                    P.op("pe", lambda e, h=h, ai_=ai_, pr=pr, ms=ms: e.matmul(pr[:], lhsT=fr(F256[:, h, 2, ms]), rhs=fr(ai_), start=False, stop=(h == 1)), reads=["F256", "AT"], writes=[kr])
                    P.op("pe", lambda e, h=h, ar=ar, pi_=pi_, ms=ms: e.matmul(pi_[:], lhsT=fr(F256[:, h, 1, ms]), rhs=fr(ar), start=(h == 0), stop=False), reads=["F256", "AT"], writes=[ki])
                    P.op("pe", lambda e, h=h, ai_=ai_, pi_=pi_, ms=ms: e.matmul(pi_[:], lhsT=fr(F256[:, h, 0, ms]), rhs=fr(ai_), start=False, stop=(h == 1)), reads=["F256", "AT"], writes=[ki])
                consume(m, c, pr, pi_, [kr, ki])

    for o in range(2):
        for fg in range(NFG):
            P.op("sp", lambda e, o=o, fg=fg: e.dma_start(out=STG[:, 0:S * 256].rearrange("p (s n) -> p s n", s=S), in_=ft[o, fg]), writes=["G1"], dma=True)
            P.op("act", lambda e: e.activation(out=U[:, :, :].rearrange("p s n -> p (s n)"), in_=STG[:, 0:S * 256], func=AF.Identity), reads=["G1"], writes=["U"])
            forward(128, "U")

            def cons_f(m, c, pr, pi_, keys, o=o, fg=fg):
                ti = cnt["t"] % NTMP
                cnt["t"] += 1
                tr, tii = tmp[0][ti], tmp[1][ti]
                P.op("act", lambda e: e.activation(out=tr[:], in_=pr[:], func=AF.Identity), reads=[keys[0]], writes=["tmp0_%d" % ti])
                P.op("act", lambda e: e.activation(out=tii[:], in_=pi_[:], func=AF.Identity), reads=[keys[1]], writes=["tmp1_%d" % ti])
                P.op("sp", lambda e: e.dma_start(out=HFd[o, fg, :, m, 0, c * 4:(c + 1) * 4, :], in_=tr[:].rearrange("p (s n) -> p s n", s=4)), reads=["tmp0_%d" % ti], writes=["HFd_%d_%d_%d_%d_0" % (o, fg, m, c)], dma=True)
                P.op("sp", lambda e: e.dma_start(out=HFd[o, fg, :, m, 1, c * 4:(c + 1) * 4, :], in_=tii[:].rearrange("p (s n) -> p s n", s=4)), reads=["tmp1_%d" % ti], writes=["HFd_%d_%d_%d_%d_1" % (o, fg, m, c)], dma=True)
            second(cons_f)

    def conv(o, g, ukey, gate, gkey, final):
        fg, off = divmod(g * CHG, S)
        for m in range(2):
            for r in range(2):
                P.op("sp", lambda e, m=m, r=r: e.dma_start(out=HF[:, m, r], in_=HFd[o, fg, :, m, r, off:off + CHG, :]),
                     reads=["HFd_%d_%d_%d_%d_%d" % (o, fg, m, cc, r) for cc in range(S // 4)], writes=["HF"], dma=True)
        forward(64, ukey)

        def cons(m, c, pr, pi_, keys):
            b, c2 = divmod(c, CHG // 4)
            hs = slice(c2 * 4, c2 * 4 + 4)
            cmul(pr[:].rearrange("p (s n) -> p s n", s=4), pi_[:].rearrange("p (s n) -> p s n", s=4),
                 HF[:, m, 0, hs, :], HF[:, m, 1, hs, :],
                 YS[:, m, 0, c * 4:(c + 1) * 4, :], YS[:, m, 1, c * 4:(c + 1) * 4, :], keys + ["HF"], ["YS"])
        second(cons)
        ZTv = ZT[:, :, :, :, :].rearrange("p a b s n -> p (a b s n)").rearrange("p (r s n) -> p r s n", r=2, s=S)
        for s in range(S):
            ai = cnt["a"] % 2
            cnt["a"] += 1
            first = True
            for m in range(2):
                for v, ri in ((0, 0), (1, 1)):
                    last = (m == 1 and v == 1)
                    P.op("pe", lambda e, s=s, m=m, v=v, ri=ri, ai=ai, first=first, last=last: e.matmul(pA[ai][:], lhsT=fr(YS[:, m, ri, s, :]), rhs=fr(IC[:, m, v, :]), start=first, stop=last),
                         reads=["YS", "IC"], writes=["pA%d" % ai])
                    first = False
            cmul(pA[ai][:, 0:256], pA[ai][:, 256:512], TWI[:, 0, :], TWI[:, 1, :], ZTv[:, 0, s, :], ZTv[:, 1, s, :], ["pA%d" % ai, "TWI"], ["AT"])
        for c in range(S // 2):
            yi = cnt["y"] % 2
            cnt["y"] += 1
            zr = ZTv[:, 0, 2 * c:2 * c + 2, :].rearrange("p s n -> p (s n)")
            zi = ZTv[:, 1, 2 * c:2 * c + 2, :].rearrange("p s n -> p (s n)")
            P.op("pe", lambda e, zr=zr, yi=yi: e.matmul(pY[yi][0:64, :], lhsT=fr(I2M[:, 0, :]), rhs=fr(zr), start=True, stop=False), reads=["I2M", "AT"], writes=["pY%d" % yi])
            P.op("pe", lambda e, zi=zi, yi=yi: e.matmul(pY[yi][0:64, :], lhsT=fr(I2M[:, 1, :]), rhs=fr(zi), start=False, stop=True), reads=["I2M", "AT"], writes=["pY%d" % yi])
            gv = gate[0:64, 2 * c:2 * c + 2, :].rearrange("p s n -> p (s n)")
            if not final:
                ov = U[0:64, 2 * c:2 * c + 2, :].rearrange("p s n -> p (s n)")
                P.op("dve", lambda e, yi=yi, gv=gv, ov=ov: e.tensor_tensor(out=ov, in0=pY[yi][0:64, :], in1=gv, op=ALU.mult), reads=["pY%d" % yi, gkey], writes=["U2"])
            else:
                ov = YO[:, 2 * c:2 * c + 2, :].rearrange("p s n -> p (s n)")
                P.op("dve", lambda e, yi=yi, gv=gv, ov=ov: e.tensor_tensor(out=ov, in0=pY[yi][0:64, :], in1=gv, op=ALU.mult), reads=["pY%d" % yi, gkey], writes=["YO"])

    for g in range(NG):
        fg, off = divmod(g * CHG, S)
        P.op("sp", lambda e, g=g: e.dma_start(out=STG[0:64, 0:S * 256].rearrange("p (s n) -> p s n", s=S), in_=ut[g]), writes=["G1"], dma=True)
        P.op("act", lambda e: e.activation(out=U[0:64, :, :].rearrange("p s n -> p (s n)"), in_=STG[0:64, 0:S * 256], func=AF.Identity), reads=["G1"], writes=["U", "U2"])
        for o in range(2):
            P.op("sp", lambda e, g=g, o=o: e.dma_start(out=G[o][0:64], in_=gt[o, g]), writes=["G%d" % o], dma=True)
        conv(0, g, "U", G[0], "G0", False)
        conv(1, g, "U2", G[1], "G1", True)
        P.op("pool", lambda e, g=g: e.dma_start(out=yt[g], in_=YO[:]), reads=["YO"], dma=True)
    return C


def rope_tables_fm(pos, nctx):
    pos = np.asarray(pos, np.float32)
    row = np.floor(pos / 64.0).astype(np.float32)
    col = (pos - 64.0 * row).astype(np.float32)
    out = np.zeros((4, 128, len(pos) + nctx), np.float32)
    out[0, :, :] = 1.0
    out[2, :, :] = 1.0
    for typ, nf in ((0, 16), (1, 32)):
        inv = (np.float32(10000.0) ** (-np.arange(nf, dtype=np.float32) / np.float32(nf))).astype(np.float32)
        p = np.arange(128)
        axis = (p % (4 * nf)) // (2 * nf)
        f = p % nf
        ang = np.where(axis[:, None] == 0, row[None, :], col[None, :]).astype(np.float32) * inv[f][:, None]
        out[2 * typ, :, :len(pos)] = np.cos(ang)
        out[2 * typ + 1, :, :len(pos)] = np.sin(ang)
    return out

def perm_mats():
    pm = np.zeros((2, 128, 128), np.float32)
    for typ, nf in ((0, 16), (1, 32)):
        for pd in range(128):
            half = (pd % (2 * nf)) // nf
            if half == 0:
                pm[typ, pd + nf, pd] = -1.0
            else:
                pm[typ, pd - nf, pd] = 1.0
    return pm

def block_ones():
    b = np.zeros((2, 128, 128), np.float32)
    b[0, :64, :64] = 1; b[0, 64:, 64:] = 1
    b[1] = 1
    return b.astype(NPBF)


B_, L_, D_, DFF_ = 2, 16384, 2048, 5632
NCORE = 8
TOK_ = L_ * B_ // NCORE
SH_ = L_ // TOK_
NCTX_ = 256
HA_, HB_, GB_ = 8, 8, 2
_f32 = np.float32


def _bf(a):
    return np.ascontiguousarray(np.asarray(a)).astype(NPBF)


def _halo(full, b, t0):
    out = np.zeros((TOK_ + 2, full.shape[-1]), _f32)
    lo, hi = t0 - 1, t0 + TOK_ + 1
    slo, shi = max(lo, 0), min(hi, L_)
    out[slo - lo:shi - lo] = full[b, slo:shi]
    return out


def _edge(q):
    e = np.ones((128, 2), _f32)
    if q == 0:
        e[:, 0] = 0
    if q == SH_ - 1:
        e[:, 1] = 0
    return e


def _filt_consts(L, ch0, CH, Dm):
    t = np.linspace(0.0, 1.0, L, dtype=_f32)
    bands = 16
    w = (2.0 * math.pi * np.arange(L, dtype=_f32) / L).astype(_f32)
    f = np.linspace(1e-4, bands - 1, bands, dtype=_f32)
    z = np.concatenate([t[:, None], np.cos(f[None] * w[:, None]), -np.sin(f[None] * w[:, None])], -1).astype(_f32)
    maxd = math.log(1e-2) / 0.3
    mind = math.log(1e-2) / 1.5
    deltas = np.linspace(mind, maxd, Dm, dtype=_f32)
    nd = -np.abs(deltas[ch0:ch0 + CH])
    return np.ascontiguousarray(z.T), t[None], pl(nd)


def kernel(**inp):
    I = {k: np.asarray(v) for k, v in inp.items()}
    ident = np.eye(128).astype(NPBF)
    cores = [(c // SH_, c % SH_) for c in range(NCORE)]

    wlist = [I["attn_w_in"][0], I["attn_w_out"][0], I["hy_w_in"][0], I["hy_w_out"][0],
             I["ffn_w_gate"][0], I["ffn_w_val"][0], I["ffn_w_down"][0],
             I["ffn_w_gate"][1], I["ffn_w_val"][1], I["ffn_w_down"][1]]
    shapes = [(w.shape[0] // NCORE, w.shape[1]) for w in wlist]
    nc = build_cast(shapes).finish()
    ims = [{("w%d" % i): np.ascontiguousarray(w[c * s[0]:(c + 1) * s[0]]) for i, (w, s) in enumerate(zip(wlist, shapes))} for c in range(NCORE)]
    res = run(nc, ims)
    wb = [np.concatenate([res[c]["o%d" % i] for c in range(NCORE)], 0) for i in range(len(wlist))]
    w_in_b, w_out_b, hy_in_b, hy_out_b = wb[0:4]
    ffn_b = [wb[4:7], wb[7:10]]
    del res, ims

    NU = 6
    cT = pl(np.stack([I["c"][0], I["c"][1], I["c_ctx"]], 1))
    units = [(u // 24, u % 24) for u in range(48)]
    nc = build_ada(D_, NU).finish()
    ims = []
    for c in range(NCORE):
        us = units[c * NU:(c + 1) * NU]
        ims.append(dict(cT=cT, aw=np.stack([I["ada_w"][l][:, cb * 512:(cb + 1) * 512] for l, cb in us]),
                        ab=np.stack([I["ada_b"][l][None, cb * 512:(cb + 1) * 512] for l, cb in us])))
    res = run(nc, ims)
    m = np.zeros((2, 3, 6 * D_), _f32)
    for c in range(NCORE):
        for k, (l, cb) in enumerate(units[c * NU:(c + 1) * NU]):
            m[l, :, cb * 512:(cb + 1) * 512] = res[c]["out"][k]
    mod = m.reshape(2, 3, 6, D_)
    del res, ims

    x, ctx = I["x"], I["ctx"]

    def post_mixer(l, xres, mix_of_core, wo_b):
        nc = build_outproj(TOK_, D_, D_).finish()
        ims = []
        for c, (b, q) in enumerate(cores):
            t0 = q * TOK_
            ims.append(dict(x=np.ascontiguousarray(xres[b, t0:t0 + TOK_]), mixT=mix_of_core(c), wo=wo_b, g1row=np.ascontiguousarray(mod[l, b, 2][None])))
        res = run(nc, ims)
        xs1 = np.stack([np.concatenate([res[b * SH_ + q]["out"] for q in range(SH_)], 0) for b in range(B_)], 0)
        del res, ims
        nc = build_ffn(TOK_, D_, DFF_).finish()
        wg, wv, wd = ffn_b[l]
        cw = np.ascontiguousarray(pl(I["ffn_conv_w"][l].T).transpose(0, 2, 1))
        cbv = pl(I["ffn_conv_b"][l])
        ims = []
        for c, (b, q) in enumerate(cores):
            t0 = q * TOK_
            ims.append(dict(xh=_halo(xs1, b, t0), edge=_edge(q), modp=np.stack([pl(I["norm2_g"][l]), pl(mod[l, b, 4]), pl(mod[l, b, 3])], 1),
                            g2row=np.ascontiguousarray(mod[l, b, 5][None]), wg=wg, wv=wv, wd=wd, cw=cw, cb=cbv, ident=ident))
        res = run(nc, ims)
        return np.stack([np.concatenate([res[b * SH_ + q]["out"] for q in range(SH_)], 0) for b in range(B_)], 0)

    nc = build_qkv(TOK_, NCTX_, D_, HA_, HB_, GB_).finish()
    gvec = np.stack([np.tile(I["a_q_g"][0], 2), np.tile(I["a_k_g"][0], 2), I["b_q_g"][0], I["b_k_g"][0]], 1).astype(_f32)
    pm, bo = perm_mats(), block_ones()
    ims = []
    for c, (b, q) in enumerate(cores):
        t0 = q * TOK_
        modp = np.stack([np.stack([pl(I["norm1_g"][0]), pl(mod[0, r, 1]), pl(mod[0, r, 0])], 1) for r in (b, 2)], 1)
        ims.append(dict(x=np.concatenate([x[b, t0:t0 + TOK_], ctx[b]], 0), modp=modp, w_in=w_in_b, gvec=gvec,
                        rope=rope_tables_fm(t0 + np.arange(TOK_), NCTX_), perm=pm, bones=bo, ident=ident))
    res = run(nc, ims)
    qTs = [res[c]["qT"] for c in range(NCORE)]
    NA = HA_ * 128
    kaT, va, kbT, vb = [], [], [], []
    for b in range(B_):
        rs = [res[b * SH_ + q] for q in range(SH_)]
        kaT.append(np.concatenate([rs[0]["kT"][:NA, TOK_:]] + [r["kT"][:NA, :TOK_] for r in rs], 1))
        va.append(np.concatenate([rs[0]["v"][TOK_:, :NA]] + [r["v"][:TOK_, :NA] for r in rs], 0))
        kbT.append((rs[0]["kT"][NA:, TOK_:], np.concatenate([r["kT"][NA:, :TOK_] for r in rs], 1)))
        vb.append((rs[0]["v"][TOK_:, NA:], np.concatenate([r["v"][:TOK_, NA:] for r in rs], 0)))
    del res, ims

    NKEY = NCTX_ + L_
    NLT = TOK_ // 128 + 4
    lam_init = 0.8 - 0.6 * math.exp(-0.3 * 0)
    nc = build_attn(TOK_, NKEY, HA_, HB_, GB_, lam_init).finish()
    a = np.arange(128)
    masks = np.stack([(a[:, None] >= a[None, :]), (a[:, None] <= a[None, :])], 0).astype(NPBF)
    lamv = np.concatenate([I["a_lam_q1"][0], I["a_lam_k1"][0], I["a_lam_q2"][0], I["a_lam_k2"][0]])[None].astype(_f32)
    ims = []
    for c, (b, q) in enumerate(cores):
        t0 = q * TOK_
        kc_, kl_ = kbT[b]
        vc_, vl_ = vb[b]
        kloc = np.zeros((GB_ * 128, NLT * 128), NPBF)
        vloc = np.zeros((NLT * 128, GB_ * 128), NPBF)
        kbias = np.zeros((128, NLT), _f32)
        kloc[:, :NCTX_] = kc_
        vloc[:NCTX_] = vc_
        lo, hi = t0 - 128, t0 + TOK_ + 128
        slo, shi = max(lo, 0), min(hi, L_)
        kloc[:, NCTX_ + (slo - lo):NCTX_ + (shi - lo)] = kl_[:, slo:shi]
        vloc[NCTX_ + (slo - lo):NCTX_ + (shi - lo)] = vl_[slo:shi]
        if lo < 0:
            kbias[:, 2] = -30000.0
        if hi > L_:
            kbias[:, NLT - 1] = -30000.0
        ims.append(dict(qT=qTs[c], kaT=kaT[b], va=va[b], kbT=kloc, vb=vloc, kbias=kbias, masks=masks, lamv=lamv,
                        sinkv=I["b_sink"][0][None].astype(_f32), subg=I["a_sub_g"][0][:, None].astype(_f32)))
    res = run(nc, ims)
    mix0 = [res[c]["mixT"] for c in range(NCORE)]
    del res, ims, kaT, va, kbT, vb, qTs

    xs = post_mixer(0, x, lambda c: mix0[c], w_out_b)
    del mix0

    NCH = 3 * D_
    nc = build_hyin(TOK_, D_, NCH).finish()
    cvec = np.stack([pl(I["hy_b_in"][0]), pl(I["hy_sconv_w"][0][0]), pl(I["hy_sconv_w"][0][1]), pl(I["hy_sconv_w"][0][2]), pl(I["hy_sconv_b"][0])], 1)
    ims = []
    for c, (b, q) in enumerate(cores):
        t0 = q * TOK_
        ims.append(dict(xh=_halo(xs, b, t0), edge=_edge(q), modp=np.stack([pl(I["norm1_g"][1]), pl(mod[1, b, 1]), pl(mod[1, b, 0])], 1),
                        w_in=hy_in_b, cvec=cvec, ident=ident))
    res = run(nc, ims)
    zfull = np.zeros((NCH, B_, L_), _f32)
    for c, (b, q) in enumerate(cores):
        zfull[:, b, q * TOK_:(q + 1) * TOK_] = res[c]["zT"]
    del res, ims

    CH = D_ // NCORE
    nc = build_hyfilt(L_, CH).finish()
    w3 = I["hy_f_w3"][0].reshape(64, 4, D_)
    bvec = np.stack([I["hy_f_b0"][0], I["hy_f_b1"][0], I["hy_f_b2"][0], I["hy_f_freq"][0]], 1).astype(_f32)
    ims = []
    for c in range(NCORE):
        zT_, tpos, nd = _filt_consts(L_, c * CH, CH, D_)
        dsk = np.ascontiguousarray(pl(I["hy_d"][0][:, c * CH:(c + 1) * CH].T).transpose(0, 2, 1))
        ims.append(dict(zT=zT_, w0=I["hy_f_w0"][0], w12=np.stack([I["hy_f_w1"][0], I["hy_f_w2"][0]]), bvec=bvec,
                        w3c=np.ascontiguousarray(w3[:, :, c * CH:(c + 1) * CH]), tpos=tpos, ndelta=nd, dsk=dsk))
    fres = run(nc, ims)
    del ims

    CHG, S = 8, 16
    NG = CH // CHG
    nc = build_hyfft(NG, CHG, B_).finish()
    fc = fft_consts()

    def grp(a_):
        v = a_.reshape(NG, CHG, B_, 64, 256).transpose(0, 3, 2, 1, 4)
        return np.ascontiguousarray(v.reshape(NG, 64, S, 256))

    ims = []
    for c in range(NCORE):
        hn = fres[c]["hnorm"]
        filt = np.zeros((2, CH, 2 * L_), _f32)
        for o in range(2):
            filt[o, :, :L_] = hn[2 * o]
            filt[o, :, L_ + 1:] = hn[2 * o + 1][:, :0:-1]
        ft = filt.reshape(2, CH // S, S, 128, 256).transpose(0, 1, 3, 2, 4)
        zc = [zfull[k * D_ + c * CH:k * D_ + (c + 1) * CH] for k in range(3)]
        ims.append(dict(ut=grp(zc[0]), gt=np.stack([grp(zc[1]), grp(zc[2])]), ft=np.ascontiguousarray(ft), **fc))
    res = run(nc, ims)
    yfull = np.concatenate([np.asarray(res[c]["yt"]).reshape(NG, 64, B_, CHG, 256).transpose(0, 3, 2, 1, 4).reshape(CH, B_, L_)
                            for c in range(NCORE)], 0)
    del res, ims, fres, zfull

    def mix1(c):
        b, q = cores[c]
        return np.ascontiguousarray(yfull[:, b, q * TOK_:(q + 1) * TOK_])
    out = post_mixer(1, xs, mix1, hy_out_b)
    return out.astype(np.float32)
```

```python
import math
import numpy as np
import concourse.bass as bass
import concourse.mybir as mybir
from concourse.bass_utils import run_bass_kernel_spmd

F32 = mybir.dt.float32
BF16 = mybir.dt.bfloat16
AF = mybir.ActivationFunctionType
ALU = mybir.AluOpType
AX = mybir.AxisListType


class Prog:
    ENGS = ("pe", "act", "dve", "pool", "sp")

    def __init__(self, nc):
        self.nc = nc
        self.ops = []
        self.relaxed = set()
        self.last_w = {}
        self.readers = {}

    def op(self, eng, fn, reads=(), writes=(), dma=False, relaxed=False):
        self.ops.append((eng, fn, tuple(reads), tuple(writes), dma))
        if relaxed:
            self.relaxed.add(len(self.ops) - 1)

    def emit(self):
        nc = self.nc
        ops = self.ops
        n = len(ops)
        deps = [None] * n
        last_w, readers = {}, {}
        has_dep = [False] * n
        for i, (eng, fn, reads, writes, dma) in enumerate(ops):
            d = set()
            for r in reads:
                if r in last_w:
                    d.add(last_w[r])
            for w in writes:
                if w in last_w:
                    d.add(last_w[w])
                for rr in readers.get(w, ()):
                    d.add(rr)
            d.discard(i)
            deps[i] = d
            for j in d:
                has_dep[j] = True
            for r in reads:
                readers.setdefault(r, []).append(i)
            for w in writes:
                last_w[w] = i
                readers[w] = []
        eng_cnt = {e: 0 for e in self.ENGS}
        sig = [None] * n
        prevw = [None] * n
        NDMA = {"sp": 12, "pool": 8, "act": 4, "dve": 2, "pe": 2}
        dma_cnt = {(e, k): 0 for e in self.ENGS for k in range(NDMA[e])}
        dma_rr = {e: 0 for e in self.ENGS}
        for i, (eng, fn, reads, writes, dma) in enumerate(ops):
            if dma:
                k = dma_rr[eng] % NDMA[eng]
                dma_rr[eng] += 1
                prevw[i] = (("dma", eng, k), dma_cnt[(eng, k)])
                dma_cnt[(eng, k)] += 16
                sig[i] = (("dma", eng, k), dma_cnt[(eng, k)])
            elif has_dep[i]:
                eng_cnt[eng] += 1
                sig[i] = ((eng,), eng_cnt[eng])
        self.n_sig = dict(eng_cnt)
        streams = {e: [] for e in self.ENGS}
        waited = {e: {} for e in self.ENGS}
        for i, (eng, fn, reads, writes, dma) in enumerate(ops):
            need = {}
            for j in deps[i]:
                sk, v = sig[j]
                ej = ops[j][0]
                if (not ops[j][4]) and ej == eng and eng == "pe":
                    continue
                if (not ops[j][4]) and ej == eng and i in self.relaxed:
                    continue
                if v > need.get(sk, 0):
                    need[sk] = v
            ws = []
            if prevw[i] is not None and prevw[i][1] > 0:
                sk, v = prevw[i]
                if v > need.get(sk, 0):
                    need[sk] = v
            for sk, v in need.items():
                if waited[eng].get(sk, 0) >= v:
                    continue
                waited[eng][sk] = v
                ws.append((sk, v))
            streams[eng].append((ws, fn, sig[i]))
        self.streams = streams
        import contextlib
        with contextlib.ExitStack() as es:
            sems = {}
            for e in self.ENGS:
                sems[(e,)] = es.enter_context(nc.semaphore("s_" + e))
            for (e, k), c in dma_cnt.items():
                if c:
                    sems[("dma", e, k)] = es.enter_context(nc.semaphore("s_dma_%s%d" % (e, k)))
            block = es.enter_context(nc.Block())
            engobj = {"pe": "tensor", "act": "scalar", "dve": "vector", "pool": "gpsimd", "sp": "sync"}

            def mk(ename):
                def body(eng):
                    for ws, fn, sg in streams[ename]:
                        for sk, v in ws:
                            eng.wait_ge(sems[sk], v)
                        ins = fn(eng)
                        if sg is not None:
                            sk, v = sg
                            ins.then_inc(sems[sk], 16 if sk[0] == "dma" else 1)
                    for k in range(NDMA[ename]):
                        if dma_cnt[(ename, k)]:
                            eng.wait_ge(sems[("dma", ename, k)], dma_cnt[(ename, k)])
                return body

            for e in self.ENGS:
                if streams[e]:
                    getattr(block, engobj[e])(mk(e))


import contextlib
import ml_dtypes
NPBF = ml_dtypes.bfloat16


class Ctx:
    def __init__(self):
        self.nc = bass.Bass("TRN2", target_bir_lowering=False)
        self.P = Prog(self.nc)
        self.es = contextlib.ExitStack()
        self.n = 0

    def din(self, name, shape, dt):
        return self.nc.dram_tensor(name, list(shape), dt, kind="ExternalInput").ap()

    def dout(self, name, shape, dt):
        return self.nc.dram_tensor(name, list(shape), dt, kind="ExternalOutput").ap()

    def dint(self, name, shape, dt):
        return self.nc.dram_tensor(name, list(shape), dt, kind="Internal").ap()

    def sb(self, name, shape, dt):
        return self.es.enter_context(self.nc.sbuf_tensor(name, list(shape), dt))

    def ps(self, name, shape, dt):
        return self.es.enter_context(self.nc.psum_tensor(name, list(shape), dt))

    def finish(self):
        self.P.emit()
        self.es.close()
        return self.nc


def run(nc, in_maps):
    res = run_bass_kernel_spmd(nc, in_maps, core_ids=list(range(len(in_maps))))
    return res.results


def pl(v, p=128):
    v = np.asarray(v)
    return np.ascontiguousarray(v.reshape((-1, p) + v.shape[1:]).swapaxes(0, 1))


def windows(TOK, WN):
    out, s = [], 0
    while s < TOK:
        n = min(WN, TOK - s)
        out.append((s, n))
        s += n
    return out


def build_ffn(TOK, D, DFF, WN=382):
    C = Ctx()
    P = C.P
    KC, FC = D // 128, DFF // 128
    NCB = D // 512
    xh = C.din("xh", [TOK + 2, D], F32)
    edge = C.din("edge", [128, 2], F32)
    modp = C.din("modp", [128, 3, KC], F32)
    g2row = C.din("g2row", [1, D], F32)
    wg = C.din("wg", [D, DFF], BF16)
    wv = C.din("wv", [D, DFF], BF16)
    wd = C.din("wd", [DFF, D], BF16)
    cw = C.din("cw", [128, 3, FC], F32)
    cb = C.din("cb", [128, FC], F32)
    ident = C.din("ident", [128, 128], BF16)
    out = C.dout("out", [TOK, D], F32)

    NT = 3
    xw = C.sb("xw", [128, NT, D], F32)
    xr = C.sb("xr", [128, NT, D], F32)
    xn = [C.sb("xn%d" % i, [128, D], BF16) for i in range(2)]
    hT = C.sb("hT", [128, KC, NT * 128], BF16)
    uT = C.sb("uT", [128, FC, NT * 128], BF16)
    NR = 3
    ring = [C.sb("ring%d" % i, [128, 16, 512], BF16) for i in range(NR)]
    g2t = C.sb("g2t", [128, D], F32)
    idb = C.sb("idb", [128, 128], BF16)
    mp = C.sb("mp", [128, 3, KC], F32)
    A2 = C.sb("A2", [128, KC], F32)
    cwt = C.sb("cwt", [128, 3, FC], F32)
    cbt = C.sb("cbt", [128, FC], F32)
    edg = C.sb("edg", [128, 2], F32)
    epsb = C.sb("epsb", [128, 1], F32)
    ss = C.sb("ss", [128, NT], F32)
    rstd = C.sb("rstd", [128, NT], F32)
    junk = C.sb("junk", [128, D], BF16)
    acc = [C.sb("acc%d" % i, [128, NT * 128], F32) for i in range(2)]
    gl = [C.sb("gl%d" % i, [128, NT * 128], F32) for i in range(2)]
    tmp = [C.sb("tmp%d" % i, [128, 512], F32) for i in range(2)]
    pT = C.ps("pT", [128, 2, 512], BF16)
    pG = [C.ps("pG%d" % i, [128, 512], F32) for i in range(2)]
    pV = [C.ps("pV%d" % i, [128, 512], F32) for i in range(2)]
    pD = [C.ps("pD%d" % i, [128, 512], F32) for i in range(NT)]

    P.op("sp", lambda e: e.dma_start(out=idb[:], in_=ident), writes=["idb"], dma=True)
    P.op("sp", lambda e: e.dma_start(out=mp[:], in_=modp), writes=["mp"], dma=True)
    P.op("sp", lambda e: e.dma_start(out=cwt[:], in_=cw), writes=["cwt"], dma=True)
    P.op("sp", lambda e: e.dma_start(out=cbt[:], in_=cb), writes=["cbt"], dma=True)
    P.op("sp", lambda e: e.dma_start(out=edg[:], in_=edge), writes=["edg"], dma=True)
    P.op("sp", lambda e: e.dma_start(out=g2t[:], in_=g2row.partition_broadcast(128)), writes=["g2t"], dma=True)
    P.op("dve", lambda e: e.memset(epsb[:], 1e-6), writes=["epsb"])
    P.op("dve", lambda e: e.scalar_tensor_tensor(out=A2[:], in0=mp[:, 1, :], scalar=1.0, in1=mp[:, 0, :], op0=ALU.add, op1=ALU.mult),
         reads=["mp"], writes=["A2"])

    rcnt = [0]

    def load_block(src_ap, nk):
        i = rcnt[0] % NR
        rcnt[0] += 1
        buf = ring[i]
        key = "ring%d" % i
        P.op("sp", lambda e: e.dma_start(out=buf[:, 0:nk, :], in_=src_ap.rearrange("(c p) n -> p c n", p=128)),
             writes=[key], dma=True)
        return buf, key

    wins = windows(TOK, WN)
    def do_window(wi, s, n_out):
        n_in = n_out + 2
        tiles = [(j, min(128, n_in - j * 128)) for j in range((n_in + 127) // 128)]
        for j, rows in tiles:
            P.op("sp", lambda e, j=j, rows=rows: e.dma_start(out=xw[:rows, j, :], in_=xh[s + j * 128: s + j * 128 + rows, :]),
                 writes=["xw%d" % j], dma=True)
        for j in range((n_out + 127) // 128):
            rows = min(128, n_out - j * 128)
            P.op("sp", lambda e, j=j, rows=rows: e.dma_start(out=xr[:rows, j, :], in_=xh[s + 1 + j * 128: s + 1 + j * 128 + rows, :]),
                 writes=["xr%d" % j], dma=True)
        for j, rows in tiles:
            P.op("act", lambda e, j=j, rows=rows: e.activation(out=junk[:rows, :], in_=xw[:rows, j, :], func=AF.Square, accum_out=ss[:rows, j:j + 1]),
                 reads=["xw%d" % j], writes=["junk", "ss%d" % j])
        for j, rows in tiles:
            P.op("act", lambda e, j=j, rows=rows: e.activation(out=rstd[:rows, j:j + 1], in_=ss[:rows, j:j + 1], func=AF.Sqrt, scale=1.0 / D, bias=epsb[:rows, :]),
                 reads=["ss%d" % j, "epsb"], writes=["rstd%d" % j])
            P.op("dve", lambda e, j=j, rows=rows: e.reciprocal(out=rstd[:rows, j:j + 1], in_=rstd[:rows, j:j + 1]),
                 reads=["rstd%d" % j], writes=["rstd%d" % j])
        for j, rows in tiles:
            xb = xn[j % 2]
            P.op("dve", lambda e, j=j, rows=rows, xb=xb: e.tensor_scalar(out=xb[:rows, :], in0=xw[:rows, j, :], scalar1=rstd[:rows, j:j + 1], scalar2=None, op0=ALU.mult),
                 reads=["xw%d" % j, "rstd%d" % j], writes=["xn%d" % (j % 2)])
            for g4 in range(KC // 4):
                slot = g4 % 2
                for q in range(4):
                    kc = g4 * 4 + q
                    P.op("pe", lambda e, kc=kc, q=q, rows=rows, xb=xb, slot=slot: e.transpose(out=pT[:, slot, q * 128:q * 128 + rows], in_=xb[:rows, kc * 128:(kc + 1) * 128], identity=idb[:rows, :rows]),
                         reads=["xn%d" % (j % 2), "idb"], writes=["pT"])
                for q in range(4):
                    kc = g4 * 4 + q
                    P.op("act", lambda e, kc=kc, q=q, j=j, rows=rows, slot=slot: e.activation(out=hT[:, kc, j * 128:j * 128 + rows], in_=pT[:, slot, q * 128:q * 128 + rows], func=AF.Identity, scale=A2[:, kc:kc + 1], bias=mp[:, 2, kc:kc + 1]),
                         reads=["pT", "A2", "mp"], writes=["hT"])
        if wi == 0:
            P.op("dve", lambda e: e.tensor_scalar(out=hT[:, :, 0:1], in0=hT[:, :, 0:1], scalar1=edg[:, 0:1], scalar2=None, op0=ALU.mult),
                 reads=["hT", "edg"], writes=["hT"])
        if wi == len(wins) - 1:
            P.op("dve", lambda e, c=n_in - 1: e.tensor_scalar(out=hT[:, :, c:c + 1], in0=hT[:, :, c:c + 1], scalar1=edg[:, 1:2], scalar2=None, op0=ALU.mult),
                 reads=["hT", "edg"], writes=["hT"])
        for fb in range(FC // 4):
            gb, gk = load_block(wg[:, fb * 512:(fb + 1) * 512], KC)
            vb, vk = load_block(wv[:, fb * 512:(fb + 1) * 512], KC)
            for q in range(4):
                fc = fb * 4 + q
                sl = fc % 2
                for kc in range(KC):
                    P.op("pe", lambda e, kc=kc, q=q, sl=sl, gb=gb: e.matmul(pG[sl][:, 0:n_in], lhsT=gb[:, kc, q * 128:(q + 1) * 128], rhs=hT[:, kc, 0:n_in], start=(kc == 0), stop=(kc == KC - 1)),
                         reads=[gk, "hT"], writes=["pG%d" % sl])
                for kc in range(KC):
                    P.op("pe", lambda e, kc=kc, q=q, sl=sl, vb=vb: e.matmul(pV[sl][:, 0:n_in], lhsT=vb[:, kc, q * 128:(q + 1) * 128], rhs=hT[:, kc, 0:n_in], start=(kc == 0), stop=(kc == KC - 1)),
                         reads=[vk, "hT"], writes=["pV%d" % sl])
                a, g_ = acc[sl], gl[sl]
                P.op("dve", lambda e, fc=fc, sl=sl, a=a: e.tensor_scalar(out=a[:, 0:n_out], in0=pG[sl][:, 0:n_out], scalar1=cwt[:, 0, fc:fc + 1], scalar2=None, op0=ALU.mult),
                     reads=["pG%d" % sl, "cwt"], writes=["acc%d" % sl])
                P.op("dve", lambda e, fc=fc, sl=sl, a=a: e.scalar_tensor_tensor(out=a[:, 0:n_out], in0=pG[sl][:, 1:n_out + 1], scalar=cwt[:, 1, fc:fc + 1], in1=a[:, 0:n_out], op0=ALU.mult, op1=ALU.add),
                     reads=["pG%d" % sl, "cwt", "acc%d" % sl], writes=["acc%d" % sl])
                P.op("dve", lambda e, fc=fc, sl=sl, a=a: e.scalar_tensor_tensor(out=a[:, 0:n_out], in0=pG[sl][:, 2:n_out + 2], scalar=cwt[:, 2, fc:fc + 1], in1=a[:, 0:n_out], op0=ALU.mult, op1=ALU.add),
                     reads=["pG%d" % sl, "cwt", "acc%d" % sl], writes=["acc%d" % sl])
                P.op("act", lambda e, fc=fc, sl=sl, a=a, g_=g_: e.activation(out=g_[:, 0:n_out], in_=a[:, 0:n_out], func=AF.Gelu_apprx_tanh, bias=cbt[:, fc:fc + 1], scale=1.0),
                     reads=["acc%d" % sl, "cbt"], writes=["gl%d" % sl])
                P.op("dve", lambda e, fc=fc, sl=sl, g_=g_: e.tensor_tensor(out=uT[:, fc, 0:n_out], in0=g_[:, 0:n_out], in1=pV[sl][:, 1:n_out + 1], op=ALU.mult),
                     reads=["gl%d" % sl, "pV%d" % sl], writes=["uT"])
        otiles = [(j, min(128, n_out - j * 128)) for j in range((n_out + 127) // 128)]
        kparts = []
        k0 = 0
        while k0 < FC:
            kparts.append((k0, min(16, FC - k0)))
            k0 += 16
        for cbk in range(NCB):
            for pi, (k0, nk) in enumerate(kparts):
                db, dk = load_block(wd[k0 * 128:(k0 + nk) * 128, cbk * 512:(cbk + 1) * 512], nk)
                for j, rows in otiles:
                    for kk in range(nk):
                        fc = k0 + kk
                        P.op("pe", lambda e, j=j, rows=rows, kk=kk, fc=fc, db=db: e.matmul(pD[j][:rows, :], lhsT=uT[:, fc, j * 128:j * 128 + rows], rhs=db[:, kk, :], start=(fc == 0), stop=(fc == FC - 1)),
                             reads=[dk, "uT"], writes=["pD%d" % j])
            for j, rows in otiles:
                t = tmp[j % 2]
                cs = slice(cbk * 512, (cbk + 1) * 512)
                P.op("dve", lambda e, j=j, rows=rows, t=t, cs=cs: e.tensor_tensor(out=t[:rows, :], in0=pD[j][:rows, :], in1=g2t[:rows, cs], op=ALU.mult),
                     reads=["pD%d" % j, "g2t"], writes=["tmp%d" % (j % 2)])
                P.op("dve", lambda e, j=j, rows=rows, t=t, cs=cs: e.tensor_tensor(out=xr[:rows, j, cs], in0=t[:rows, :], in1=xr[:rows, j, cs], op=ALU.add),
                     reads=["tmp%d" % (j % 2), "xr%d" % j], writes=["xr%d" % j])
        for j, rows in otiles:
            P.op("pool", lambda e, j=j, rows=rows: e.dma_start(out=out[s + j * 128:s + j * 128 + rows, :], in_=xr[:rows, j, :]),
                 reads=["xr%d" % j], dma=True)
    for wi, (s_, n_) in enumerate(wins):
        do_window(wi, s_, n_)
    return C


def build_outproj(TOK, D, DM):
    C = Ctx()
    P = C.P
    MC, NCB = DM // 128, D // 512
    x = C.din("x", [TOK, D], F32)
    mixT = C.din("mixT", [DM, TOK], BF16)
    wo = C.din("wo", [DM, D], BF16)
    g1row = C.din("g1row", [1, D], F32)
    out = C.dout("out", [TOK, D], F32)
    NT = 4
    xw = [C.sb("xw%d" % i, [128, NT, D], F32) for i in range(2)]
    mT = [C.sb("mT%d" % i, [128, MC, NT * 128], BF16) for i in range(2)]
    ring = [C.sb("ring%d" % i, [128, MC, 512], BF16) for i in range(2)]
    g1t = C.sb("g1t", [128, D], F32)
    tmp = [C.sb("tmp%d" % i, [128, 512], F32) for i in range(2)]
    pD = [C.ps("pD%d" % i, [128, 512], F32) for i in range(4)]
    P.op("sp", lambda e: e.dma_start(out=g1t[:], in_=g1row.partition_broadcast(128)), writes=["g1t"], dma=True)
    cnt = [0, 0]

    def do_window(wi, s, n):
        b = wi % 2
        X, M = xw[b], mT[b]
        tiles = [(j, min(128, n - j * 128)) for j in range((n + 127) // 128)]
        for j, rows in tiles:
            P.op("sp", lambda e, j=j, rows=rows: e.dma_start(out=X[:rows, j, :], in_=x[s + j * 128:s + j * 128 + rows, :]),
                 writes=["xw%d_%d" % (b, j)], dma=True)
        P.op("sp", lambda e: e.dma_start(out=M[:, :, 0:n], in_=mixT[:, s:s + n].rearrange("(c p) n -> p c n", p=128)),
             writes=["mT%d" % b], dma=True)
        for cbk in range(NCB):
            ri = cnt[0] % 2
            cnt[0] += 1
            R_ = ring[ri]
            P.op("sp", lambda e, cbk=cbk, R_=R_: e.dma_start(out=R_[:], in_=wo[:, cbk * 512:(cbk + 1) * 512].rearrange("(c p) n -> p c n", p=128)),
                 writes=["ring%d" % ri], dma=True)
            for j, rows in tiles:
                pi = cnt[1] % 4
                cnt[1] += 1
                pd = pD[pi]
                for kc in range(MC):
                    P.op("pe", lambda e, j=j, rows=rows, kc=kc, pd=pd, R_=R_: e.matmul(pd[:rows, :], lhsT=M[:, kc, j * 128:j * 128 + rows], rhs=R_[:, kc, :], start=(kc == 0), stop=(kc == MC - 1)),
                         reads=["ring%d" % ri, "mT%d" % b], writes=["pD%d" % pi])
                t = tmp[pi % 2]
                cs = slice(cbk * 512, (cbk + 1) * 512)
                P.op("dve", lambda e, rows=rows, pd=pd, t=t, cs=cs: e.tensor_tensor(out=t[:rows, :], in0=pd[:rows, :], in1=g1t[:rows, cs], op=ALU.mult),
                     reads=["pD%d" % pi, "g1t"], writes=["tmp%d" % (pi % 2)])
                P.op("dve", lambda e, j=j, rows=rows, t=t, cs=cs: e.tensor_tensor(out=X[:rows, j, cs], in0=t[:rows, :], in1=X[:rows, j, cs], op=ALU.add),
                     reads=["tmp%d" % (pi % 2), "xw%d_%d" % (b, j)], writes=["xw%d_%d" % (b, j)])
        for j, rows in tiles:
            P.op("pool", lambda e, j=j, rows=rows: e.dma_start(out=out[s + j * 128:s + j * 128 + rows, :], in_=X[:rows, j, :]),
                 reads=["xw%d_%d" % (b, j)], dma=True)

    s = 0
    wi = 0
    while s < TOK:
        n = min(512, TOK - s)
        do_window(wi, s, n)
        s += n
        wi += 1
    return C


def build_ada(D, NU):
    C = Ctx()
    P = C.P
    KC = D // 128
    cT = C.din("cT", [128, KC, 3], F32)
    aw = C.din("aw", [NU, D, 512], F32)
    ab = C.din("ab", [NU, 1, 512], F32)
    out = C.dout("out", [NU, 3, 512], F32)
    ct = C.sb("ct", [128, KC, 3], F32)
    st = C.sb("st", [128, KC, 3], F32)
    ones = C.sb("ones", [1, 3], F32)
    wt = [C.sb("wt%d" % i, [128, KC, 512], F32) for i in range(2)]
    bt = [C.sb("bt%d" % i, [1, 512], F32) for i in range(2)]
    ot = [C.sb("ot%d" % i, [3, 512], F32) for i in range(2)]
    pm = [C.ps("pm%d" % i, [128, 512], F32) for i in range(2)]
    P.op("sp", lambda e: e.dma_start(out=ct[:], in_=cT), writes=["ct"], dma=True)
    P.op("act", lambda e: e.activation(out=st[:], in_=ct[:], func=AF.Silu), reads=["ct"], writes=["st"])
    P.op("dve", lambda e: e.memset(ones[:], 1.0), writes=["ones"])
    for u in range(NU):
        b = u % 2
        P.op("sp", lambda e, u=u, b=b: e.dma_start(out=wt[b][:], in_=aw[u].rearrange("(c p) n -> p c n", p=128)), writes=["wt%d" % b], dma=True)
        P.op("sp", lambda e, u=u, b=b: e.dma_start(out=bt[b][:], in_=ab[u]), writes=["bt%d" % b], dma=True)
        for kc in range(KC):
            P.op("pe", lambda e, kc=kc, b=b: e.matmul(pm[b][0:3, :], lhsT=st[:, kc, :], rhs=wt[b][:, kc, :], start=(kc == 0), stop=False),
                 reads=["st", "wt%d" % b], writes=["pm%d" % b])
        P.op("pe", lambda e, b=b: e.matmul(pm[b][0:3, :], lhsT=ones[:], rhs=bt[b][:], start=False, stop=True),
             reads=["ones", "bt%d" % b], writes=["pm%d" % b])
        P.op("dve", lambda e, b=b: e.tensor_copy(out=ot[b][:], in_=pm[b][0:3, :]), reads=["pm%d" % b], writes=["ot%d" % b])
        P.op("pool", lambda e, u=u, b=b: e.dma_start(out=out[u], in_=ot[b][:]), reads=["ot%d" % b], dma=True)
    return C


def build_cast(shapes):
    C = Ctx()
    P = C.P
    for i, (r, c) in enumerate(shapes):
        src = C.din("w%d" % i, [r, c], F32)
        dst = C.dout("o%d" % i, [r, c], BF16)
        r0 = 0
        while r0 < r:
            nr = min(64, r - r0)
            P.op("pool", lambda e, src=src, dst=dst, r0=r0, nr=nr: e.dma_start(out=dst[r0:r0 + nr, :], in_=src[r0:r0 + nr, :]), dma=True)
            r0 += nr
    return C


def build_qkv(NLAT, NCTX, D, HA, HB, GB):
    C = Ctx()
    P = C.P
    KC = D // 128
    NROW = NLAT + NCTX
    QC = (HA + HB) * 128
    KA0, VA0 = QC, QC + HA * 128
    KB0 = QC + 2 * HA * 128
    VB0 = KB0 + GB * 128
    NIN = VB0 + GB * 128
    NK = HA * 128 + GB * 128
    x = C.din("x", [NROW, D], F32)
    modp = C.din("modp", [128, 2, 3, KC], F32)
    w_in = C.din("w_in", [D, NIN], BF16)
    gvec = C.din("gvec", [128, 4], F32)
    rope = C.din("rope", [4, 128, NROW], F32)
    perm = C.din("perm", [2, 128, 128], F32)
    bones = C.din("bones", [2, 128, 128], BF16)
    ident = C.din("ident", [128, 128], BF16)
    qT = C.dout("qT", [QC, NLAT], BF16)
    kT = C.dout("kT", [NK, NROW], BF16)
    v = C.dout("v", [NROW, NK], BF16)

    NT = 4
    xw = C.sb("xw", [128, NT, D], F32)
    xn = [C.sb("xn%d" % i, [128, D], BF16) for i in range(2)]
    hT = C.sb("hT", [128, KC, 512], BF16)
    NR = 3
    ring = [C.sb("ring%d" % i, [128, KC, 512], BF16) for i in range(NR)]
    rp = [C.sb("rp%d" % i, [128, 4, 512], F32) for i in range(2)]
    idb = C.sb("idb", [128, 128], BF16)
    mp = C.sb("mp", [128, 2, 3, KC], F32)
    A1 = C.sb("A1", [128, 2, KC], F32)
    gv = C.sb("gv", [128, 4], F32)
    pmt = C.sb("pmt", [128, 2, 128], F32)
    bon = C.sb("bon", [128, 2, 128], BF16)
    epsb = C.sb("epsb", [128, 1], F32)
    ss = C.sb("ss", [128, NT], F32)
    rstd = C.sb("rstd", [128, NT], F32)
    junk = C.sb("junk", [128, D], BF16)
    sq = [C.sb("sq%d" % i, [128, 512], BF16) for i in range(3)]
    rs = [C.sb("rs%d" % i, [128, 512], F32) for i in range(3)]
    qn = [C.sb("qn%d" % i, [128, 512], F32) for i in range(3)]
    t1 = [C.sb("t1%d" % i, [128, 512], F32) for i in range(2)]
    t2 = [C.sb("t2%d" % i, [128, 512], F32) for i in range(2)]
    oT = [C.sb("oT%d" % i, [128, 512], BF16) for i in range(2)]
    vs = [C.sb("vs%d" % i, [128, 512], BF16) for i in range(2)]
    pT = C.ps("pT", [128, 2, 512], BF16)
    pQ = [C.ps("pQ%d" % i, [128, 512], F32) for i in range(3)]
    pS = C.ps("pS", [128, 512], F32)
    pR = [C.ps("pR%d" % i, [128, 512], F32) for i in range(2)]
    pV = C.ps("pV", [128, 512], F32)

    P.op("sp", lambda e: e.dma_start(out=idb[:], in_=ident), writes=["idb"], dma=True)
    P.op("sp", lambda e: e.dma_start(out=mp[:], in_=modp), writes=["mp"], dma=True)
    P.op("sp", lambda e: e.dma_start(out=gv[:], in_=gvec), writes=["gv"], dma=True)
    P.op("sp", lambda e: e.dma_start(out=pmt[:], in_=perm.rearrange("t p n -> p t n")), writes=["pmt"], dma=True)
    P.op("sp", lambda e: e.dma_start(out=bon[:], in_=bones.rearrange("t p n -> p t n")), writes=["bon"], dma=True)
    P.op("dve", lambda e: e.memset(epsb[:], 1e-6), writes=["epsb"])
    for t in range(2):
        P.op("dve", lambda e, t=t: e.scalar_tensor_tensor(out=A1[:, t, :], in0=mp[:, t, 1, :], scalar=1.0, in1=mp[:, t, 0, :], op0=ALU.add, op1=ALU.mult),
             reads=["mp"], writes=["A1"])
    P.op("dve", lambda e: e.tensor_scalar(out=gv[:, 0:1], in0=gv[:, 0:1], scalar1=64.0 ** -0.5, scalar2=None, op0=ALU.mult), reads=["gv"], writes=["gv"])
    P.op("dve", lambda e: e.tensor_scalar(out=gv[:, 2:3], in0=gv[:, 2:3], scalar1=128.0 ** -0.5, scalar2=None, op0=ALU.mult), reads=["gv"], writes=["gv"])

    rcnt = [0]
    ccnt = [0]
    vcnt = [0]

    def load_block(c0, ncols):
        i = rcnt[0] % NR
        rcnt[0] += 1
        buf = ring[i]
        P.op("sp", lambda e: e.dma_start(out=buf[:, :, 0:ncols], in_=w_in[:, c0:c0 + ncols].rearrange("(c p) n -> p c n", p=128)),
             writes=["ring%d" % i], dma=True)
        return buf, "ring%d" % i

    def do_window(wi, s, n, seg):
        tiles = [(j, min(128, n - j * 128)) for j in range((n + 127) // 128)]
        rb = wi % 2
        RP = rp[rb]
        for j, rows in tiles:
            P.op("sp", lambda e, j=j, rows=rows: e.dma_start(out=xw[:rows, j, :], in_=x[s + j * 128: s + j * 128 + rows, :]),
                 writes=["xw%d" % j], dma=True)
        P.op("sp", lambda e: e.dma_start(out=RP[:, :, 0:n], in_=rope[:, :, s:s + n].rearrange("t p n -> p t n")), writes=["rp%d" % rb], dma=True)
        for j, rows in tiles:
            P.op("act", lambda e, j=j, rows=rows: e.activation(out=junk[:rows, :], in_=xw[:rows, j, :], func=AF.Square, accum_out=ss[:rows, j:j + 1]),
                 reads=["xw%d" % j], writes=["junk", "ss%d" % j])
        for j, rows in tiles:
            P.op("act", lambda e, j=j, rows=rows: e.activation(out=rstd[:rows, j:j + 1], in_=ss[:rows, j:j + 1], func=AF.Sqrt, scale=1.0 / D, bias=epsb[:rows, :]),
                 reads=["ss%d" % j, "epsb"], writes=["rstd%d" % j])
            P.op("dve", lambda e, j=j, rows=rows: e.reciprocal(out=rstd[:rows, j:j + 1], in_=rstd[:rows, j:j + 1]),
                 reads=["rstd%d" % j], writes=["rstd%d" % j])
        for j, rows in tiles:
            xb = xn[j % 2]
            P.op("dve", lambda e, j=j, rows=rows, xb=xb: e.tensor_scalar(out=xb[:rows, :], in0=xw[:rows, j, :], scalar1=rstd[:rows, j:j + 1], scalar2=None, op0=ALU.mult),
                 reads=["xw%d" % j, "rstd%d" % j], writes=["xn%d" % (j % 2)])
            for g4 in range(KC // 4):
                slot = g4 % 2
                for q in range(4):
                    kc = g4 * 4 + q
                    P.op("pe", lambda e, kc=kc, q=q, rows=rows, xb=xb, slot=slot: e.transpose(out=pT[:, slot, q * 128:q * 128 + rows], in_=xb[:rows, kc * 128:(kc + 1) * 128], identity=idb[:rows, :rows]),
                         reads=["xn%d" % (j % 2), "idb"], writes=["pT"])
                for q in range(4):
                    kc = g4 * 4 + q
                    P.op("act", lambda e, kc=kc, q=q, j=j, rows=rows, slot=slot: e.activation(out=hT[:, kc, j * 128:j * 128 + rows], in_=pT[:, slot, q * 128:q * 128 + rows], func=AF.Identity, scale=A1[:, seg, kc:kc + 1], bias=mp[:, seg, 2, kc:kc + 1]),
                         reads=["pT", "A1", "mp"], writes=["hT"])

        chunks = []
        if seg == 0:
            for blk in range(QC // 512):
                for lc in range(4):
                    ch = blk * 4 + lc
                    typ = 0 if ch < HA else 1
                    chunks.append((blk * 512, 512, lc, typ, 0 if typ == 0 else 2, qT, ch * 128))
        for blk in range(HA * 128 // 512):
            for lc in range(4):
                chunks.append((KA0 + blk * 512, 512, lc, 0, 1, kT, (blk * 4 + lc) * 128))
        for lc in range(GB):
            chunks.append((KB0, GB * 128, lc, 1, 3, kT, HA * 128 + lc * 128))
        cur = {"c0": None, "buf": None, "bk": None}
        base = ccnt[0]
        ccnt[0] += len(chunks)

        def stageA(k):
            c0, ncb, lc, typ, gi, dst, drow = chunks[k]
            if cur["c0"] != c0:
                cur["buf"], cur["bk"] = load_block(c0, ncb)
                cur["c0"] = c0
            buf, bk = cur["buf"], cur["bk"]
            i = (base + k) % 3
            for kc in range(KC):
                P.op("pe", lambda e, kc=kc: e.matmul(pQ[i][:, 0:n], lhsT=buf[:, kc, lc * 128:(lc + 1) * 128], rhs=hT[:, kc, 0:n], start=(kc == 0), stop=(kc == KC - 1)),
                     reads=[bk, "hT"], writes=["pQ%d" % i])
            P.op("act", lambda e: e.activation(out=sq[i][:, 0:n], in_=pQ[i][:, 0:n], func=AF.Square), reads=["pQ%d" % i], writes=["sq%d" % i])

        def stageB(k):
            c0, ncb, lc, typ, gi, dst, drow = chunks[k]
            i = (base + k) % 3
            dh = 64.0 if typ == 0 else 128.0
            P.op("pe", lambda e: e.matmul(pS[:, 0:n], lhsT=bon[:, typ, :], rhs=sq[i][:, 0:n], start=True, stop=True),
                 reads=["bon", "sq%d" % i], writes=["pS"])
            P.op("act", lambda e: e.activation(out=rs[i][:, 0:n], in_=pS[:, 0:n], func=AF.Sqrt, scale=1.0 / dh, bias=epsb[:, :]),
                 reads=["pS", "epsb"], writes=["rs%d" % i])
            P.op("dve", lambda e: e.reciprocal(out=rs[i][:, 0:n], in_=rs[i][:, 0:n]), reads=["rs%d" % i], writes=["rs%d" % i])
            P.op("dve", lambda e: e.scalar_tensor_tensor(out=qn[i][:, 0:n], in0=pQ[i][:, 0:n], scalar=gv[:, gi:gi + 1], in1=rs[i][:, 0:n], op0=ALU.mult, op1=ALU.mult),
                 reads=["pQ%d" % i, "gv", "rs%d" % i], writes=["qn%d" % i])

        def stageC(k):
            c0, ncb, lc, typ, gi, dst, drow = chunks[k]
            i = (base + k) % 3
            j = (base + k) % 2
            P.op("pe", lambda e: e.matmul(pR[j][:, 0:n], lhsT=pmt[:, typ, :], rhs=qn[i][:, 0:n], start=True, stop=True),
                 reads=["pmt", "qn%d" % i], writes=["pR%d" % j])
            P.op("dve", lambda e: e.tensor_tensor(out=t1[j][:, 0:n], in0=qn[i][:, 0:n], in1=RP[:, 2 * typ, 0:n], op=ALU.mult),
                 reads=["qn%d" % i, "rp%d" % rb], writes=["t1%d" % j])
            P.op("dve", lambda e: e.tensor_tensor(out=t2[j][:, 0:n], in0=pR[j][:, 0:n], in1=RP[:, 2 * typ + 1, 0:n], op=ALU.mult),
                 reads=["pR%d" % j, "rp%d" % rb], writes=["t2%d" % j])
            P.op("dve", lambda e: e.tensor_tensor(out=oT[j][:, 0:n], in0=t1[j][:, 0:n], in1=t2[j][:, 0:n], op=ALU.add),
                 reads=["t1%d" % j, "t2%d" % j], writes=["oT%d" % j])
            P.op("pool", lambda e: e.dma_start(out=dst[drow:drow + 128, s:s + n], in_=oT[j][:, 0:n]), reads=["oT%d" % j], dma=True)

        for step in range(len(chunks) + 2):
            if step < len(chunks):
                stageA(step)
            if 0 <= step - 1 < len(chunks):
                stageB(step - 1)
            if 0 <= step - 2 < len(chunks):
                stageC(step - 2)
        vblocks = [(VA0 + b * 512, 512, b * 512) for b in range(HA * 128 // 512)] + [(VB0, GB * 128, HA * 128)]
        for c0, ncols, d0 in vblocks:
            buf, bk = load_block(c0, ncols)
            for j, rows in tiles:
                for kc in range(KC):
                    P.op("pe", lambda e, kc=kc, j=j, rows=rows, ncols=ncols, buf=buf: e.matmul(pV[:rows, 0:ncols], lhsT=hT[:, kc, j * 128:j * 128 + rows], rhs=buf[:, kc, 0:ncols], start=(kc == 0), stop=(kc == KC - 1)),
                         reads=[bk, "hT"], writes=["pV"])
                vi = vcnt[0] % 2
                vcnt[0] += 1
                P.op("act", lambda e, rows=rows, vi=vi, ncols=ncols: e.activation(out=vs[vi][:rows, 0:ncols], in_=pV[:rows, 0:ncols], func=AF.Identity),
                     reads=["pV"], writes=["vs%d" % vi])
                P.op("pool", lambda e, j=j, rows=rows, vi=vi, ncols=ncols, d0=d0: e.dma_start(out=v[s + j * 128:s + j * 128 + rows, d0:d0 + ncols], in_=vs[vi][:rows, 0:ncols]),
                     reads=["vs%d" % vi], dma=True)

    wi = 0
    for (s0, n0, seg) in ((0, NLAT, 0), (NLAT, NCTX, 1)):
        s = s0
        while s < s0 + n0:
            n = min(512, s0 + n0 - s)
            do_window(wi, s, n, seg)
            s += n
            wi += 1
    return C


def build_attn(TQ, NKEY, HA, HB, GB, lam_init):
    C = Ctx()
    P = C.P
    NKT = NKEY // 128
    NQB = TQ // 128
    NLT = NQB + 4
    NKB = NLT * 128
    RG = HB // GB
    QC = (HA + HB) * 128
    qT = C.din("qT", [QC, TQ], BF16)
    kaT = C.din("kaT", [HA * 128, NKEY], BF16)
    va = C.din("va", [NKEY, HA * 128], BF16)
    kbT = C.din("kbT", [GB * 128, NKB], BF16)
    vb = C.din("vb", [NKB, GB * 128], BF16)
    kbias = C.din("kbias", [128, NLT], F32)
    masks = C.din("masks", [2, 128, 128], BF16)
    lamv = C.din("lamv", [1, 256], F32)
    sinkv = C.din("sinkv", [1, HB], F32)
    subg = C.din("subg", [128, 1], F32)
    mixT = C.dout("mixT", [QC, TQ], BF16)

    NPT = 3
    KT = C.sb("KT", [128, NKEY], BF16)
    VT = C.sb("VT", [128, NKT, 128], BF16)
    QH = [C.sb("QH%d" % i, [128, TQ], BF16) for i in range(2)]
    KB = C.sb("KB", [128, GB, NKB], BF16)
    VB = C.sb("VB", [128, NLT, GB * 128], BF16)
    Pt = [C.sb("Pt%d" % i, [128, 2, 512], BF16) for i in range(NPT)]
    onesb = C.sb("onesb", [128, 128], BF16)
    onesf = C.sb("onesf", [128, 128], F32)
    Lacc = [[C.sb("Lacc%d_%d" % (i, j), [128, 2, 512], F32) for j in range(2)] for i in range(2)]
    msk = C.sb("msk", [128, 2, 128], BF16)
    kbs = C.sb("kbs", [128, NLT], F32)
    lv = C.sb("lv", [128, 256], F32)
    lt = C.sb("lt", [128, 64], F32)
    ls = C.sb("ls", [128, 4], F32)
    sk = C.sb("sk", [128, HB], F32)
    sg = C.sb("sg", [128, 1], F32)
    epsb = C.sb("epsb", [128, 1], F32)
    f = {nm: [C.sb("%s%d" % (nm, i), [128, 512], F32) for i in range(2)] for nm in ("r1", "r2", "u1", "u2", "oo", "rn")}
    sqo = [C.sb("sqo%d" % i, [128, 512], BF16) for i in range(2)]
    ob = [C.sb("ob%d" % i, [128, 512], BF16) for i in range(2)]
    pS = [C.ps("pS%d" % i, [128, 2, 512], F32) for i in range(2)]
    pO = [C.ps("pO%d" % i, [128, 512], F32) for i in range(2)]
    pL = [C.ps("pL%d" % i, [128, 512], F32) for i in range(2)]

    P.op("dve", lambda e: e.memset(onesb[:], 1.0), writes=["onesb"])
    P.op("dve", lambda e: e.memset(onesf[:], 1.0), writes=["onesf"])
    P.op("dve", lambda e: e.memset(epsb[:], 1e-6), writes=["epsb"])
    P.op("sp", lambda e: e.dma_start(out=msk[:], in_=masks.rearrange("t p n -> p t n")), writes=["msk"], dma=True)
    P.op("sp", lambda e: e.dma_start(out=kbs[:], in_=kbias), writes=["kbs"], dma=True)
    P.op("sp", lambda e: e.dma_start(out=lv[:], in_=lamv.partition_broadcast(128)), writes=["lv"], dma=True)
    P.op("sp", lambda e: e.dma_start(out=sk[:], in_=sinkv.partition_broadcast(128)), writes=["sk"], dma=True)
    P.op("sp", lambda e: e.dma_start(out=sg[:], in_=subg), writes=["sg"], dma=True)
    for t in range(2):
        P.op("dve", lambda e, t=t: e.tensor_tensor(out=lt[:], in0=lv[:, t * 128:t * 128 + 64], in1=lv[:, t * 128 + 64:t * 128 + 128], op=ALU.mult),
             reads=["lv"], writes=["lt"])
        P.op("dve", lambda e, t=t: e.reduce_sum(out=ls[:, t:t + 1], in_=lt[:], axis=AX.X), reads=["lt"], writes=["ls%d" % t])
    P.op("act", lambda e: e.activation(out=ls[:, 0:2], in_=ls[:, 0:2], func=AF.Exp), reads=["ls0", "ls1"], writes=["ls0", "ls1"])
    P.op("dve", lambda e: e.tensor_tensor(out=ls[:, 2:3], in0=ls[:, 1:2], in1=ls[:, 0:1], op=ALU.subtract), reads=["ls0", "ls1"], writes=["ls2"])
    P.op("dve", lambda e: e.tensor_scalar(out=ls[:, 2:3], in0=ls[:, 2:3], scalar1=-float(lam_init), scalar2=None, op0=ALU.add), reads=["ls2"], writes=["ls2"])
    P.op("act", lambda e: e.activation(out=sk[:], in_=sk[:], func=AF.Exp), reads=["sk"], writes=["sk"])
    P.op("dve", lambda e: e.tensor_scalar(out=sg[:], in0=sg[:], scalar1=1.0 - float(lam_init), scalar2=None, op0=ALU.mult), reads=["sg"], writes=["sg"])

    cnt = {"s": 0, "p": 0, "f": 0}

    def diff_head(h):
        qb = h % 2
        Q = QH[qb]
        P.op("sp", lambda e: e.dma_start(out=KT[:], in_=kaT[h * 128:(h + 1) * 128, :]), writes=["KT"], dma=True)
        P.op("sp", lambda e: e.dma_start(out=VT[:], in_=va[:, h * 128:(h + 1) * 128].rearrange("(t p) d -> p t d", p=128)), writes=["VT"], dma=True)
        P.op("sp", lambda e: e.dma_start(out=Q[:], in_=qT[h * 128:(h + 1) * 128, :]), writes=["QH%d" % qb], dma=True)

        def S(qc, kt):
            si = cnt["s"] % 2
            cnt["s"] += 1
            for m in range(2):
                P.op("pe", lambda e, m=m: e.matmul(pS[si][:, m, :], lhsT=KT[m * 64:(m + 1) * 64, kt * 128:(kt + 1) * 128], rhs=Q[m * 64:(m + 1) * 64, qc * 512:(qc + 1) * 512], start=True, stop=True),
                     reads=["KT", "QH%d" % qb], writes=["pS%d" % si])
            return si

        def rest(qc, kt, si):
            pi = cnt["p"] % NPT
            cnt["p"] += 1
            li = qc % 2
            P.op("act", lambda e: e.activation(out=Pt[pi][:], in_=pS[si][:], func=AF.Exp), reads=["pS%d" % si], writes=["Pt%d" % pi])
            for m in range(2):
                P.op("pe", lambda e, m=m: e.matmul(pO[m][:], lhsT=VT[:, kt, :], rhs=Pt[pi][:, m, :], start=(kt == 0), stop=(kt == NKT - 1)),
                     reads=["VT", "Pt%d" % pi], writes=["pO%d" % m])
            ab = kt % 2
            acc, ak = Lacc[li][ab], "Lacc%d_%d" % (li, ab)
            if kt < 2:
                P.op("dve", lambda e: e.tensor_copy(out=acc[:], in_=Pt[pi][:]), reads=["Pt%d" % pi], writes=[ak])
            else:
                P.op("dve", lambda e: e.tensor_tensor(out=acc[:], in0=acc[:], in1=Pt[pi][:], op=ALU.add), reads=["Pt%d" % pi, ak], writes=[ak])

        def fin(qc):
            i = cnt["f"] % 2
            cnt["f"] += 1
            r1, r2, u1, u2, oo, rn = [f[k][i] for k in ("r1", "r2", "u1", "u2", "oo", "rn")]
            li = qc % 2
            for m in range(2):
                for ab in range(2):
                    P.op("pe", lambda e, m=m, ab=ab: e.matmul(pL[m][:], lhsT=onesf[:], rhs=Lacc[li][ab][:, m, :], start=(ab == 0), stop=(ab == 1)),
                         reads=["onesf", "Lacc%d_%d" % (li, ab)], writes=["pL%d" % m])
            P.op("dve", lambda e: e.reciprocal(out=r1[:], in_=pL[0][:]), reads=["pL0"], writes=["r1%d" % i])
            P.op("dve", lambda e: e.reciprocal(out=r2[:], in_=pL[1][:]), reads=["pL1"], writes=["r2%d" % i])
            P.op("dve", lambda e: e.tensor_tensor(out=u1[:], in0=pO[0][:], in1=r1[:], op=ALU.mult), reads=["pO0", "r1%d" % i], writes=["u1%d" % i])
            P.op("dve", lambda e: e.tensor_tensor(out=u2[:], in0=pO[1][:], in1=r2[:], op=ALU.mult), reads=["pO1", "r2%d" % i], writes=["u2%d" % i])
            P.op("dve", lambda e: e.scalar_tensor_tensor(out=oo[:], in0=u2[:], scalar=ls[:, 2:3], in1=u1[:], op0=ALU.mult, op1=ALU.add),
                 reads=["u1%d" % i, "u2%d" % i, "ls2"], writes=["oo%d" % i])
            P.op("act", lambda e: e.activation(out=sqo[i][:], in_=oo[:], func=AF.Square), reads=["oo%d" % i], writes=["sqo%d" % i])
            P.op("pe", lambda e: e.matmul(pL[0][:], lhsT=onesb[:], rhs=sqo[i][:], start=True, stop=True), reads=["onesb", "sqo%d" % i], writes=["pL0"])
            P.op("act", lambda e: e.activation(out=rn[:], in_=pL[0][:], func=AF.Sqrt, scale=1.0 / 128, bias=epsb[:]), reads=["pL0", "epsb"], writes=["rn%d" % i])
            P.op("dve", lambda e: e.reciprocal(out=rn[:], in_=rn[:]), reads=["rn%d" % i], writes=["rn%d" % i])
            P.op("dve", lambda e: e.scalar_tensor_tensor(out=ob[i][:], in0=oo[:], scalar=sg[:, 0:1], in1=rn[:], op0=ALU.mult, op1=ALU.mult),
                 reads=["oo%d" % i, "sg", "rn%d" % i], writes=["ob%d" % i])
            P.op("pool", lambda e: e.dma_start(out=mixT[h * 128:(h + 1) * 128, qc * 512:(qc + 1) * 512], in_=ob[i][:]), reads=["ob%d" % i], dma=True)

        for qc in range(TQ // 512):
            si = S(qc, 0)
            for kt in range(NKT):
                nsi = S(qc, kt + 1) if kt + 1 < NKT else None
                rest(qc, kt, si)
                si = nsi
            fin(qc)

    for h in range(HA):
        diff_head(h)

    P.op("sp", lambda e: e.dma_start(out=KB[:], in_=kbT.rearrange("(g p) n -> p g n", p=128)), writes=["KB"], dma=True)
    P.op("sp", lambda e: e.dma_start(out=VB[:], in_=vb.rearrange("(t p) d -> p t d", p=128)), writes=["VB"], dma=True)

    def win_head(g, r):
        hh = g * RG + r
        qrow = (HA + hh) * 128
        qb = hh % 2
        Q = QH[qb]
        P.op("sp", lambda e: e.dma_start(out=Q[:], in_=qT[qrow:qrow + 128, :]), writes=["QH%d" % qb], dma=True)

        def chunk(qc):
            i0 = qc * 4
            items = [(0, 0, 512, []), (1, 0, 512, [])]
            for kt in range(i0 + 2, i0 + 8):
                lo, hi = max(i0, kt - 4), min(i0 + 3, kt - 2)
                if lo > hi:
                    continue
                ml = []
                if i0 <= kt - 2 <= i0 + 3:
                    ml.append((0, (kt - 2 - i0) * 128))
                if i0 <= kt - 4 <= i0 + 3:
                    ml.append((1, (kt - 4 - i0) * 128))
                items.append((kt, (lo - i0) * 128, (hi - lo + 1) * 128, ml))
            last = len(items) - 1
            for ii, (kt, c0, ncw, ml) in enumerate(items):
                si = cnt["s"] % 2
                cnt["s"] += 1
                pi = cnt["p"] % NPT
                cnt["p"] += 1
                P.op("pe", lambda e, kt=kt, c0=c0, ncw=ncw, si=si: e.matmul(pS[si][:, 0, c0:c0 + ncw], lhsT=KB[:, g, kt * 128:(kt + 1) * 128], rhs=Q[:, qc * 512 + c0:qc * 512 + c0 + ncw], start=True, stop=True),
                     reads=["KB", "QH%d" % qb], writes=["pS%d" % si])
                P.op("act", lambda e, kt=kt, c0=c0, ncw=ncw, si=si, pi=pi: e.activation(out=Pt[pi][:, 0, c0:c0 + ncw], in_=pS[si][:, 0, c0:c0 + ncw], func=AF.Exp, bias=kbs[:, kt:kt + 1], scale=1.0),
                     reads=["pS%d" % si, "kbs"], writes=["Pt%d" % pi])
                for (mt, mc) in ml:
                    P.op("dve", lambda e, mt=mt, mc=mc, pi=pi: e.tensor_tensor(out=Pt[pi][:, 0, mc:mc + 128], in0=Pt[pi][:, 0, mc:mc + 128], in1=msk[:, mt, :], op=ALU.mult),
                         reads=["Pt%d" % pi, "msk"], writes=["Pt%d" % pi])
                P.op("pe", lambda e, ii=ii, kt=kt, c0=c0, ncw=ncw, pi=pi: e.matmul(pO[0][:, c0:c0 + ncw], lhsT=VB[:, kt, g * 128:(g + 1) * 128], rhs=Pt[pi][:, 0, c0:c0 + ncw], start=(ii == 0), stop=(ii == last)),
                     reads=["VB", "Pt%d" % pi], writes=["pO0"])
                P.op("pe", lambda e, ii=ii, c0=c0, ncw=ncw, pi=pi: e.matmul(pL[0][:, c0:c0 + ncw], lhsT=onesb[:], rhs=Pt[pi][:, 0, c0:c0 + ncw], start=(ii == 0), stop=(ii == last)),
                     reads=["onesb", "Pt%d" % pi], writes=["pL0"])
            i = cnt["f"] % 2
            cnt["f"] += 1
            r1 = f["r1"][i]
            P.op("dve", lambda e: e.tensor_scalar(out=r1[:], in0=pL[0][:], scalar1=sk[:, hh:hh + 1], scalar2=None, op0=ALU.add), reads=["pL0", "sk"], writes=["r1%d" % i])
            P.op("dve", lambda e: e.reciprocal(out=r1[:], in_=r1[:]), reads=["r1%d" % i], writes=["r1%d" % i])
            P.op("dve", lambda e: e.tensor_tensor(out=ob[i][:], in0=pO[0][:], in1=r1[:], op=ALU.mult), reads=["pO0", "r1%d" % i], writes=["ob%d" % i])
            P.op("pool", lambda e: e.dma_start(out=mixT[qrow:qrow + 128, qc * 512:(qc + 1) * 512], in_=ob[i][:]), reads=["ob%d" % i], dma=True)

        for qc in range(TQ // 512):
            chunk(qc)

    for g in range(GB):
        for r in range(RG):
            win_head(g, r)
    return C


def build_hyin(TOK, D, NCH):
    C = Ctx()
    P = C.P
    KC = D // 128
    CC = NCH // 128
    xh = C.din("xh", [TOK + 2, D], F32)
    edge = C.din("edge", [128, 2], F32)
    modp = C.din("modp", [128, 3, KC], F32)
    w_in = C.din("w_in", [D, NCH], BF16)
    cvec = C.din("cvec", [128, 5, CC], F32)
    ident = C.din("ident", [128, 128], BF16)
    zT = C.dout("zT", [NCH, TOK], F32)
    NT = 4
    xw = C.sb("xw", [128, NT, D], F32)
    xn = [C.sb("xn%d" % i, [128, D], BF16) for i in range(2)]
    hT = C.sb("hT", [128, KC, 512], BF16)
    NR = 3
    ring = [C.sb("ring%d" % i, [128, KC, 512], BF16) for i in range(NR)]
    idb = C.sb("idb", [128, 128], BF16)
    mp = C.sb("mp", [128, 3, KC], F32)
    A1 = C.sb("A1", [128, KC], F32)
    cv = C.sb("cv", [128, 5, CC], F32)
    edg = C.sb("edg", [128, 2], F32)
    epsb = C.sb("epsb", [128, 1], F32)
    ss = C.sb("ss", [128, NT], F32)
    rstd = C.sb("rstd", [128, NT], F32)
    junk = C.sb("junk", [128, D], BF16)
    zp = [C.sb("zp%d" % i, [128, 512], F32) for i in range(2)]
    acc = [C.sb("acc%d" % i, [128, 512], F32) for i in range(2)]
    pT = C.ps("pT", [128, 2, 512], BF16)
    pZ = [C.ps("pZ%d" % i, [128, 512], F32) for i in range(2)]
    P.op("sp", lambda e: e.dma_start(out=idb[:], in_=ident), writes=["idb"], dma=True)
    P.op("sp", lambda e: e.dma_start(out=mp[:], in_=modp), writes=["mp"], dma=True)
    P.op("sp", lambda e: e.dma_start(out=cv[:], in_=cvec), writes=["cv"], dma=True)
    P.op("sp", lambda e: e.dma_start(out=edg[:], in_=edge), writes=["edg"], dma=True)
    P.op("dve", lambda e: e.memset(epsb[:], 1e-6), writes=["epsb"])
    P.op("dve", lambda e: e.scalar_tensor_tensor(out=A1[:], in0=mp[:, 1, :], scalar=1.0, in1=mp[:, 0, :], op0=ALU.add, op1=ALU.mult), reads=["mp"], writes=["A1"])
    rcnt = [0]
    wins = windows(TOK, 510)

    def do_window(wi, s, n_out):
        n = n_out + 2
        tiles = [(j, min(128, n - j * 128)) for j in range((n + 127) // 128)]
        for j, rows in tiles:
            P.op("sp", lambda e, j=j, rows=rows: e.dma_start(out=xw[:rows, j, :], in_=xh[s + j * 128: s + j * 128 + rows, :]), writes=["xw%d" % j], dma=True)
        for j, rows in tiles:
            P.op("act", lambda e, j=j, rows=rows: e.activation(out=junk[:rows, :], in_=xw[:rows, j, :], func=AF.Square, accum_out=ss[:rows, j:j + 1]),
                 reads=["xw%d" % j], writes=["junk", "ss%d" % j])
        for j, rows in tiles:
            P.op("act", lambda e, j=j, rows=rows: e.activation(out=rstd[:rows, j:j + 1], in_=ss[:rows, j:j + 1], func=AF.Sqrt, scale=1.0 / D, bias=epsb[:rows, :]),
                 reads=["ss%d" % j, "epsb"], writes=["rstd%d" % j])
            P.op("dve", lambda e, j=j, rows=rows: e.reciprocal(out=rstd[:rows, j:j + 1], in_=rstd[:rows, j:j + 1]), reads=["rstd%d" % j], writes=["rstd%d" % j])
        for j, rows in tiles:
            xb = xn[j % 2]
            P.op("dve", lambda e, j=j, rows=rows, xb=xb: e.tensor_scalar(out=xb[:rows, :], in0=xw[:rows, j, :], scalar1=rstd[:rows, j:j + 1], scalar2=None, op0=ALU.mult),
                 reads=["xw%d" % j, "rstd%d" % j], writes=["xn%d" % (j % 2)])
            for g4 in range(KC // 4):
                slot = g4 % 2
                for q in range(4):
                    kc = g4 * 4 + q
                    P.op("pe", lambda e, kc=kc, q=q, rows=rows, xb=xb, slot=slot: e.transpose(out=pT[:, slot, q * 128:q * 128 + rows], in_=xb[:rows, kc * 128:(kc + 1) * 128], identity=idb[:rows, :rows]),
                         reads=["xn%d" % (j % 2), "idb"], writes=["pT"])
                for q in range(4):
                    kc = g4 * 4 + q
                    P.op("act", lambda e, kc=kc, q=q, j=j, rows=rows, slot=slot: e.activation(out=hT[:, kc, j * 128:j * 128 + rows], in_=pT[:, slot, q * 128:q * 128 + rows], func=AF.Identity, scale=A1[:, kc:kc + 1], bias=mp[:, 2, kc:kc + 1]),
                         reads=["pT", "A1", "mp"], writes=["hT"])

        def chunk(buf, bk, lc, cc):
            i = cc % 2
            for kc in range(KC):
                P.op("pe", lambda e, kc=kc: e.matmul(pZ[i][:, 0:n], lhsT=buf[:, kc, lc * 128:(lc + 1) * 128], rhs=hT[:, kc, 0:n], start=(kc == 0), stop=(kc == KC - 1)),
                     reads=[bk, "hT"], writes=["pZ%d" % i])
            P.op("act", lambda e: e.activation(out=zp[i][:, 0:n], in_=pZ[i][:, 0:n], func=AF.Identity, bias=cv[:, 0, cc:cc + 1], scale=1.0),
                 reads=["pZ%d" % i, "cv"], writes=["zp%d" % i])
            if wi == 0:
                P.op("dve", lambda e: e.tensor_scalar(out=zp[i][:, 0:1], in0=zp[i][:, 0:1], scalar1=edg[:, 0:1], scalar2=None, op0=ALU.mult), reads=["zp%d" % i, "edg"], writes=["zp%d" % i])
            if wi == len(wins) - 1:
                P.op("dve", lambda e: e.tensor_scalar(out=zp[i][:, n - 1:n], in0=zp[i][:, n - 1:n], scalar1=edg[:, 1:2], scalar2=None, op0=ALU.mult), reads=["zp%d" % i, "edg"], writes=["zp%d" % i])
            a = acc[i]
            P.op("dve", lambda e: e.tensor_scalar(out=a[:, 0:n_out], in0=zp[i][:, 0:n_out], scalar1=cv[:, 1, cc:cc + 1], scalar2=cv[:, 4, cc:cc + 1], op0=ALU.mult, op1=ALU.add),
                 reads=["zp%d" % i, "cv"], writes=["acc%d" % i])
            P.op("dve", lambda e: e.scalar_tensor_tensor(out=a[:, 0:n_out], in0=zp[i][:, 1:n_out + 1], scalar=cv[:, 2, cc:cc + 1], in1=a[:, 0:n_out], op0=ALU.mult, op1=ALU.add),
                 reads=["zp%d" % i, "cv", "acc%d" % i], writes=["acc%d" % i])
            P.op("dve", lambda e: e.scalar_tensor_tensor(out=a[:, 0:n_out], in0=zp[i][:, 2:n_out + 2], scalar=cv[:, 3, cc:cc + 1], in1=a[:, 0:n_out], op0=ALU.mult, op1=ALU.add),
                 reads=["zp%d" % i, "cv", "acc%d" % i], writes=["acc%d" % i])
            P.op("pool", lambda e: e.dma_start(out=zT[cc * 128:(cc + 1) * 128, s:s + n_out], in_=a[:, 0:n_out]), reads=["acc%d" % i], dma=True)

        for blk in range(NCH // 512):
            ri = rcnt[0] % NR
            rcnt[0] += 1
            buf = ring[ri]
            P.op("sp", lambda e, blk=blk, buf=buf: e.dma_start(out=buf[:], in_=w_in[:, blk * 512:(blk + 1) * 512].rearrange("(c p) n -> p c n", p=128)),
                 writes=["ring%d" % ri], dma=True)
            for lc in range(4):
                chunk(buf, "ring%d" % ri, lc, blk * 4 + lc)

    for wi, (s_, n_) in enumerate(wins):
        do_window(wi, s_, n_)
    return C


def build_hyfilt(L, CH):
    C = Ctx()
    P = C.P
    CK = CH // 128
    NPC = L // 512
    zT = C.din("zT", [33, L], F32)
    w0 = C.din("w0", [33, 64], F32)
    w12 = C.din("w12", [2, 64, 64], F32)
    bvec = C.din("bvec", [64, 4], F32)
    w3c = C.din("w3c", [64, 4, CH], F32)
    tpos = C.din("tpos", [1, L], F32)
    ndelta = C.din("ndelta", [128, CK], F32)
    dskd = C.din("dsk", [128, 2, CK], F32)
    hraw = C.dint("hraw_scratch", [4, CH, L], F32)
    hnorm = C.dout("hnorm", [4, CH, L], F32)
    hsum = C.dout("hsum", [128, 2, CK], F32)
    dkt = C.sb("dkt", [128, 2, CK], F32)
    rinv = C.sb("rinv", [128, 2, CK], F32)
    NBW = min(2048, L)
    nb = [C.sb("nb%d" % i, [128, NBW], F32) for i in range(2)]
    zt = [C.sb("zt%d" % i, [33, 512], F32) for i in range(2)]
    tb = [C.sb("tb%d" % i, [128, 512], F32) for i in range(2)]
    w0t = C.sb("w0t", [33, 64], F32)
    w12t = C.sb("w12t", [64, 2, 64], F32)
    bv = C.sb("bv", [64, 4], F32)
    f3 = C.sb("f3", [64, 1], F32)
    fb3 = C.sb("fb3", [64, 3], F32)
    w3t = C.sb("w3t", [64, 4, CH], F32)
    nd = C.sb("nd", [128, CK], F32)
    sv = [C.sb("sv%d" % i, [64, 512], F32) for i in range(2)]
    s2 = [C.sb("s2%d" % i, [64, 512], F32) for i in range(2)]
    av = [C.sb("av%d" % i, [64, 512], F32) for i in range(3)]
    dec = [C.sb("dec%d" % i, [128, 512], F32) for i in range(2)]
    hs = [C.sb("hs%d" % i, [128, 512], F32) for i in range(3)]
    part = [C.sb("part%d" % i, [128, 1], F32) for i in range(2)]
    tot = C.sb("tot", [128, 2, CK], F32)
    pm = [C.ps("pm%d" % i, [128, 512], F32) for i in range(2)]
    ph = [C.ps("ph%d" % i, [128, 512], F32) for i in range(3)]
    P.op("sp", lambda e: e.dma_start(out=w0t[:], in_=w0), writes=["w0t"], dma=True)
    P.op("sp", lambda e: e.dma_start(out=w12t[:], in_=w12.rearrange("t k n -> k t n")), writes=["w12t"], dma=True)
    P.op("sp", lambda e: e.dma_start(out=bv[:], in_=bvec), writes=["bv"], dma=True)
    P.op("sp", lambda e: e.dma_start(out=w3t[:], in_=w3c), writes=["w3t"], dma=True)
    P.op("sp", lambda e: e.dma_start(out=nd[:], in_=ndelta), writes=["nd"], dma=True)
    P.op("dve", lambda e: e.memset(tot[:], 0.0), writes=["tot"])
    P.op("dve", lambda e: e.tensor_scalar(out=f3[:], in0=bv[:, 3:4], scalar1=1.0 / 3.0, scalar2=None, op0=ALU.mult), reads=["bv"], writes=["f3"])
    P.op("dve", lambda e: e.tensor_scalar(out=fb3[:], in0=bv[:, 0:3], scalar1=f3[:, 0:1], scalar2=None, op0=ALU.mult), reads=["bv", "f3"], writes=["fb3"])
    cnt = {"m": 0, "h": 0, "p": 0}

    def sin3(src_ps, li, dst, dkey):
        i = cnt["m"] % 2
        P.op("act", lambda e: e.activation(out=sv[i][:], in_=src_ps[0:64, :], func=AF.Sin, scale=f3[:, 0:1], bias=fb3[:, li:li + 1]),
             reads=["pm%d" % i, "f3", "fb3"], writes=["sv%d" % i])
        P.op("dve", lambda e: e.tensor_tensor(out=s2[i][:], in0=sv[i][:], in1=sv[i][:], op=ALU.mult), reads=["sv%d" % i], writes=["s2%d" % i])
        P.op("dve", lambda e: e.tensor_scalar(out=s2[i][:], in0=s2[i][:], scalar1=-4.0, scalar2=3.0, op0=ALU.mult, op1=ALU.add), reads=["s2%d" % i], writes=["s2%d" % i])
        P.op("dve", lambda e: e.tensor_tensor(out=dst[:], in0=s2[i][:], in1=sv[i][:], op=ALU.mult), reads=["s2%d" % i, "sv%d" % i], writes=[dkey])

    def pchunk(pc):
        b = pc % 2
        cs = slice(pc * 512, (pc + 1) * 512)
        P.op("sp", lambda e: e.dma_start(out=zt[b][:], in_=zT[:, cs]), writes=["zt%d" % b], dma=True)
        P.op("sp", lambda e: e.dma_start(out=tb[b][:], in_=tpos[:, cs].partition_broadcast(128)), writes=["tb%d" % b], dma=True)
        srcs = [(w0t[:, :], zt[b][:, :], ["w0t", "zt%d" % b])]
        for li in range(3):
            i = cnt["m"] % 2
            lhsT, rhs, rd = srcs[-1]
            P.op("pe", lambda e, lhsT=lhsT, rhs=rhs, i=i: e.matmul(pm[i][0:64, :], lhsT=lhsT, rhs=rhs, start=True, stop=True), reads=rd, writes=["pm%d" % i])
            sin3(pm[i], li, av[li], "av%d" % li)
            cnt["m"] += 1
            if li < 2:
                srcs.append((w12t[:, li, :], av[li][:, :], ["w12t", "av%d" % li]))
        for ck in range(CK):
            d = dec[ck % 2]
            P.op("act", lambda e, ck=ck, d=d: e.activation(out=d[:], in_=tb[b][:], func=AF.Exp, scale=nd[:, ck:ck + 1]), reads=["tb%d" % b, "nd"], writes=["dec%d" % (ck % 2)])
            for od in range(4):
                hi = cnt["h"] % 3
                cnt["h"] += 1
                P.op("pe", lambda e, od=od, ck=ck, hi=hi: e.matmul(ph[hi][:], lhsT=w3t[:, od, ck * 128:(ck + 1) * 128], rhs=av[2][:], start=True, stop=True),
                     reads=["w3t", "av2"], writes=["ph%d" % hi])
                P.op("dve", lambda e, hi=hi, d=d: e.tensor_tensor(out=hs[hi][:], in0=ph[hi][:], in1=d[:], op=ALU.mult), reads=["ph%d" % hi, "dec%d" % (ck % 2)], writes=["hs%d" % hi])
                P.op("pool", lambda e, od=od, ck=ck, hi=hi: e.dma_start(out=hraw[od, ck * 128:(ck + 1) * 128, cs], in_=hs[hi][:]), reads=["hs%d" % hi], writes=["hraw_%d_%d_%d" % (od, ck, pc * 512 // NBW)], dma=True)
                pi = cnt["p"] % 2
                cnt["p"] += 1
                c0 = 1 if (pc == 0 and od % 2 == 1) else 0
                P.op("dve", lambda e, hi=hi, pi=pi, c0=c0: e.tensor_reduce(out=part[pi][:], in_=hs[hi][:, c0:512], axis=AX.X, op=ALU.add, apply_absolute_value=True),
                     reads=["hs%d" % hi], writes=["part%d" % pi])
                o = od // 2
                P.op("dve", lambda e, pi=pi, o=o, ck=ck: e.tensor_tensor(out=tot[:, o, ck:ck + 1], in0=tot[:, o, ck:ck + 1], in1=part[pi][:], op=ALU.add),
                     reads=["part%d" % pi, "tot"], writes=["tot"])

    for pc in range(NPC):
        pchunk(pc)
    P.op("pool", lambda e: e.dma_start(out=hsum, in_=tot[:]), reads=["tot"], dma=True)
    P.op("sp", lambda e: e.dma_start(out=dkt[:], in_=dskd), writes=["dkt"], dma=True)
    P.op("dve", lambda e: e.reciprocal(out=rinv[:], in_=tot[:]), reads=["tot"], writes=["rinv"])
    k = 0
    for od in range(4):
        o = od // 2
        for ck in range(CK):
            for q in range(L // NBW):
                B_ = nb[k % 2]
                bk = "nb%d" % (k % 2)
                k += 1
                cs = slice(q * NBW, (q + 1) * NBW)
                P.op("sp", lambda e, od=od, ck=ck, cs=cs, B_=B_: e.dma_start(out=B_[:], in_=hraw[od, ck * 128:(ck + 1) * 128, cs]),
                     reads=["hraw_%d_%d_%d" % (od, ck, q)], writes=[bk], dma=True)
                P.op("dve", lambda e, o=o, ck=ck, B_=B_: e.tensor_scalar(out=B_[:], in0=B_[:], scalar1=rinv[:, o, ck:ck + 1], scalar2=None, op0=ALU.mult), reads=[bk, "rinv"], writes=[bk])
                if od % 2 == 0 and q == 0:
                    P.op("dve", lambda e, o=o, ck=ck, B_=B_: e.tensor_scalar(out=B_[:, 0:1], in0=B_[:, 0:1], scalar1=dkt[:, o, ck:ck + 1], scalar2=None, op0=ALU.add), reads=[bk, "dkt"], writes=[bk])
                P.op("pool", lambda e, od=od, ck=ck, cs=cs, B_=B_: e.dma_start(out=hnorm[od, ck * 128:(ck + 1) * 128, cs], in_=B_[:]), reads=[bk], dma=True)
    return C


def build_hyconv(L, CH, NB, LB=1024):
    C = Ctx()
    P = C.P
    CK = CH // 128
    H = L // 2
    zc = C.din("zc", [3, CH, NB, L], F32)
    hraw = C.din("hraw", [4, CH, L], F32)
    hsum = C.din("hsum", [128, 2, CK], F32)
    dsk = C.din("dsk", [128, 2, CK], F32)
    yT = C.dout("yT", [CH, NB, L], BF16)
    U = C.sb("U", [128, L], F32)
    Y = C.sb("Y", [128, L], F32)
    hb = [[C.sb("hb%d_%d" % (d, i), [128, LB], F32) for i in range(2)] for d in range(2)]
    gt = [C.sb("gt%d" % i, [128, LB], F32) for i in range(2)]
    tm = [C.sb("tm%d" % i, [128, LB], F32) for i in range(2)]
    ob = [C.sb("ob%d" % i, [128, LB], BF16) for i in range(2)]
    hsm = C.sb("hsm", [128, 2, CK], F32)
    dk = C.sb("dk", [128, 2, CK], F32)
    P.op("sp", lambda e: e.dma_start(out=hsm[:], in_=hsum), writes=["hsm"], dma=True)
    P.op("sp", lambda e: e.dma_start(out=dk[:], in_=dsk), writes=["dk"], dma=True)
    P.op("dve", lambda e: e.reciprocal(out=hsm[:], in_=hsm[:]), reads=["hsm"], writes=["hsm"])
    cnt = {"h": 0, "g": 0}
    halves = (("dve", 0, L, "Ylo"),)

    def stt(eng, o_ap, in0, sc, in1, rd, wr):
        P.op(eng, lambda e: e.scalar_tensor_tensor(out=o_ap, in0=in0, scalar=sc, in1=in1, op0=ALU.mult, op1=ALU.add), reads=rd, writes=wr)

    def one(ck, b, o):
        P.op("dve", lambda e: e.memset(Y[:, :], 0.0), writes=["Ylo"])
        for lb in range(L // LB):
            bi = cnt["h"] % 2
            cnt["h"] += 1
            hf, hbk = hb[0][bi], hb[1][bi]
            kf, kb = "hb0_%d" % bi, "hb1_%d" % bi
            P.op("sp", lambda e, lb=lb, hf=hf: e.dma_start(out=hf[:], in_=hraw[2 * o, ck * 128:(ck + 1) * 128, lb * LB:(lb + 1) * LB]), writes=[kf], dma=True)
            P.op("sp", lambda e, lb=lb, hbk=hbk: e.dma_start(out=hbk[:], in_=hraw[2 * o + 1, ck * 128:(ck + 1) * 128, lb * LB:(lb + 1) * LB]), writes=[kb], dma=True)
            for j in range(LB):
                tau = lb * LB + j
                for eng, a, bnd, yk in halves:
                    t0 = max(a, tau)
                    if t0 < bnd:
                        stt(eng, Y[:, t0:bnd], U[:, t0 - tau:bnd - tau], hf[:, j:j + 1], Y[:, t0:bnd], [kf, "U", yk], [yk])
                    t1 = min(bnd, L - tau)
                    if tau >= 1 and a < t1:
                        stt(eng, Y[:, a:t1], U[:, a + tau:t1 + tau], hbk[:, j:j + 1], Y[:, a:t1], [kb, "U", yk], [yk])
        for pc in range(L // LB):
            gi = cnt["g"] % 2
            cnt["g"] += 1
            cs = slice(pc * LB, (pc + 1) * LB)
            yk = "Ylo"
            G, T = gt[gi], tm[gi]
            P.op("sp", lambda e, cs=cs, G=G: e.dma_start(out=G[:], in_=zc[1 + o, ck * 128:(ck + 1) * 128, b, cs]), writes=["gt%d" % gi], dma=True)
            P.op("dve", lambda e, cs=cs, T=T: e.tensor_scalar(out=T[:], in0=Y[:, cs], scalar1=hsm[:, o, ck:ck + 1], scalar2=None, op0=ALU.mult), reads=[yk, "hsm"], writes=["tm%d" % gi])
            P.op("dve", lambda e, cs=cs, T=T: e.scalar_tensor_tensor(out=T[:], in0=U[:, cs], scalar=dk[:, o, ck:ck + 1], in1=T[:], op0=ALU.mult, op1=ALU.add), reads=["U", "dk", "tm%d" % gi], writes=["tm%d" % gi])
            if o == 0:
                P.op("dve", lambda e, cs=cs, T=T, G=G: e.tensor_tensor(out=U[:, cs], in0=T[:], in1=G[:], op=ALU.mult), reads=["tm%d" % gi, "gt%d" % gi], writes=["U"])
            else:
                O = ob[gi]
                P.op("dve", lambda e, T=T, G=G, O=O: e.tensor_tensor(out=O[:], in0=T[:], in1=G[:], op=ALU.mult), reads=["tm%d" % gi, "gt%d" % gi], writes=["ob%d" % gi])
                P.op("sp", lambda e, cs=cs, O=O: e.dma_start(out=yT[ck * 128:(ck + 1) * 128, b, cs], in_=O[:]), reads=["ob%d" % gi], dma=True)

    for ck in range(CK):
        for b in range(NB):
            P.op("sp", lambda e, ck=ck, b=b: e.dma_start(out=U[:], in_=zc[0, ck * 128:(ck + 1) * 128, b, :]), writes=["U"], dma=True)
            for o in range(2):
                one(ck, b, o)
    return C


def fft_consts():
    N = 32768
    f64 = np.float64
    n1 = np.arange(128)
    k1 = np.arange(128)
    ang = 2 * np.pi * np.outer(n1, k1) / 128.0
    F128 = np.stack([np.cos(ang), -np.sin(ang)], 1)
    n2 = np.arange(256)
    k2 = np.arange(256)
    ang2 = 2 * np.pi * np.outer(n2, k2) / 256.0
    Fr, Fi = np.cos(ang2), -np.sin(ang2)
    F256 = np.stack([Fr, Fi, -Fi], 1).reshape(2, 128, 3, 256)
    angt = 2 * np.pi * np.outer(n2, k1) / N
    TW = np.stack([np.cos(angt), -np.sin(angt)], 1).reshape(2, 128, 2, 1, 128)
    TW2 = np.repeat(TW, 4, axis=3)
    Cr, Ci = np.cos(ang2), np.sin(ang2)
    IC = np.stack([np.concatenate([Cr, Ci], 1), np.concatenate([-Ci, Cr], 1)], 1).reshape(2, 128, 2, 512)
    angti = 2 * np.pi * np.outer(k1, n2) / N
    TWI = np.repeat((np.stack([np.cos(angti), np.sin(angti)], 1) / N)[:, :, None, :], 2, axis=2)
    angi = 2 * np.pi * np.outer(k1, n1[:64]) / 128.0
    I2M = np.stack([np.cos(angi), -np.sin(angi)], 1)
    c = lambda a: np.ascontiguousarray(a.astype(np.float32))
    return dict(F128=c(F128), F256=c(F256), TW2=c(TW2), IC=c(IC), TWI=c(TWI), I2M=c(I2M))


def build_hyfft(NG, CHG=8, NB=2, fast=True):
    C = Ctx()
    P = C.P
    S = NB * CHG
    NGF = NG
    NCHAN = NG * CHG
    assert NCHAN % S == 0
    NFG = NCHAN // S
    ut = C.din("ut", [NG, 64, S, 256], F32)
    gt = C.din("gt", [2, NG, 64, S, 256], F32)
    ft = C.din("ft", [2, NFG, 128, S, 256], F32)
    F128d = C.din("F128", [128, 2, 128], F32)
    F256d = C.din("F256", [2, 128, 3, 256], F32)
    TW2d = C.din("TW2", [2, 128, 2, 4, 128], F32)
    ICd = C.din("IC", [2, 128, 2, 512], F32)
    TWId = C.din("TWI", [128, 2, 2, 256], F32)
    I2Md = C.din("I2M", [128, 2, 64], F32)
    HFd = C.dint("HFd", [2, NFG, 128, 2, 2, S, 128], F32)
    yt = C.dout("yt", [NG, 64, S, 256], BF16)

    MT = mybir.dt.float32r if fast else F32
    U = C.sb("U", [128, S, 256], MT)
    G = [C.sb("G%d" % i, [128, S, 256], F32) for i in range(2)]
    AT = C.sb("AT", [128, 2, 2, S, 128], MT)
    YS = C.sb("YS", [128, 2, 2, S, 128], MT)
    HF = C.sb("HF", [128, 2, 2, CHG, 128], F32)
    YO = C.sb("YO", [64, S, 256], BF16)
    F128 = C.sb("F128s", [128, 2, 128], MT)
    F256 = C.sb("F256s", [128, 2, 3, 256], MT)
    TW2 = C.sb("TW2s", [128, 2, 2, 4, 128], F32)
    IC = C.sb("ICs", [128, 2, 2, 512], MT)
    TWI = C.sb("TWIs", [128, 2, 2, 256], F32)
    I2M = C.sb("I2Ms", [128, 2, 64], MT)
    NTMP = 2
    tmp = [[C.sb("tmp%d_%d" % (k, i), [128, 512], F32) for i in range(NTMP)] for k in range(4)]
    pA = [C.ps("pA%d" % i, [128, 1024], F32) for i in range(2)]
    pX = [[C.ps("pX%d_%d" % (r, i), [128, 512], F32) for i in range(2)] for r in range(2)]
    ZT = AT

    STG = G[1][:, :, :].rearrange("p s n -> p (s n)")

    def stage(dst_flat, src_ap, nel, key):
        P.op("sp", lambda e: e.dma_start(out=STG[:, 0:nel].rearrange(src_ap[1], **src_ap[2]) if src_ap[1] else STG[:, 0:nel], in_=src_ap[0]), writes=["G1"], dma=True)
        P.op("act", lambda e: e.activation(out=dst_flat, in_=STG[:, 0:nel], func=AF.Identity), reads=["G1"], writes=[key])

    stage(F128[:, :, :].rearrange("p r n -> p (r n)"), (F128d.rearrange("p r n -> p (r n)"), None, None), 256, "F128")
    stage(F256[:, :, :, :].rearrange("p t v n -> p (t v n)"), (F256d.rearrange("t p v n -> p t (v n)"), "p (t x) -> p t x", dict(t=2)), 1536, "F256")
    stage(IC[:, :, :, :].rearrange("p t v n -> p (t v n)"), (ICd.rearrange("t p v n -> p t (v n)"), "p (t x) -> p t x", dict(t=2)), 2048, "IC")
    stage(I2M[:, :, :].rearrange("p v n -> p (v n)"), (I2Md.rearrange("p v n -> p (v n)"), None, None), 128, "I2M")
    P.op("sp", lambda e: e.dma_start(out=TW2[:], in_=TW2d.rearrange("h p r s n -> p h r s n")), writes=["TW2"], dma=True)
    P.op("sp", lambda e: e.dma_start(out=TWI[:], in_=TWId), writes=["TWI"], dma=True)
    cnt = {"a": 0, "x": 0, "t": 0, "y": 0}

    def fr(ap):
        return ap

    def cmul(ar, ai, br, bi, outr, outi, rd, wr, neg_first=True):
        ti = cnt["t"] % NTMP
        cnt["t"] += 1
        n = None
        T = [tmp[k][ti] for k in range(4)]
        keys = ["tmp%d_%d" % (k, ti) for k in range(4)]

        def view(t, ref):
            return t

        P.op("dve", lambda e: e.tensor_tensor(out=shape_like(T[0], ar), in0=ar, in1=br, op=ALU.mult), reads=rd, writes=[keys[0]])
        P.op("dve", lambda e: e.tensor_tensor(out=shape_like(T[1], ar), in0=ai, in1=bi, op=ALU.mult), reads=rd, writes=[keys[1]])
        P.op("dve", lambda e: e.tensor_tensor(out=shape_like(T[2], ar), in0=ar, in1=bi, op=ALU.mult), reads=rd, writes=[keys[2]])
        P.op("dve", lambda e: e.tensor_tensor(out=shape_like(T[3], ar), in0=ai, in1=br, op=ALU.mult), reads=rd, writes=[keys[3]])
        P.op("pool", lambda e: e.tensor_tensor(out=outr, in0=shape_like(T[0], ar), in1=shape_like(T[1], ar), op=ALU.subtract), reads=keys[0:2], writes=wr)
        P.op("pool", lambda e: e.tensor_tensor(out=outi, in0=shape_like(T[2], ar), in1=shape_like(T[3], ar), op=ALU.add), reads=keys[2:4], writes=wr)

    def shape_like(t, ref):
        shp = list(ref.shape)
        if len(shp) == 2:
            return t[:, 0:shp[1]]
        assert len(shp) == 3
        return t[:, 0:shp[1] * shp[2]].rearrange("p (a b) -> p a b", a=shp[1])

    def forward(K, ukey):
        for s in range(0, S, 4):
            for h in range(2):
                ai = cnt["a"] % 2
                cnt["a"] += 1
                for d in range(4):
                    P.op("pe", lambda e, s=s, h=h, d=d, ai=ai: e.matmul(pA[ai][:, d * 256:(d + 1) * 256], lhsT=fr(U[0:K, s + d, h * 128:(h + 1) * 128]), rhs=fr(F128[0:K, :, :].rearrange("p r n -> p (r n)")), start=True, stop=True),
                         reads=[ukey, "F128"], writes=["pA%d" % ai])
                pv = pA[ai][:, :].rearrange("p (d r n) -> p d r n", d=4, r=2)
                cmul(pv[:, :, 0, :], pv[:, :, 1, :], TW2[:, h, 0, :, :], TW2[:, h, 1, :, :],
                     AT[:, h, 0, s:s + 4, :], AT[:, h, 1, s:s + 4, :], ["pA%d" % ai, "TW2"], ["AT"])

    def second(consume):
        for m in range(2):
            for c in range(S // 4):
                xi = cnt["x"] % 2
                cnt["x"] += 1
                pr, pi_ = pX[0][xi], pX[1][xi]
                kr, ki = "pX0_%d" % xi, "pX1_%d" % xi
                ms = slice(m * 128, (m + 1) * 128)
                for h in range(2):
                    ar = AT[:, h, 0, c * 4:(c + 1) * 4, :].rearrange("p s n -> p (s n)")
                    ai_ = AT[:, h, 1, c * 4:(c + 1) * 4, :].rearrange("p s n -> p (s n)")
                    P.op("pe", lambda e, h=h, ar=ar, pr=pr, ms=ms: e.matmul(pr[:], lhsT=fr(F256[:, h, 0, ms]), rhs=fr(ar), start=(h == 0), stop=False), reads=["F256", "AT"], writes=[kr])
                    P.op("pe", lambda e, h=h, ai_=ai_, pr=pr, ms=ms: e.matmul(pr[:], lhsT=fr(F256[:, h, 2, ms]), rhs=fr(ai_), start=False, stop=(h == 1)), reads=["F256", "AT"], writes=[kr])
                    P.op("pe", lambda e, h=h, ar=ar, pi_=pi_, ms=ms: e.matmul(pi_[:], lhsT=fr(F256[:, h, 1, ms]), rhs=fr(ar), start=(h == 0), stop=False), reads=["F256", "AT"], writes=[ki])
                    P.op("pe", lambda e, h=h, ai_=ai_, pi_=pi_, ms=ms: e.matmul(pi_[:], lhsT=fr(F256[:, h, 0, ms]), rhs=fr(ai_), start=False, stop=(h == 1)), reads=["F256", "AT"], writes=[ki])
                consume(m, c, pr, pi_, [kr, ki])

    for o in range(2):
        for fg in range(NFG):
            P.op("sp", lambda e, o=o, fg=fg: e.dma_start(out=STG[:, 0:S * 256].rearrange("p (s n) -> p s n", s=S), in_=ft[o, fg]), writes=["G1"], dma=True)
            P.op("act", lambda e: e.activation(out=U[:, :, :].rearrange("p s n -> p (s n)"), in_=STG[:, 0:S * 256], func=AF.Identity), reads=["G1"], writes=["U"])
            forward(128, "U")

            def cons_f(m, c, pr, pi_, keys, o=o, fg=fg):
                ti = cnt["t"] % NTMP
                cnt["t"] += 1
                tr, tii = tmp[0][ti], tmp[1][ti]
                P.op("act", lambda e: e.activation(out=tr[:], in_=pr[:], func=AF.Identity), reads=[keys[0]], writes=["tmp0_%d" % ti])
                P.op("act", lambda e: e.activation(out=tii[:], in_=pi_[:], func=AF.Identity), reads=[keys[1]], writes=["tmp1_%d" % ti])
                P.op("sp", lambda e: e.dma_start(out=HFd[o, fg, :, m, 0, c * 4:(c + 1) * 4, :], in_=tr[:].rearrange("p (s n) -> p s n", s=4)), reads=["tmp0_%d" % ti], writes=["HFd_%d_%d_%d_%d_0" % (o, fg, m, c)], dma=True)
                P.op("sp", lambda e: e.dma_start(out=HFd[o, fg, :, m, 1, c * 4:(c + 1) * 4, :], in_=tii[:].rearrange("p (s n) -> p s n", s=4)), reads=["tmp1_%d" % ti], writes=["HFd_%d_%d_%d_%d_1" % (o, fg, m, c)], dma=True)
            second(cons_f)

    def conv(o, g, ukey, gate, gkey, final):
        fg, off = divmod(g * CHG, S)
        for m in range(2):
            for r in range(2):
                P.op("sp", lambda e, m=m, r=r: e.dma_start(out=HF[:, m, r], in_=HFd[o, fg, :, m, r, off:off + CHG, :]),
                     reads=["HFd_%d_%d_%d_%d_%d" % (o, fg, m, cc, r) for cc in range(S // 4)], writes=["HF"], dma=True)
        forward(64, ukey)

        def cons(m, c, pr, pi_, keys):
            b, c2 = divmod(c, CHG // 4)
            hs = slice(c2 * 4, c2 * 4 + 4)
            cmul(pr[:].rearrange("p (s n) -> p s n", s=4), pi_[:].rearrange("p (s n) -> p s n", s=4),
                 HF[:, m, 0, hs, :], HF[:, m, 1, hs, :],
                 YS[:, m, 0, c * 4:(c + 1) * 4, :], YS[:, m, 1, c * 4:(c + 1) * 4, :], keys + ["HF"], ["YS"])
        second(cons)
        ZTv = ZT[:, :, :, :, :].rearrange("p a b s n -> p (a b s n)").rearrange("p (r s n) -> p r s n", r=2, s=S)
        for s in range(0, S, 2):
            ai = cnt["a"] % 2
            cnt["a"] += 1
            for d in range(2):
                first = True
                for m in range(2):
                    for v, ri in ((0, 0), (1, 1)):
                        last = (m == 1 and v == 1)
                        P.op("pe", lambda e, s=s, d=d, m=m, v=v, ri=ri, ai=ai, first=first, last=last: e.matmul(pA[ai][:, d * 512:(d + 1) * 512], lhsT=fr(YS[:, m, ri, s + d, :]), rhs=fr(IC[:, m, v, :]), start=first, stop=last),
                             reads=["YS", "IC"], writes=["pA%d" % ai])
                        first = False
            pv = pA[ai][:, :].rearrange("p (d r n) -> p d r n", d=2, r=2)
            cmul(pv[:, :, 0, :], pv[:, :, 1, :], TWI[:, 0, :, :], TWI[:, 1, :, :], ZTv[:, 0, s:s + 2, :], ZTv[:, 1, s:s + 2, :], ["pA%d" % ai, "TWI"], ["AT"])
        for c in range(S // 2):
            yi = cnt["a"] % 2
            cnt["a"] += 1
            zr = ZTv[:, 0, 2 * c:2 * c + 2, :].rearrange("p s n -> p (s n)")
            zi = ZTv[:, 1, 2 * c:2 * c + 2, :].rearrange("p s n -> p (s n)")
            P.op("pe", lambda e, zr=zr, yi=yi: e.matmul(pA[yi][0:64, 0:512], lhsT=fr(I2M[:, 0, :]), rhs=fr(zr), start=True, stop=False), reads=["I2M", "AT"], writes=["pA%d" % yi])
            P.op("pe", lambda e, zi=zi, yi=yi: e.matmul(pA[yi][0:64, 0:512], lhsT=fr(I2M[:, 1, :]), rhs=fr(zi), start=False, stop=True), reads=["I2M", "AT"], writes=["pA%d" % yi])
            gv = gate[0:64, 2 * c:2 * c + 2, :].rearrange("p s n -> p (s n)")
            if not final:
                ov = U[0:64, 2 * c:2 * c + 2, :].rearrange("p s n -> p (s n)")
                P.op("dve", lambda e, yi=yi, gv=gv, ov=ov: e.tensor_tensor(out=ov, in0=pA[yi][0:64, 0:512], in1=gv, op=ALU.mult), reads=["pA%d" % yi, gkey], writes=["U2"])
            else:
                ov = YO[:, 2 * c:2 * c + 2, :].rearrange("p s n -> p (s n)")
                P.op("dve", lambda e, yi=yi, gv=gv, ov=ov: e.tensor_tensor(out=ov, in0=pA[yi][0:64, 0:512], in1=gv, op=ALU.mult), reads=["pA%d" % yi, gkey], writes=["YO"])

    for g in range(NG):
        fg, off = divmod(g * CHG, S)
        P.op("sp", lambda e, g=g: e.dma_start(out=STG[0:64, 0:S * 256].rearrange("p (s n) -> p s n", s=S), in_=ut[g]), writes=["G1"], dma=True)
        P.op("act", lambda e: e.activation(out=U[0:64, :, :].rearrange("p s n -> p (s n)"), in_=STG[0:64, 0:S * 256], func=AF.Identity), reads=["G1"], writes=["U", "U2"])
        for o in range(2):
            P.op("sp", lambda e, g=g, o=o: e.dma_start(out=G[o][0:64], in_=gt[o, g]), writes=["G%d" % o], dma=True)
        conv(0, g, "U", G[0], "G0", False)
        conv(1, g, "U2", G[1], "G1", True)
        P.op("pool", lambda e, g=g: e.dma_start(out=yt[g], in_=YO[:]), reads=["YO"], dma=True)
    return C


def rope_tables_fm(pos, nctx):
    pos = np.asarray(pos, np.float32)
    row = np.floor(pos / 64.0).astype(np.float32)
    col = (pos - 64.0 * row).astype(np.float32)
    out = np.zeros((4, 128, len(pos) + nctx), np.float32)
    out[0, :, :] = 1.0
    out[2, :, :] = 1.0
    for typ, nf in ((0, 16), (1, 32)):
        inv = (np.float32(10000.0) ** (-np.arange(nf, dtype=np.float32) / np.float32(nf))).astype(np.float32)
        p = np.arange(128)
        axis = (p % (4 * nf)) // (2 * nf)
        f = p % nf
        ang = np.where(axis[:, None] == 0, row[None, :], col[None, :]).astype(np.float32) * inv[f][:, None]
        out[2 * typ, :, :len(pos)] = np.cos(ang)
        out[2 * typ + 1, :, :len(pos)] = np.sin(ang)
    return out

def perm_mats():
    pm = np.zeros((2, 128, 128), np.float32)
    for typ, nf in ((0, 16), (1, 32)):
        for pd in range(128):
            half = (pd % (2 * nf)) // nf
            if half == 0:
                pm[typ, pd + nf, pd] = -1.0
            else:
                pm[typ, pd - nf, pd] = 1.0
    return pm

def block_ones():
    b = np.zeros((2, 128, 128), np.float32)
    b[0, :64, :64] = 1; b[0, 64:, 64:] = 1
    b[1] = 1
    return b.astype(NPBF)


B_, L_, D_, DFF_ = 2, 16384, 2048, 5632
NCORE = 8
TOK_ = L_ * B_ // NCORE
SH_ = L_ // TOK_
NCTX_ = 256
HA_, HB_, GB_ = 8, 8, 2
_f32 = np.float32


def _bf(a):
    return np.ascontiguousarray(np.asarray(a)).astype(NPBF)


def _halo(full, b, t0):
    out = np.zeros((TOK_ + 2, full.shape[-1]), _f32)
    lo, hi = t0 - 1, t0 + TOK_ + 1
    slo, shi = max(lo, 0), min(hi, L_)
    out[slo - lo:shi - lo] = full[b, slo:shi]
    return out


def _edge(q):
    e = np.ones((128, 2), _f32)
    if q == 0:
        e[:, 0] = 0
    if q == SH_ - 1:
        e[:, 1] = 0
    return e


def _filt_consts(L, ch0, CH, Dm):
    t = np.linspace(0.0, 1.0, L, dtype=_f32)
    bands = 16
    w = (2.0 * math.pi * np.arange(L, dtype=_f32) / L).astype(_f32)
    f = np.linspace(1e-4, bands - 1, bands, dtype=_f32)
    z = np.concatenate([t[:, None], np.cos(f[None] * w[:, None]), -np.sin(f[None] * w[:, None])], -1).astype(_f32)
    maxd = math.log(1e-2) / 0.3
    mind = math.log(1e-2) / 1.5
    deltas = np.linspace(mind, maxd, Dm, dtype=_f32)
    nd = -np.abs(deltas[ch0:ch0 + CH])
    return np.ascontiguousarray(z.T), t[None], pl(nd)


def kernel(**inp):
    I = {k: np.asarray(v) for k, v in inp.items()}
    ident = np.eye(128).astype(NPBF)
    cores = [(c // SH_, c % SH_) for c in range(NCORE)]

    wlist = [I["attn_w_in"][0], I["attn_w_out"][0], I["hy_w_in"][0], I["hy_w_out"][0],
             I["ffn_w_gate"][0], I["ffn_w_val"][0], I["ffn_w_down"][0],
             I["ffn_w_gate"][1], I["ffn_w_val"][1], I["ffn_w_down"][1]]
    shapes = [(w.shape[0] // NCORE, w.shape[1]) for w in wlist]
    nc = build_cast(shapes).finish()
    ims = [{("w%d" % i): np.ascontiguousarray(w[c * s[0]:(c + 1) * s[0]]) for i, (w, s) in enumerate(zip(wlist, shapes))} for c in range(NCORE)]
    res = run(nc, ims)
    wb = [np.concatenate([res[c]["o%d" % i] for c in range(NCORE)], 0) for i in range(len(wlist))]
    w_in_b, w_out_b, hy_in_b, hy_out_b = wb[0:4]
    ffn_b = [wb[4:7], wb[7:10]]
    del res, ims

    NU = 6
    cT = pl(np.stack([I["c"][0], I["c"][1], I["c_ctx"]], 1))
    units = [(u // 24, u % 24) for u in range(48)]
    nc = build_ada(D_, NU).finish()
    ims = []
    for c in range(NCORE):
        us = units[c * NU:(c + 1) * NU]
        ims.append(dict(cT=cT, aw=np.stack([I["ada_w"][l][:, cb * 512:(cb + 1) * 512] for l, cb in us]),
                        ab=np.stack([I["ada_b"][l][None, cb * 512:(cb + 1) * 512] for l, cb in us])))
    res = run(nc, ims)
    m = np.zeros((2, 3, 6 * D_), _f32)
    for c in range(NCORE):
        for k, (l, cb) in enumerate(units[c * NU:(c + 1) * NU]):
            m[l, :, cb * 512:(cb + 1) * 512] = res[c]["out"][k]
    mod = m.reshape(2, 3, 6, D_)
    del res, ims

    x, ctx = I["x"], I["ctx"]

    def post_mixer(l, xres, mix_of_core, wo_b):
        nc = build_outproj(TOK_, D_, D_).finish()
        ims = []
        for c, (b, q) in enumerate(cores):
            t0 = q * TOK_
            ims.append(dict(x=np.ascontiguousarray(xres[b, t0:t0 + TOK_]), mixT=mix_of_core(c), wo=wo_b, g1row=np.ascontiguousarray(mod[l, b, 2][None])))
        res = run(nc, ims)
        xs1 = np.stack([np.concatenate([res[b * SH_ + q]["out"] for q in range(SH_)], 0) for b in range(B_)], 0)
        del res, ims
        nc = build_ffn(TOK_, D_, DFF_).finish()
        wg, wv, wd = ffn_b[l]
        cw = np.ascontiguousarray(pl(I["ffn_conv_w"][l].T).transpose(0, 2, 1))
        cbv = pl(I["ffn_conv_b"][l])
        ims = []
        for c, (b, q) in enumerate(cores):
            t0 = q * TOK_
            ims.append(dict(xh=_halo(xs1, b, t0), edge=_edge(q), modp=np.stack([pl(I["norm2_g"][l]), pl(mod[l, b, 4]), pl(mod[l, b, 3])], 1),
                            g2row=np.ascontiguousarray(mod[l, b, 5][None]), wg=wg, wv=wv, wd=wd, cw=cw, cb=cbv, ident=ident))
        res = run(nc, ims)
        return np.stack([np.concatenate([res[b * SH_ + q]["out"] for q in range(SH_)], 0) for b in range(B_)], 0)

    nc = build_qkv(TOK_, NCTX_, D_, HA_, HB_, GB_).finish()
    gvec = np.stack([np.tile(I["a_q_g"][0], 2), np.tile(I["a_k_g"][0], 2), I["b_q_g"][0], I["b_k_g"][0]], 1).astype(_f32)
    pm, bo = perm_mats(), block_ones()
    ims = []
    for c, (b, q) in enumerate(cores):
        t0 = q * TOK_
        modp = np.stack([np.stack([pl(I["norm1_g"][0]), pl(mod[0, r, 1]), pl(mod[0, r, 0])], 1) for r in (b, 2)], 1)
        ims.append(dict(x=np.concatenate([x[b, t0:t0 + TOK_], ctx[b]], 0), modp=modp, w_in=w_in_b, gvec=gvec,
                        rope=rope_tables_fm(t0 + np.arange(TOK_), NCTX_), perm=pm, bones=bo, ident=ident))
    res = run(nc, ims)
    qTs = [res[c]["qT"] for c in range(NCORE)]
    NA = HA_ * 128
    kaT, va, kbT, vb = [], [], [], []
    for b in range(B_):
        rs = [res[b * SH_ + q] for q in range(SH_)]
        kaT.append(np.concatenate([rs[0]["kT"][:NA, TOK_:]] + [r["kT"][:NA, :TOK_] for r in rs], 1))
        va.append(np.concatenate([rs[0]["v"][TOK_:, :NA]] + [r["v"][:TOK_, :NA] for r in rs], 0))
        kbT.append((rs[0]["kT"][NA:, TOK_:], np.concatenate([r["kT"][NA:, :TOK_] for r in rs], 1)))
        vb.append((rs[0]["v"][TOK_:, NA:], np.concatenate([r["v"][:TOK_, NA:] for r in rs], 0)))
    del res, ims

    NKEY = NCTX_ + L_
    NLT = TOK_ // 128 + 4
    lam_init = 0.8 - 0.6 * math.exp(-0.3 * 0)
    nc = build_attn(TOK_, NKEY, HA_, HB_, GB_, lam_init).finish()
    a = np.arange(128)
    masks = np.stack([(a[:, None] >= a[None, :]), (a[:, None] <= a[None, :])], 0).astype(NPBF)
    lamv = np.concatenate([I["a_lam_q1"][0], I["a_lam_k1"][0], I["a_lam_q2"][0], I["a_lam_k2"][0]])[None].astype(_f32)
    ims = []
    for c, (b, q) in enumerate(cores):
        t0 = q * TOK_
        kc_, kl_ = kbT[b]
        vc_, vl_ = vb[b]
        kloc = np.zeros((GB_ * 128, NLT * 128), NPBF)
        vloc = np.zeros((NLT * 128, GB_ * 128), NPBF)
        kbias = np.zeros((128, NLT), _f32)
        kloc[:, :NCTX_] = kc_
        vloc[:NCTX_] = vc_
        lo, hi = t0 - 128, t0 + TOK_ + 128
        slo, shi = max(lo, 0), min(hi, L_)
        kloc[:, NCTX_ + (slo - lo):NCTX_ + (shi - lo)] = kl_[:, slo:shi]
        vloc[NCTX_ + (slo - lo):NCTX_ + (shi - lo)] = vl_[slo:shi]
        if lo < 0:
            kbias[:, 2] = -30000.0
        if hi > L_:
            kbias[:, NLT - 1] = -30000.0
        ims.append(dict(qT=qTs[c], kaT=kaT[b], va=va[b], kbT=kloc, vb=vloc, kbias=kbias, masks=masks, lamv=lamv,
                        sinkv=I["b_sink"][0][None].astype(_f32), subg=I["a_sub_g"][0][:, None].astype(_f32)))
    res = run(nc, ims)
    mix0 = [res[c]["mixT"] for c in range(NCORE)]
    del res, ims, kaT, va, kbT, vb, qTs

    xs = post_mixer(0, x, lambda c: mix0[c], w_out_b)
    del mix0

    NCH = 3 * D_
    nc = build_hyin(TOK_, D_, NCH).finish()
    cvec = np.stack([pl(I["hy_b_in"][0]), pl(I["hy_sconv_w"][0][0]), pl(I["hy_sconv_w"][0][1]), pl(I["hy_sconv_w"][0][2]), pl(I["hy_sconv_b"][0])], 1)
    ims = []
    for c, (b, q) in enumerate(cores):
        t0 = q * TOK_
        ims.append(dict(xh=_halo(xs, b, t0), edge=_edge(q), modp=np.stack([pl(I["norm1_g"][1]), pl(mod[1, b, 1]), pl(mod[1, b, 0])], 1),
                        w_in=hy_in_b, cvec=cvec, ident=ident))
    res = run(nc, ims)
    zfull = np.zeros((NCH, B_, L_), _f32)
    for c, (b, q) in enumerate(cores):
        zfull[:, b, q * TOK_:(q + 1) * TOK_] = res[c]["zT"]
    del res, ims

    CH = D_ // NCORE
    nc = build_hyfilt(L_, CH).finish()
    w3 = I["hy_f_w3"][0].reshape(64, 4, D_)
    bvec = np.stack([I["hy_f_b0"][0], I["hy_f_b1"][0], I["hy_f_b2"][0], I["hy_f_freq"][0]], 1).astype(_f32)
    ims = []
    for c in range(NCORE):
        zT_, tpos, nd = _filt_consts(L_, c * CH, CH, D_)
        dsk = np.ascontiguousarray(pl(I["hy_d"][0][:, c * CH:(c + 1) * CH].T).transpose(0, 2, 1))
        ims.append(dict(zT=zT_, w0=I["hy_f_w0"][0], w12=np.stack([I["hy_f_w1"][0], I["hy_f_w2"][0]]), bvec=bvec,
                        w3c=np.ascontiguousarray(w3[:, :, c * CH:(c + 1) * CH]), tpos=tpos, ndelta=nd, dsk=dsk))
    fres = run(nc, ims)
    del ims

    CHG, S = 8, 16
    NG = CH // CHG
    nc = build_hyfft(NG, CHG, B_).finish()
    fc = fft_consts()

    def grp(a_):
        v = a_.reshape(NG, CHG, B_, 64, 256).transpose(0, 3, 2, 1, 4)
        return np.ascontiguousarray(v.reshape(NG, 64, S, 256))

    ims = []
    for c in range(NCORE):
        hn = fres[c]["hnorm"]
        filt = np.zeros((2, CH, 2 * L_), _f32)
        for o in range(2):
            filt[o, :, :L_] = hn[2 * o]
            filt[o, :, L_ + 1:] = hn[2 * o + 1][:, :0:-1]
        ft = filt.reshape(2, CH // S, S, 128, 256).transpose(0, 1, 3, 2, 4)
        zc = [zfull[k * D_ + c * CH:k * D_ + (c + 1) * CH] for k in range(3)]
        ims.append(dict(ut=grp(zc[0]), gt=np.stack([grp(zc[1]), grp(zc[2])]), ft=np.ascontiguousarray(ft), **fc))
    res = run(nc, ims)
    yfull = np.concatenate([np.asarray(res[c]["yt"]).reshape(NG, 64, B_, CHG, 256).transpose(0, 3, 2, 1, 4).reshape(CH, B_, L_)
                            for c in range(NCORE)], 0)
    del res, ims, fres, zfull

    def mix1(c):
        b, q = cores[c]
        return np.ascontiguousarray(yfull[:, b, q * TOK_:(q + 1) * TOK_])
    out = post_mixer(1, xs, mix1, hy_out_b)
    return out.astype(np.float32)
```
